# Optimizing a Trainium2 kernel written in Bass

```python
import math
import jax, jax.numpy as jnp
from jax import lax
import numpy as np

D_MODEL = 2048
BATCH = 1
SEQ = 8192
DEPTH = 4

C_A = 512
H_B = 16
HEAD_DIM = 64
C_B = H_B * HEAD_DIM
C_C = 512
POOL_WINDOWS = (2, 4, 8, 16)
N_POOL_GROUPS = len(POOL_WINDOWS)
C_G = C_C // N_POOL_GROUPS
MIX_WIDTH = C_A + C_B + C_C
IN_WIDTH = 2 * C_A + 3 * C_B + C_C

CONV_WIDTH = 31
CONV_HALF = CONV_WIDTH // 2

DILATED_PATTERNS = ((128, 1), (512, 4), (2048, 16))
ATTN_BLOCK = 64
ROT_DIM = HEAD_DIM // 4
ROPE_THETA = 500000.0

FFN_HIDDEN = int(math.ceil((8 * D_MODEL / 3) / 256) * 256)
EPS = 1e-6
NEG = -1e30

kernel_name = "hybrid_conv_dilatedattn_pool_encoder"


def rms_normalize(t):
    tf = t.astype(jnp.float32)
    return tf * lax.rsqrt(jnp.mean(tf * tf, axis=-1, keepdims=True) + EPS)


def rmsnorm(t, g):
    return (rms_normalize(t) * g.astype(jnp.float32)).astype(t.dtype)


def rope_tables(S):
    pos = jnp.arange(S, dtype=jnp.float32)
    inv = ROPE_THETA ** (-jnp.arange(0, ROT_DIM, 2, dtype=jnp.float32) / ROT_DIM)
    ang = pos[:, None] * inv[None, :]
    return jnp.cos(ang), jnp.sin(ang)


def apply_partial_rope(t, cos, sin):
    half = ROT_DIM // 2
    t1 = t[..., :half]
    t2 = t[..., half:ROT_DIM]
    c = cos[None, :, None, :]
    s = sin[None, :, None, :]
    return jnp.concatenate([t1 * c - t2 * s, t2 * c + t1 * s, t[..., ROT_DIM:]], axis=-1)


def conformer_conv(u, w, b, ln_g, ln_b):
    a, gate = jnp.split(u, 2, axis=-1)
    h = a * jax.nn.sigmoid(gate)
    h = lax.conv_general_dilated(
        h, w[:, None, :], window_strides=(1,),
        padding=[(CONV_HALF, CONV_HALF)],
        dimension_numbers=('NWC', 'WIO', 'NWC'),
        feature_group_count=C_A) + b
    hf = h.astype(jnp.float32)
    mu = jnp.mean(hf, axis=-1, keepdims=True)
    var = jnp.mean(jnp.square(hf - mu), axis=-1, keepdims=True)
    hf = (hf - mu) * lax.rsqrt(var + EPS) * ln_g.astype(jnp.float32) + ln_b.astype(jnp.float32)
    return jax.nn.silu(hf).astype(u.dtype)


def banded_dilated_stats(q, k, v, dilation, half):
    B, S, H, E = q.shape
    L = S // dilation
    nb = -(-L // ATTN_BLOCK)
    Lp = nb * ATTN_BLOCK
    qd = q.reshape(B, L, dilation, H, E)
    kd = k.reshape(B, L, dilation, H, E)
    vd = v.reshape(B, L, dilation, H, E)
    pad_q = ((0, 0), (0, Lp - L), (0, 0), (0, 0), (0, 0))
    pad_kv = ((0, 0), (ATTN_BLOCK, Lp - L + ATTN_BLOCK), (0, 0), (0, 0), (0, 0))
    qb = jnp.pad(qd, pad_q).reshape(B, nb, ATTN_BLOCK, dilation, H, E)
    kb = jnp.pad(kd, pad_kv).reshape(B, nb + 2, ATTN_BLOCK, dilation, H, E)
    vb = jnp.pad(vd, pad_kv).reshape(B, nb + 2, ATTN_BLOCK, dilation, H, E)
    kwin = jnp.concatenate([kb[:, :-2], kb[:, 1:-1], kb[:, 2:]], axis=2)
    vwin = jnp.concatenate([vb[:, :-2], vb[:, 1:-1], vb[:, 2:]], axis=2)
    blk = jnp.arange(nb)[:, None]
    jq = blk * ATTN_BLOCK + jnp.arange(ATTN_BLOCK)[None, :]
    jk = (blk - 1) * ATTN_BLOCK + jnp.arange(3 * ATTN_BLOCK)[None, :]
    mask = ((jnp.abs(jk[:, None, :] - jq[:, :, None]) <= half)
            & (jk[:, None, :] >= 0) & (jk[:, None, :] < L))
    scores = jnp.einsum('bnqrhe,bnkrhe->bnrhqk', qb, kwin)
    maskb = mask[None, :, None, None, :, :]
    scores = jnp.where(maskb, scores, NEG)
    m = jnp.max(scores, axis=-1)
    p = jnp.where(maskb, jnp.exp(scores - m[..., None]), 0.0)
    s = jnp.sum(p, axis=-1)
    o = jnp.einsum('bnrhqk,bnkrhe->bnqrhe', p, vwin)
    m = jnp.transpose(m, (0, 1, 4, 2, 3)).reshape(B, Lp, dilation, H)[:, :L].reshape(B, S, H)
    s = jnp.transpose(s, (0, 1, 4, 2, 3)).reshape(B, Lp, dilation, H)[:, :L].reshape(B, S, H)
    o = o.reshape(B, Lp, dilation, H, E)[:, :L].reshape(B, S, H, E)
    return m, s, o


def dilated_attention(qkv, cos, sin):
    B, S, _ = qkv.shape
    q, k, v = jnp.split(qkv.astype(jnp.float32), 3, axis=-1)
    q = apply_partial_rope(q.reshape(B, S, H_B, HEAD_DIM), cos, sin) * (HEAD_DIM ** -0.5)
    k = apply_partial_rope(k.reshape(B, S, H_B, HEAD_DIM), cos, sin)
    v = v.reshape(B, S, H_B, HEAD_DIM)
    ms, ss, os_ = [], [], []
    for window, dilation in DILATED_PATTERNS:
        m, s, o = banded_dilated_stats(q, k, v, dilation, window // (2 * dilation))
        ms.append(m); ss.append(s); os_.append(o)
    m_all = jnp.stack(ms)
    s_all = jnp.stack(ss)
    o_all = jnp.stack(os_)
    wgt = jnp.exp(m_all - jnp.max(m_all, axis=0, keepdims=True))
    num = jnp.sum(wgt[..., None] * o_all, axis=0)
    den = jnp.sum(wgt * s_all, axis=0)
    out = num / den[..., None]
    return out.reshape(B, S, C_B).astype(qkv.dtype)


def pool_mixer(u, w, scale):
    B, S, _ = u.shape
    uf = u.astype(jnp.float32)
    cs = jnp.concatenate([jnp.zeros((B, 1, C_C), jnp.float32), jnp.cumsum(uf, axis=1)], axis=1)
    pos = jnp.arange(S)
    outs = []
    for gi, win in enumerate(POOL_WINDOWS):
        seg = cs[..., gi * C_G:(gi + 1) * C_G]
        lo = jnp.clip(pos - win // 2, 0, S)
        hi = jnp.clip(pos + win - win // 2, 0, S)
        mean = (seg[:, hi] - seg[:, lo]) / (hi - lo).astype(jnp.float32)[None, :, None]
        outs.append(mean - uf[..., gi * C_G:(gi + 1) * C_G])
    pooled = jnp.stack(outs, axis=2)
    mixed = jnp.einsum('bsgc,gcd->bsgd', pooled, w.astype(jnp.float32))
    return (mixed.reshape(B, S, C_C) * scale.astype(jnp.float32)).astype(u.dtype)


def setup_inputs(seed: int = 0) -> dict:
    key = jax.random.key(seed)
    ks = jax.random.split(key, 20)
    f32 = jnp.float32

    def nrm(k, shape, scale):
        return jax.random.normal(k, shape, f32) * scale

    def gain(k, shape):
        return 1.0 + 0.05 * jax.random.normal(k, shape, f32)

    return {
        "x": jax.random.normal(ks[0], (BATCH, SEQ, D_MODEL), f32),
        "w_in": nrm(ks[1], (DEPTH, D_MODEL, IN_WIDTH), D_MODEL ** -0.5),
        "conv_w": nrm(ks[2], (DEPTH, CONV_WIDTH, C_A), CONV_WIDTH ** -0.5),
        "conv_b": nrm(ks[3], (DEPTH, C_A), 0.02),
        "conv_ln_g": gain(ks[4], (DEPTH, C_A)),
        "conv_ln_b": nrm(ks[5], (DEPTH, C_A), 0.02),
        "pool_w": nrm(ks[6], (DEPTH, N_POOL_GROUPS, C_G, C_G), C_G ** -0.5),
        "pool_scale": gain(ks[7], (DEPTH, C_C)),
        "g_mix": gain(ks[8], (DEPTH, MIX_WIDTH)),
        "w_out": nrm(ks[9], (DEPTH, MIX_WIDTH, D_MODEL), MIX_WIDTH ** -0.5),
        "g_pre_mix": gain(ks[10], (DEPTH, D_MODEL)),
        "g_post_mix": gain(ks[11], (DEPTH, D_MODEL)),
        "g_pre_ffn": gain(ks[12], (DEPTH, D_MODEL)),
        "g_post_ffn": gain(ks[13], (DEPTH, D_MODEL)),
        "w_gate": nrm(ks[14], (DEPTH, D_MODEL, FFN_HIDDEN), D_MODEL ** -0.5),
        "w_up": nrm(ks[15], (DEPTH, D_MODEL, FFN_HIDDEN), D_MODEL ** -0.5),
        "w_down": nrm(ks[16], (DEPTH, FFN_HIDDEN, D_MODEL), FFN_HIDDEN ** -0.5),
    }


def reference(x, w_in, conv_w, conv_b, conv_ln_g, conv_ln_b, pool_w, pool_scale, g_mix,
              w_out, g_pre_mix, g_post_mix, g_pre_ffn, g_post_ffn, w_gate, w_up, w_down):
    S = x.shape[1]
    cos, sin = rope_tables(S)
    a_end = 2 * C_A
    b_end = a_end + 3 * C_B
    for l in range(DEPTH):
        h = rmsnorm(x, g_pre_mix[l])
        proj = h @ w_in[l]
        y_a = conformer_conv(proj[..., :a_end], conv_w[l], conv_b[l], conv_ln_g[l], conv_ln_b[l])
        y_b = dilated_attention(proj[..., a_end:b_end], cos, sin)
        y_c = pool_mixer(proj[..., b_end:], pool_w[l], pool_scale[l])
        y = jnp.concatenate([rms_normalize(y_a), rms_normalize(y_b), rms_normalize(y_c)], axis=-1)
        y = (y * g_mix[l].astype(jnp.float32)).astype(x.dtype)
        x = x + rmsnorm(y @ w_out[l], g_post_mix[l])
        h = rmsnorm(x, g_pre_ffn[l])
        f = (jax.nn.silu(h @ w_gate[l]) * (h @ w_up[l])) @ w_down[l]
        x = x + rmsnorm(f, g_post_ffn[l])
    return x
```

```python
import contextlib
import numpy as np
import ml_dtypes
import concourse.bass as bass
import concourse.mybir as mybir
from concourse.bass_utils import run_bass_kernel_spmd

F32 = mybir.dt.float32
BF16 = mybir.dt.bfloat16
AF = mybir.ActivationFunctionType
ALU = mybir.AluOpType

NCORES = 8
S_LEN = 8192
T = 1024
NT = 8
D = 2048
DEPTH = 4
INW = 4608
FF = 5632
NHC = FF // 128
EPS = 1e-6
ENGS = ("pe", "act", "dve", "pool", "sp")
import os
DEBUG = bool(os.environ.get("KDEBUG"))


class Sched:
    def __init__(self, nc):
        self.nc = nc
        self.ops = {e: [] for e in ENGS}
        self.cnt = {}
        self.waited = {e: {} for e in ENGS}
        self.last_w = {}
        self.readers = {}
        self.semnames = set(ENGS)
        self.sem_h = {}

    def _deps(self, eng, reads, writes):
        need = {}

        def add(k, v):
            if k == "pe" and eng == "pe":
                return
            if need.get(k, 0) < v:
                need[k] = v

        for k in reads:
            t = self.last_w.get(k)
            if t is not None:
                add(*t)
        for k in writes:
            t = self.last_w.get(k)
            if t is not None:
                add(*t)
            for kk, vv in self.readers.get(k, {}).items():
                add(kk, vv)
        return self._filter(eng, need)

    def _filter(self, eng, need):
        out = []
        w = self.waited[eng]
        for k, v in need.items():
            if w.get(k, 0) < v:
                w[k] = v
                out.append((k, v))
        return out

    def _track(self, tok, reads, writes):
        for k in writes:
            self.last_w[k] = tok
            self.readers[k] = {}
        for k in reads:
            r = self.readers.setdefault(k, {})
            if r.get(tok[0], 0) < tok[1]:
                r[tok[0]] = tok[1]

    def op(self, eng, fn, reads=(), writes=()):
        waits = self._deps(eng, reads, writes)
        self.cnt[eng] = self.cnt.get(eng, 0) + 1
        tok = (eng, self.cnt[eng])
        self._track(tok, reads, writes)
        self.ops[eng].append((waits, fn, eng, 1))
        return tok

    def dma(self, eng, fn, sem, reads=(), writes=()):
        if sem is None:
            self.nuniq = getattr(self, "nuniq", 0) + 1
            sem = f"u{self.nuniq}"
        self.semnames.add(sem)
        waits = self._deps(eng, reads, writes)
        self.cnt[sem] = self.cnt.get(sem, 0) + 16
        tok = (sem, self.cnt[sem])
        self._track(tok, reads, writes)
        self.ops[eng].append((waits, fn, sem, 16))
        return tok

    def barrier(self):
        for e in ENGS:
            waits = self._filter(e, dict(self.cnt))
            if waits:
                self.ops[e].append((waits, None, None, 0))

    def run(self):
        nc = self.nc
        with contextlib.ExitStack() as st:
            for name in sorted(self.semnames):
                self.sem_h[name] = st.enter_context(nc.semaphore("s_" + name))
            block = st.enter_context(nc.Block())
            sem_h = self.sem_h

            def replay(e, lst):
                for waits, fn, sem, inc in lst:
                    for k, v in waits:
                        e.wait_ge(sem_h[k], v)
                    if fn is not None:
                        fn(e).then_inc(sem_h[sem], inc)

            @block.tensor
            def _(e):
                replay(e, self.ops["pe"])

            @block.scalar
            def _(e):
                replay(e, self.ops["act"])

            @block.vector
            def _(e):
                replay(e, self.ops["dve"])

            @block.gpsimd
            def _(e):
                replay(e, self.ops["pool"])

            @block.sync
            def _(e):
                replay(e, self.ops["sp"])


class Arena:
    def __init__(self, t, nbytes, start=0):
        self.t, self.n, self.off = t, nbytes, start

    def take(self, shape, dt):
        esz = 2 if dt == BF16 else 4
        n = esz
        for d in shape:
            n *= d
        n = (n + 31) // 32 * 32
        assert self.off + n <= self.n, (self.off, n, self.n)
        ap = self.t[:, self.off // 4:(self.off + n) // 4]
        if dt != F32:
            ap = ap.bitcast(dt)
        tot = 1
        for d in shape:
            tot *= d
        ap = ap[:, 0:tot]
        if len(shape) == 2:
            ap = ap.rearrange("p (a b) -> p a b", a=shape[0])
        elif len(shape) == 3:
            ap = ap.rearrange("p (a b c) -> p a b c", a=shape[0], b=shape[1])
        self.off += n
        return ap


def build_program(do_B, do_A):
    nc = bass.Bass("TRN2", target_bir_lowering=False)
    S = Sched(nc)

    def din(name, shape, dt=F32):
        return nc.dram_tensor(name, list(shape), dt, kind="ExternalInput").ap()

    def dout(name, shape, dt=F32):
        return nc.dram_tensor(name, list(shape), dt, kind="ExternalOutput").ap()

    ident_d = din("ident", [128, 128])
    x_in = din("x_in", [T, D])
    if do_B:
        qT_d = din("qT", [8, 128, T], BF16)
        kT_d = din("kTh", [8, 128, 3 * T], BF16)
        v_d = din("vh", [8, 128, 24, 130], BF16)
        hg_d = din("hgh", [4, 128, T + 30])
        up_d = din("uh", [4, 128, T + 16])
        mask_d = din("masks", [128, 17 * 128], BF16)
        corr_d = din("corr", [128, 4, 16])
        cw_d = din("cw", [128, 4, 31])
        vec4_d = din("vec4", [128, 4, 4])
        gmixT_d = din("gmixT", [128, 16])
        gmixB_d = din("gmixB", [1, 1024])
        poolw_d = din("poolw", [4, 128, 128])
        wout_d = din("w_out", [D, D])
        gpm_d = din("g_post_mix", [1, D])
        gpf_d = din("g_pre_ffn", [1, D])
        wg_d = din("w_gate", [D, FF])
        wu_d = din("w_up", [D, FF])
        wd_d = din("w_down", [FF, D])
        gpo_d = din("g_post_ffn", [1, D])
        x_out = dout("x_out", [T, D])
        if DEBUG:
            x1_d = dout("x1_dbg", [T, D])
            yT_dbg = dout("yT_dbg", [16, 128, T], BF16)
        else:
            x1_d = nc.dram_tensor("x1_scratch", [T, D], F32).ap()
    if do_A:
        win_d = din("w_in", [D, INW])
        gpre_d = din("g_pre_mix", [1, D])
        ropec_d = din("rope_c", [128, T])
        ropes_d = din("rope_s", [128, T])
        pmat_d = din("pmat", [128, 128])
        qT_o = dout("qT_o", [8, 128, T], BF16)
        kT_o = dout("kT_o", [8, 128, T], BF16)
        v_o = dout("v_o", [NT, 128, 16 * 65], BF16)
        hg_o = dout("hg_o", [4, 128, T])
        up_o = dout("up_o", [4, 128, T])

    with contextlib.ExitStack() as top:
        def sbt(st, name, shape, dt):
            return st.enter_context(nc.sbuf_tensor("sb_" + name, list(shape), dt))

        ps = [top.enter_context(nc.psum_tensor(f"ps{i}", [128, 512], F32)) for i in range(8)]
        ident = sbt(top, "ident_sb", [128, 128], BF16)
        onesf = sbt(top, "onesf", [128, 128], F32)
        R1 = sbt(top, "R1", [128, 16384], F32)
        R2 = sbt(top, "R2", [128, 22528], F32)
        R1B, R2B = 65536, 90112
        h2T = Arena(R1, R1B).take([16, T], BF16)
        o_all = Arena(R1, R1B).take([NT, D], F32)
        aT = Arena(R2, R2B).take([NHC, T], BF16)
        hT = Arena(R2, R2B).take([16, T], BF16)
        ss = sbt(top, "ss", [128, 16], F32)
        rs = sbt(top, "rs", [128, 16], F32)
        ptr = ps[7][:].bitcast(BF16).rearrange("p (j t) -> p j t", j=8)

        S.dma("pool", lambda e: e.dma_start(out=ident[:], in_=ident_d), None, writes=["ident"])
        S.op("dve", lambda e: e.memset(onesf[:], 1.0 / 512.0), writes=["onesf"])

        def norm_transpose(xblk, xkey, gt, gkey, xs, xskey, sq, sqkey, dstT, dkey, tb, slot):
            c = slot
            S.op("act", lambda e: e.activation(out=sq, in_=xblk, func=AF.Square, accum_out=ss[:, c:c + 1]),
                 reads=[xkey], writes=[sqkey, ("ss", c)])
            S.op("dve", lambda e: e.tensor_scalar(out=rs[:, c:c + 1], in0=ss[:, c:c + 1], scalar1=1.0 / D, scalar2=EPS,
                                                  op0=ALU.mult, op1=ALU.add), reads=[("ss", c)], writes=[("rs", c)])
            S.op("act", lambda e: e.activation(out=rs[:, c:c + 1], in_=rs[:, c:c + 1], func=AF.Sqrt),
                 reads=[("rs", c)], writes=[("rs", c)])
            S.op("dve", lambda e: e.reciprocal(out=rs[:, c:c + 1], in_=rs[:, c:c + 1]), reads=[("rs", c)], writes=[("rs", c)])
            S.op("dve", lambda e: e.scalar_tensor_tensor(out=xs, in0=xblk, scalar=rs[:, c:c + 1], in1=gt,
                                                         op0=ALU.mult, op1=ALU.mult),
                 reads=[xkey, gkey, ("rs", c)], writes=[xskey])
            for half in range(2):
                def tr(e, half=half):
                    for j in range(8):
                        kc = half * 8 + j
                        i = e.transpose(out=ptr[:, j, :], in_=xs[:, kc * 128:(kc + 1) * 128], identity=ident[:])
                    return i
                S.op("pe", tr, reads=[xskey, "ident"], writes=[("ps", 7)])
                S.op("act", lambda e, half=half: e.activation(out=dstT[:, half * 8:(half + 1) * 8, tb * 128:(tb + 1) * 128],
                                                              in_=ptr[:], func=AF.Copy),
                     reads=[("ps", 7)], writes=[(dkey, tb, half)])

        if do_B:
            with contextlib.ExitStack() as sB:
                gmixT = sbt(sB, "gmixT", [128, 16], F32)
                vec4 = sbt(sB, "vec4", [128, 4, 4], F32)
                sY = contextlib.ExitStack()
                yT = sbt(sY, "yT", [128, 16, T], BF16)
                S.dma("sp", lambda e: e.dma_start(out=gmixT[:], in_=gmixT_d), None, writes=["gmixT"])
                S.dma("sp", lambda e: e.dma_start(out=vec4[:], in_=vec4_d), None, writes=["vec4"])

                with contextlib.ExitStack() as s1:
                    a2 = Arena(R2, R2B)
                    yb = a2.take([NT, 1024], F32)
                    hg = a2.take([4, T + 30], F32)
                    acc = a2.take([4, T], F32)
                    uh = a2.take([4, T + 16], F32)
                    masks = a2.take([17 * 128], BF16)
                    ybs = a2.take([1024], BF16)
                    a1 = Arena(R1, R1B)
                    pw = a1.take([3, T + 16], F32)
                    pooled = a1.take([4, T], BF16)
                    sqb = a1.take([1024], BF16)
                    gmixB = a1.take([1024], F32)
                    tmpA = a1.take([512], F32)
                    tmpB = a1.take([512], F32)
                    meansb = a1.take([512], F32)
                    rstdsb = a1.take([512], F32)
                    qkv_v = [a1.take([24, 130], BF16) for i in range(2)]
                    qkv_q = [a1.take([T], BF16) for i in range(2)]
                    qkv_k = [a1.take([3 * T], BF16) for i in range(2)]
                    ya = acc
                    yc = hg
                    cw = sbt(s1, "cw", [128, 4, 31], F32)
                    corr = sbt(s1, "corr", [128, 4, 16], F32)
                    poolw = sbt(s1, "poolw", [128, 4, 128], BF16)
                    pT = [sbt(s1, f"pT{i}", [128, 512], BF16) for i in range(4)]
                    rec = sbt(s1, "rec", [128, 2, 2], F32)
                    ctmp = [sbt(s1, f"ctmp{i}", [128, T], F32)[:] for i in range(2)]

                    S.dma("sp", lambda e: e.dma_start(out=hg, in_=hg_d.rearrange("c p t -> p c t")), None, writes=["hg"])
                    S.dma("sp", lambda e: e.dma_start(out=uh, in_=up_d.rearrange("c p t -> p c t")), None, writes=["uh"])
                    S.dma("sp", lambda e: e.dma_start(out=cw[:], in_=cw_d), None, writes=["cw"])
                    S.dma("sp", lambda e: e.dma_start(out=corr[:], in_=corr_d), None, writes=["corr"])
                    S.dma("sp", lambda e: e.dma_start(out=masks, in_=mask_d), None, writes=["masks"])
                    S.dma("sp", lambda e: e.dma_start(out=gmixB, in_=gmixB_d.partition_broadcast(128)), None, writes=["gmixB"])
                    S.dma("pool", lambda e: e.dma_start(out=poolw[:], in_=poolw_d.rearrange("g c d -> c g d")), None, writes=["poolw"])

                    def load_qkv(hp):
                        sl = hp % 2
                        S.dma("sp", lambda e: e.dma_start(out=qkv_q[sl], in_=qT_d[hp]), f"q{sl}", writes=[("q", sl)])
                        S.dma("sp", lambda e: e.dma_start(out=qkv_k[sl], in_=kT_d[hp]), f"k{sl}", writes=[("k", sl)])
                        S.dma("sp", lambda e: e.dma_start(out=qkv_v[sl], in_=v_d[hp]), f"v{sl}", writes=[("v", sl)])

                    load_qkv(0)

                    for c in range(4):
                        S.op("pool", lambda e, c=c: e.tensor_scalar(out=acc[:, c, :], in0=hg[:, c, 0:T], scalar1=cw[:, c, 0:1],
                                                                    scalar2=vec4[:, 0, c:c + 1], op0=ALU.mult, op1=ALU.add),
                             reads=["hg", "cw", "vec4"], writes=[("acc", c)])
                        for j in range(1, 31):
                            tj = ctmp[j % 2]
                            S.op("pool", lambda e, c=c, j=j, tj=tj: e.tensor_scalar(out=tj, in0=hg[:, c, j:j + T], scalar1=cw[:, c, j:j + 1],
                                                                                scalar2=None, op0=ALU.mult),
                                 reads=["hg", "cw"], writes=[("ctmp", j % 2)])
                            S.op("pool", lambda e, c=c, tj=tj: e.tensor_add(out=acc[:, c, :], in0=acc[:, c, :], in1=tj),
                                 reads=[("acc", c), ("ctmp", j % 2)], writes=[("acc", c)])
                    for gi in range(4):
                        w = 2 << gi
                        src = uh[:, gi, :]
                        L = T + 16
                        step = 1
                        lvl = 0
                        while step < w:
                            L2 = L - step
                            dst = pw[:, lvl % 3, :]
                            S.op("pool", lambda e, src=src, dst=dst, L2=L2, step=step: e.tensor_add(
                                out=dst[:, 0:L2], in0=src[:, 0:L2], in1=src[:, step:step + L2]),
                                reads=["uh", ("pw", (lvl + 2) % 3)], writes=[("pw", lvl % 3)])
                            src = dst
                            L = L2
                            step *= 2
                            lvl += 1
                        off = 8 - w // 2
                        win = src[:, off:off + T]
                        lastkey = ("pw", (lvl - 1) % 3)
                        S.op("pool", lambda e, win=win, gi=gi: e.tensor_mul(out=win[:, 0:8], in0=win[:, 0:8], in1=corr[:, gi, 0:8]),
                             reads=[lastkey, "corr"], writes=[lastkey])
                        S.op("pool", lambda e, win=win, gi=gi: e.tensor_mul(out=win[:, T - 8:T], in0=win[:, T - 8:T], in1=corr[:, gi, 8:16]),
                             reads=[lastkey], writes=[lastkey])
                        S.op("dve", lambda e, win=win, gi=gi, w=w: e.scalar_tensor_tensor(
                            out=pooled[:, gi, :], in0=win, scalar=1.0 / w, in1=uh[:, gi, 8:8 + T], op0=ALU.mult, op1=ALU.subtract),
                            reads=[lastkey, "uh"], writes=[("pooled", gi)])

                    units = []
                    for hp in range(8):
                        for qb in range(NT):
                            for h in range(2):
                                for g, (r0, n) in enumerate(((-8, 4), (-4, 4), (0, 4), (4, 4), (8, 1))):
                                    units.append((hp, qb, h, g, r0, n))
                    NU = len(units)

                    def emit_qk(i):
                        hp, qb, h, g, r0, n = units[i]
                        sl = hp % 2
                        bank = i % 3
                        kb0 = qb + 8 + r0

                        def f(e):
                            for j in range(n):
                                ins = e.matmul(ps[bank][:, j * 128:(j + 1) * 128],
                                               lhsT=qkv_k[sl][h * 64:(h + 1) * 64, (kb0 + j) * 128:(kb0 + j + 1) * 128],
                                               rhs=qkv_q[sl][h * 64:(h + 1) * 64, qb * 128:(qb + 1) * 128], start=True, stop=True)
                            return ins
                        S.op("pe", f, reads=[("q", sl), ("k", sl)], writes=[("ps", bank)])
                        slot = i % 4
                        S.op("act", lambda e: e.activation(out=pT[slot][:, 0:n * 128], in_=ps[bank][:, 0:n * 128], func=AF.Exp, scale=0.125),
                             reads=[("ps", bank)], writes=[("pT", slot)])
                        S.op("dve", lambda e: e.tensor_mul(out=pT[slot][:, 0:n * 128], in0=pT[slot][:, 0:n * 128],
                                                           in1=masks[:, (r0 + 8) * 128:(r0 + 8 + n) * 128]),
                             reads=[("pT", slot), "masks"], writes=[("pT", slot)])

                    def emit_pv(i):
                        hp, qb, h, g, r0, n = units[i]
                        sl = hp % 2
                        slot = i % 4
                        ob = 3 + ((hp * NT + qb) % 2)
                        kb0 = qb + 8 + r0

                        def f(e):
                            for j in range(n):
                                ins = e.matmul(ps[ob][:, h * 128:h * 128 + 65], lhsT=pT[slot][:, j * 128:(j + 1) * 128],
                                               rhs=qkv_v[sl][:, kb0 + j, h * 65:(h + 1) * 65],
                                               start=(g == 0 and j == 0), stop=(g == 4))
                            return ins
                        S.op("pe", f, reads=[("pT", slot), ("v", sl)], writes=[("ps", ob)])
                        if h == 1 and g == 4:
                            S.op("dve", lambda e: e.reciprocal(out=rec[:, (hp * NT + qb) % 2, :], in_=ps[ob][:, 64:256:128]),
                                 reads=[("ps", ob)], writes=[("rec", ob)])
                            for hh in range(2):
                                S.op("dve", lambda e, hh=hh: e.tensor_scalar(
                                    out=yb[:, qb, hp * 128 + hh * 64:hp * 128 + hh * 64 + 64], in0=ps[ob][:, hh * 128:hh * 128 + 64],
                                    scalar1=rec[:, (hp * NT + qb) % 2, hh:hh + 1], scalar2=None, op0=ALU.mult),
                                    reads=[("ps", ob), ("rec", ob)], writes=[("yb", qb, hp, hh)])

                    LAG = 2
                    for i in range(NU + LAG):
                        if i < NU:
                            emit_qk(i)
                        if i - LAG >= 0:
                            emit_pv(i - LAG)
                        if i < NU:
                            hp_, qb_, h_, g_ = units[i][:4]
                            if qb_ == 0 and h_ == 0 and g_ == LAG and hp_ + 1 < 8:
                                load_qkv(hp_ + 1)

                    ybkeys = [[("yb", qb, hp, hh) for hp in range(8) for hh in range(2)] for qb in range(NT)]
                    for qb in range(NT):
                        c = 8 + (qb % 2)
                        S.op("act", lambda e, qb=qb, c=c: e.activation(out=sqb, in_=yb[:, qb, :], func=AF.Square, accum_out=ss[:, c:c + 1]),
                             reads=ybkeys[qb], writes=["sqb", ("ss", c)])
                        S.op("dve", lambda e, c=c: e.tensor_scalar(out=rs[:, c:c + 1], in0=ss[:, c:c + 1], scalar1=1.0 / 1024.0, scalar2=EPS,
                                                                   op0=ALU.mult, op1=ALU.add), reads=[("ss", c)], writes=[("rs", c)])
                        S.op("act", lambda e, c=c: e.activation(out=rs[:, c:c + 1], in_=rs[:, c:c + 1], func=AF.Sqrt),
                             reads=[("rs", c)], writes=[("rs", c)])
                        S.op("dve", lambda e, c=c: e.reciprocal(out=rs[:, c:c + 1], in_=rs[:, c:c + 1]), reads=[("rs", c)], writes=[("rs", c)])
                        S.op("dve", lambda e, qb=qb, c=c: e.scalar_tensor_tensor(out=ybs, in0=yb[:, qb, :], scalar=rs[:, c:c + 1], in1=gmixB,
                                                                                 op0=ALU.mult, op1=ALU.mult),
                             reads=ybkeys[qb] + [("rs", c), "gmixB"], writes=["ybs"])

                        def tr(e):
                            for j in range(8):
                                i_ = e.transpose(out=ptr[:, j, :], in_=ybs[:, j * 128:(j + 1) * 128], identity=ident[:])
                            return i_
                        S.op("pe", tr, reads=["ybs", "ident"], writes=[("ps", 7)])
                        S.op("act", lambda e, qb=qb: e.activation(out=yT[:, 4:12, qb * 128:(qb + 1) * 128], in_=ptr[:], func=AF.Copy),
                             reads=[("ps", 7)], writes=[("yT", "b", qb)])

                    def rms_feat(src, srckeys, base, tagk):
                        for th in range(2):
                            tsl = slice(th * 512, (th + 1) * 512)
                            for c in range(4):
                                S.op("act", lambda e, c=c, tsl=tsl: e.activation(out=tmpA, in_=src[:, c, tsl], func=AF.Square),
                                     reads=[srckeys(c, th)], writes=["tmpA"])
                                S.op("pe", lambda e, c=c: e.matmul(ps[0][:], lhsT=onesf[:], rhs=tmpA, start=(c == 0), stop=(c == 3)),
                                     reads=["tmpA", "onesf"], writes=[("ps", 0)])
                            S.op("dve", lambda e: e.tensor_scalar(out=rstdsb, in0=ps[0][:], scalar1=EPS, scalar2=None, op0=ALU.add),
                                 reads=[("ps", 0)], writes=["rstdsb"])
                            S.op("act", lambda e: e.activation(out=rstdsb, in_=rstdsb, func=AF.Sqrt), reads=["rstdsb"], writes=["rstdsb"])
                            S.op("dve", lambda e: e.reciprocal(out=rstdsb, in_=rstdsb), reads=["rstdsb"], writes=["rstdsb"])
                            for c in range(4):
                                S.op("dve", lambda e, c=c, tsl=tsl: e.scalar_tensor_tensor(out=yT[:, base + c, tsl], in0=src[:, c, tsl],
                                                                                  scalar=gmixT[:, base + c:base + c + 1], in1=rstdsb,
                                                                                  op0=ALU.mult, op1=ALU.mult),
                                     reads=[srckeys(c, th), "rstdsb", "gmixT"], writes=[("yT", tagk, c, th)])

                    for th in range(2):
                        tsl = slice(th * 512, (th + 1) * 512)
                        for c in range(4):
                            S.op("pe", lambda e, c=c, tsl=tsl: e.matmul(ps[1][:], lhsT=onesf[:], rhs=acc[:, c, tsl], start=(c == 0), stop=(c == 3)),
                                 reads=[("acc", c), "onesf"], writes=[("ps", 1)])
                        for c in range(4):
                            S.op("act", lambda e, c=c, tsl=tsl: e.activation(out=tmpA, in_=acc[:, c, tsl], func=AF.Square),
                                 reads=[("acc", c)], writes=["tmpA"])
                            S.op("pe", lambda e, c=c: e.matmul(ps[2][:], lhsT=onesf[:], rhs=tmpA, start=(c == 0), stop=(c == 3)),
                                 reads=["tmpA", "onesf"], writes=[("ps", 2)])
                        S.op("act", lambda e: e.activation(out=meansb, in_=ps[1][:], func=AF.Copy), reads=[("ps", 1)], writes=["meansb"])
                        S.op("dve", lambda e: e.tensor_mul(out=tmpB, in0=meansb, in1=meansb), reads=["meansb"], writes=["tmpB"])
                        S.op("dve", lambda e: e.tensor_sub(out=rstdsb, in0=ps[2][:], in1=tmpB), reads=[("ps", 2), "tmpB"], writes=["rstdsb"])
                        S.op("dve", lambda e: e.tensor_scalar(out=rstdsb, in0=rstdsb, scalar1=EPS, scalar2=None, op0=ALU.add),
                             reads=["rstdsb"], writes=["rstdsb"])
                        S.op("act", lambda e: e.activation(out=rstdsb, in_=rstdsb, func=AF.Sqrt), reads=["rstdsb"], writes=["rstdsb"])
                        S.op("dve", lambda e: e.reciprocal(out=rstdsb, in_=rstdsb), reads=["rstdsb"], writes=["rstdsb"])
                        for c in range(4):
                            S.op("dve", lambda e, c=c, tsl=tsl: e.tensor_sub(out=tmpB, in0=acc[:, c, tsl], in1=meansb),
                                 reads=[("acc", c), "meansb"], writes=["tmpB"])
                            S.op("dve", lambda e: e.tensor_mul(out=tmpB, in0=tmpB, in1=rstdsb), reads=["tmpB", "rstdsb"], writes=["tmpB"])
                            S.op("act", lambda e, c=c, tsl=tsl: e.activation(out=ya[:, c, tsl], in_=tmpB, func=AF.Silu,
                                                                    scale=vec4[:, 1, c:c + 1], bias=vec4[:, 2, c:c + 1]),
                                 reads=["tmpB", "vec4"], writes=[("ya", c, th)])
                    rms_feat(ya, lambda c, th: ("ya", c, th), 0, "a")

                    for gi in range(4):
                        for th in range(2):
                            tsl = slice(th * 512, (th + 1) * 512)
                            S.op("pe", lambda e, gi=gi, tsl=tsl: e.matmul(ps[1][:], lhsT=poolw[:, gi, :], rhs=pooled[:, gi, tsl], start=True, stop=True),
                                 reads=[("pooled", gi), "poolw"], writes=[("ps", 1)])
                            S.op("act", lambda e, gi=gi, tsl=tsl: e.activation(out=yc[:, gi, tsl], in_=ps[1][:], func=AF.Copy, scale=vec4[:, 3, gi:gi + 1]),
                                 reads=[("ps", 1), "vec4"], writes=[("yc", gi, th)])
                    rms_feat(yc, lambda c, th: ("yc", c, th), 12, "c")
                    if DEBUG:
                        S.barrier()
                        S.dma("sp", lambda e: e.dma_start(out=yT_dbg.rearrange("c p t -> p c t"), in_=yT[:]), "dbg", writes=["dbg"])
                S.barrier()

                with contextlib.ExitStack() as s4:
                    a2 = Arena(R2, R2B)
                    wo = [a2.take([16, 512], BF16) for i in range(2)]
                    omix = a2.take([4, D], F32)
                    gt1 = a2.take([D], F32)
                    gt2 = a2.take([D], F32)
                    xb = a2.take([D], F32)
                    sq4 = sbt(s4, "sq4", [128, D], BF16)[:]
                    xs4 = sbt(s4, "xs4", [128, D], BF16)[:]
                    S.dma("sp", lambda e: e.dma_start(out=gt1, in_=gpm_d.partition_broadcast(128)), None, writes=["gt1"])
                    S.dma("sp", lambda e: e.dma_start(out=gt2, in_=gpf_d.partition_broadcast(128)), None, writes=["gt2"])
                    wview = wout_d.rearrange("(kc p) n -> p kc n", p=128)
                    nslab = 0
                    for th in range(2):
                        for dc in range(4):
                            sl = nslab % 2
                            nslab += 1
                            S.dma("pool", lambda e, sl=sl, dc=dc: e.dma_start(out=wo[sl], in_=wview[:, :, dc * 512:(dc + 1) * 512]),
                                  f"wo{sl}", writes=[("wo", sl)])
                            for t4 in range(4):
                                tb = th * 4 + t4
                                bank = (dc * 4 + t4) % 4

                                def f(e, tb=tb, sl=sl, bank=bank):
                                    for kc in range(16):
                                        ins = e.matmul(ps[bank][:], lhsT=yT[:, kc, tb * 128:(tb + 1) * 128], rhs=wo[sl][:, kc, :],
                                                       start=(kc == 0), stop=(kc == 15))
                                    return ins
                                S.op("pe", f, reads=[("wo", sl)] + [("yT", "b", tb)] + [("yT", k, c, tb // 4) for k in ("a", "c") for c in range(4)],
                                     writes=[("ps", bank)])
                                S.op("act", lambda e, t4=t4, dc=dc, bank=bank: e.activation(out=omix[:, t4, dc * 512:(dc + 1) * 512], in_=ps[bank][:], func=AF.Copy),
                                     reads=[("ps", bank)], writes=[("omix", t4, dc)])
                        for t4 in range(4):
                            tb = th * 4 + t4
                            c = t4 % 2
                            okeys = [("omix", t4, dc) for dc in range(4)]
                            S.dma("sp", lambda e, tb=tb: e.dma_start(out=xb, in_=x_in[tb * 128:(tb + 1) * 128, :]), "xb", writes=["xb"])
                            S.op("act", lambda e, t4=t4, c=c: e.activation(out=sq4, in_=omix[:, t4, :], func=AF.Square, accum_out=ss[:, c:c + 1]),
                                 reads=okeys, writes=["sq4", ("ss", c)])
                            S.op("dve", lambda e, c=c: e.tensor_scalar(out=rs[:, c:c + 1], in0=ss[:, c:c + 1], scalar1=1.0 / D, scalar2=EPS,
                                                                       op0=ALU.mult, op1=ALU.add), reads=[("ss", c)], writes=[("rs", c)])
                            S.op("act", lambda e, c=c: e.activation(out=rs[:, c:c + 1], in_=rs[:, c:c + 1], func=AF.Sqrt), reads=[("rs", c)], writes=[("rs", c)])
                            S.op("dve", lambda e, c=c: e.reciprocal(out=rs[:, c:c + 1], in_=rs[:, c:c + 1]), reads=[("rs", c)], writes=[("rs", c)])
                            S.op("dve", lambda e, t4=t4, c=c: e.scalar_tensor_tensor(out=omix[:, t4, :], in0=omix[:, t4, :], scalar=rs[:, c:c + 1], in1=gt1,
                                                                                     op0=ALU.mult, op1=ALU.mult),
                                 reads=okeys + [("rs", c), "gt1"], writes=okeys)
                            S.op("dve", lambda e, t4=t4: e.tensor_add(out=xb, in0=xb, in1=omix[:, t4, :]), reads=okeys + ["xb"], writes=["xb"])
                            S.dma("sp", lambda e, tb=tb: e.dma_start(out=x1_d[tb * 128:(tb + 1) * 128, :], in_=xb), "x1w", reads=["xb"], writes=[("x1d", tb)])
                            norm_transpose(xb, "xb", gt2, "gt2", xs4, "xs4", sq4, "sq4", h2T, "h2T", tb, 2 + c)
                S.barrier()
                sY.close()

                with contextlib.ExitStack() as s5:
                    wg = [sbt(s5, f"wg{i}", [128, 16, 256], BF16) for i in range(2)]
                    wu = [sbt(s5, f"wu{i}", [128, 16, 256], BF16) for i in range(2)]
                    sg = [sbt(s5, f"sg{i}", [128, 512], F32) for i in range(2)]
                    gview = wg_d.rearrange("(kc p) n -> p kc n", p=128)
                    uview = wu_d.rearrange("(kc p) n -> p kc n", p=128)
                    h2keys = [("h2T", tb, half) for tb in range(NT) for half in range(2)]
                    n5 = 0
                    for hg2 in range(NHC // 2):
                        sl = hg2 % 2
                        S.dma("pool", lambda e, sl=sl, hg2=hg2: e.dma_start(out=wg[sl][:], in_=gview[:, :, hg2 * 256:(hg2 + 1) * 256]), f"wg{sl}", writes=[("wg", sl)])
                        S.dma("pool", lambda e, sl=sl, hg2=hg2: e.dma_start(out=wu[sl][:], in_=uview[:, :, hg2 * 256:(hg2 + 1) * 256]), f"wu{sl}", writes=[("wu", sl)])
                        for hh in range(2):
                            hc = hg2 * 2 + hh
                            for th in range(2):
                                bg = (n5 % 3) * 2
                                bu = bg + 1
                                sgs = n5 % 2
                                n5 += 1

                                def f(e, sl=sl, hh=hh, th=th, bg=bg, bu=bu):
                                    for kc in range(16):
                                        e.matmul(ps[bg][:], lhsT=wg[sl][:, kc, hh * 128:(hh + 1) * 128], rhs=h2T[:, kc, th * 512:(th + 1) * 512],
                                                 start=(kc == 0), stop=(kc == 15))
                                    for kc in range(16):
                                        ins = e.matmul(ps[bu][:], lhsT=wu[sl][:, kc, hh * 128:(hh + 1) * 128], rhs=h2T[:, kc, th * 512:(th + 1) * 512],
                                                       start=(kc == 0), stop=(kc == 15))
                                    return ins
                                S.op("pe", f, reads=[("wg", sl), ("wu", sl)] + h2keys, writes=[("ps", bg), ("ps", bu)])
                                S.op("act", lambda e, bg=bg, sgs=sgs: e.activation(out=sg[sgs][:], in_=ps[bg][:], func=AF.Silu),
                                     reads=[("ps", bg)], writes=[("sg", sgs)])
                                S.op("dve", lambda e, bu=bu, sgs=sgs, hc=hc, th=th: e.tensor_mul(out=aT[:, hc, th * 512:(th + 1) * 512], in0=sg[sgs][:], in1=ps[bu][:]),
                                     reads=[("sg", sgs), ("ps", bu)], writes=[("aT", hc, th)])
                S.barrier()

                with contextlib.ExitStack() as s6:
                    wdn = [sbt(s6, f"wd{i}", [128, 4, 512], BF16) for i in range(2)]
                    gt3 = sbt(s6, "gt3", [128, D], F32)[:]
                    xb6 = sbt(s6, "xb6", [128, D], F32)[:]
                    sq6 = sbt(s6, "sq6", [128, D], BF16)[:]
                    S.dma("sp", lambda e: e.dma_start(out=gt3, in_=gpo_d.partition_broadcast(128)), None, writes=["gt3"])
                    dview = wd_d.rearrange("(hc p) n -> p hc n", p=128)
                    n6 = 0
                    for dc in range(4):
                        for hq in range(11):
                            sl = n6 % 2
                            n6 += 1
                            S.dma("pool", lambda e, sl=sl, hq=hq, dc=dc: e.dma_start(out=wdn[sl][:], in_=dview[:, hq * 4:(hq + 1) * 4, dc * 512:(dc + 1) * 512]),
                                  f"wd{sl}", writes=[("wd", sl)])
                            for tb in range(NT):
                                def f(e, sl=sl, hq=hq, tb=tb):
                                    for j in range(4):
                                        hc = hq * 4 + j
                                        ins = e.matmul(ps[tb][:], lhsT=aT[:, hc, tb * 128:(tb + 1) * 128], rhs=wdn[sl][:, j, :],
                                                       start=(hc == 0), stop=(hc == NHC - 1))
                                    return ins
                                S.op("pe", f, reads=[("wd", sl)] + [("aT", hq * 4 + j, tb // 4) for j in range(4)], writes=[("ps", tb)])
                        for tb in range(NT):
                            S.op("act", lambda e, tb=tb, dc=dc: e.activation(out=o_all[:, tb, dc * 512:(dc + 1) * 512], in_=ps[tb][:], func=AF.Copy),
                                 reads=[("ps", tb)], writes=[("o", tb, dc)])
                    if do_A:
                        sA0 = contextlib.ExitStack()
                        gtA = sbt(sA0, "gtA", [128, D], F32)[:]
                        xsA = sbt(sA0, "xsA", [128, D], BF16)[:]
                        S.dma("sp", lambda e: e.dma_start(out=gtA, in_=gpre_d.partition_broadcast(128)), None, writes=["gtA"])
                    for tb in range(NT):
                        c = tb % 2
                        okeys = [("o", tb, dc) for dc in range(4)]
                        S.dma("sp", lambda e, tb=tb: e.dma_start(out=xb6, in_=x1_d[tb * 128:(tb + 1) * 128, :]), "xb", reads=[("x1d", tb)], writes=["xb6"])
                        S.op("act", lambda e, tb=tb, c=c: e.activation(out=sq6, in_=o_all[:, tb, :], func=AF.Square, accum_out=ss[:, c:c + 1]),
                             reads=okeys, writes=["sq6", ("ss", c)])
                        S.op("dve", lambda e, c=c: e.tensor_scalar(out=rs[:, c:c + 1], in0=ss[:, c:c + 1], scalar1=1.0 / D, scalar2=EPS,
                                                                   op0=ALU.mult, op1=ALU.add), reads=[("ss", c)], writes=[("rs", c)])
                        S.op("act", lambda e, c=c: e.activation(out=rs[:, c:c + 1], in_=rs[:, c:c + 1], func=AF.Sqrt), reads=[("rs", c)], writes=[("rs", c)])
                        S.op("dve", lambda e, c=c: e.reciprocal(out=rs[:, c:c + 1], in_=rs[:, c:c + 1]), reads=[("rs", c)], writes=[("rs", c)])
                        S.op("dve", lambda e, tb=tb, c=c: e.scalar_tensor_tensor(out=o_all[:, tb, :], in0=o_all[:, tb, :], scalar=rs[:, c:c + 1], in1=gt3,
                                                                                 op0=ALU.mult, op1=ALU.mult),
                             reads=okeys + [("rs", c), "gt3"], writes=okeys)
                        S.op("dve", lambda e, tb=tb: e.tensor_add(out=xb6, in0=xb6, in1=o_all[:, tb, :]), reads=okeys + ["xb6"], writes=["xb6"])
                        S.dma("sp", lambda e, tb=tb: e.dma_start(out=x_out[tb * 128:(tb + 1) * 128, :], in_=xb6), "xow", reads=["xb6"], writes=[("xout", tb)])
                        if do_A:
                            norm_transpose(xb6, "xb6", gtA, "gtA", xsA, "xsA", sq6, "sq6", hT, "hT", tb, 4 + c)
                    if do_A:
                        S.barrier()
                        sA0.close()
            S.barrier()

        if do_A:
            with contextlib.ExitStack() as sA:
                if not do_B:
                    a1 = Arena(R1, R1B)
                    gtA = a1.take([D], F32)
                    xsA = a1.take([D], BF16)
                    xbA = a1.take([D], F32)
                    sqA = a1.take([D], BF16)
                    S.dma("sp", lambda e: e.dma_start(out=gtA, in_=gpre_d.partition_broadcast(128)), None, writes=["gtA"])
                    for tb in range(NT):
                        S.dma("sp", lambda e, tb=tb: e.dma_start(out=xbA, in_=x_in[tb * 128:(tb + 1) * 128, :]), "xb", writes=["xbA"])
                        norm_transpose(xbA, "xbA", gtA, "gtA", xsA, "xsA", sqA, "sqA", hT, "hT", tb, 4 + tb % 2)
                    S.barrier()
                a1 = Arena(R1, R1B)
                qko = a1.take([16, T], BF16)
                asb = a1.take([4, T], F32)
                hgo = a1.take([4, T], F32)
                a2 = Arena(R2, R2B, start=32768)
                wi = [a2.take([16, 512], BF16) for i in range(2)]
                upo = a2.take([4, T], F32)
                ropec = a2.take([T], F32)
                ropes = a2.take([T], F32)
                pmat = sbt(sA, "pmat_sb", [128, 128], F32)
                sgA = [sbt(sA, f"sgA{i}", [128, 512], F32) for i in range(2)]
                qf = [sbt(sA, f"qf{i}", [128, 512], F32) for i in range(2)]
                t1 = [sbt(sA, f"t1{i}", [128, 512], F32) for i in range(2)]
                t2 = [sbt(sA, f"t2{i}", [128, 512], F32) for i in range(2)]
                vsb = sbt(sA, "vsb", [128, NT, 16, 65], BF16)
                S.dma("sp", lambda e: e.dma_start(out=ropec, in_=ropec_d), None, writes=["ropec"])
                S.dma("sp", lambda e: e.dma_start(out=ropes, in_=ropes_d), None, writes=["ropes"])
                S.dma("sp", lambda e: e.dma_start(out=pmat[:], in_=pmat_d), None, writes=["pmat"])
                S.op("dve", lambda e: e.memset(vsb[:], 1.0), writes=["vsb_ones"])
                hkeys = [("hT", tb, half) for tb in range(NT) for half in range(2)]
                wiv = win_d.rearrange("(kc p) n -> p kc n", p=128)
                nmm = 0
                nr = 0
                for s_ in range(9):
                    sl = s_ % 2
                    S.dma("pool", lambda e, sl=sl, s_=s_: e.dma_start(out=wi[sl], in_=wiv[:, :, s_ * 512:(s_ + 1) * 512]), f"wi{sl}", writes=[("wi", sl)])
                    if s_ in (6, 7):
                        for tb in range(NT):
                            bank = nmm % 4
                            nmm += 1

                            def f(e, sl=sl, tb=tb, bank=bank):
                                for kc in range(16):
                                    ins = e.matmul(ps[bank][:], lhsT=hT[:, kc, tb * 128:(tb + 1) * 128], rhs=wi[sl][:, kc, :], start=(kc == 0), stop=(kc == 15))
                                return ins
                            S.op("pe", f, reads=[("wi", sl)] + hkeys, writes=[("ps", bank)])
                            h0 = (s_ - 6) * 8
                            S.op("act", lambda e, tb=tb, bank=bank, h0=h0: e.activation(out=vsb[:, tb, h0:h0 + 8, 0:64],
                                                                                        in_=ps[bank][:].rearrange("p (h e) -> p h e", h=8), func=AF.Copy),
                                 reads=[("ps", bank), "vsb_ones"], writes=[("vsb", tb, s_)])
                        continue
                    for oc in range(4):
                        for th in range(2):
                            tsl = slice(th * 512, (th + 1) * 512)
                            bank = nmm % 4
                            nmm += 1

                            def f(e, sl=sl, oc=oc, th=th, bank=bank):
                                for kc in range(16):
                                    ins = e.matmul(ps[bank][:], lhsT=wi[sl][:, kc, oc * 128:(oc + 1) * 128], rhs=hT[:, kc, th * 512:(th + 1) * 512],
                                                   start=(kc == 0), stop=(kc == 15))
                                return ins
                            S.op("pe", f, reads=[("wi", sl)] + hkeys, writes=[("ps", bank)])
                            if s_ == 0:
                                S.op("act", lambda e, oc=oc, tsl=tsl, bank=bank: e.activation(out=asb[:, oc, tsl], in_=ps[bank][:], func=AF.Copy),
                                     reads=[("ps", bank)], writes=[("asb", oc, th)])
                            elif s_ == 1:
                                k2 = nmm % 2
                                S.op("act", lambda e, bank=bank, k2=k2: e.activation(out=sgA[k2][:], in_=ps[bank][:], func=AF.Sigmoid),
                                     reads=[("ps", bank)], writes=[("sgA", k2)])
                                S.op("dve", lambda e, oc=oc, tsl=tsl, k2=k2: e.tensor_mul(out=hgo[:, oc, tsl], in0=asb[:, oc, tsl], in1=sgA[k2][:]),
                                     reads=[("sgA", k2), ("asb", oc, th)], writes=[("hgo", oc, th)])
                            elif s_ == 8:
                                S.op("act", lambda e, oc=oc, tsl=tsl, bank=bank: e.activation(out=upo[:, oc, tsl], in_=ps[bank][:], func=AF.Copy),
                                     reads=[("ps", bank)], writes=[("upo", oc, th)])
                            else:
                                ch = (s_ - 2) * 4 + oc
                                k2 = nr % 2
                                pb = 4 + k2
                                nr += 1
                                S.op("act", lambda e, bank=bank, k2=k2: e.activation(out=qf[k2][:], in_=ps[bank][:], func=AF.Copy),
                                     reads=[("ps", bank)], writes=[("qf", k2)])
                                S.op("pe", lambda e, k2=k2, pb=pb: e.matmul(ps[pb][:], lhsT=pmat[:], rhs=qf[k2][:], start=True, stop=True),
                                     reads=[("qf", k2), "pmat"], writes=[("ps", pb)])
                                S.op("dve", lambda e, k2=k2, pb=pb, tsl=tsl: e.tensor_mul(out=t1[k2][:], in0=ps[pb][:], in1=ropes[:, tsl]),
                                     reads=[("ps", pb), "ropes"], writes=[("t1", k2)])
                                S.op("pool", lambda e, k2=k2, tsl=tsl: e.tensor_mul(out=t2[k2][:], in0=qf[k2][:], in1=ropec[:, tsl]),
                                     reads=[("qf", k2), "ropec"], writes=[("t2", k2)])
                                S.op("dve", lambda e, k2=k2, ch=ch, tsl=tsl: e.tensor_add(out=qko[:, ch, tsl], in0=t1[k2][:], in1=t2[k2][:]),
                                     reads=[("t1", k2), ("t2", k2)], writes=[("qko", ch, th)])
                S.dma("sp", lambda e: e.dma_start(out=hg_o.rearrange("c p t -> p c t"), in_=hgo), "outA",
                      reads=[("hgo", c, th) for c in range(4) for th in range(2)], writes=["hg_o"])
                S.dma("sp", lambda e: e.dma_start(out=up_o.rearrange("c p t -> p c t"), in_=upo), "outA",
                      reads=[("upo", c, th) for c in range(4) for th in range(2)], writes=["up_o"])
                S.dma("sp", lambda e: e.dma_start(out=qT_o.rearrange("c p t -> p c t"), in_=qko[:, 0:8, :]), "outA",
                      reads=[("qko", c, th) for c in range(8) for th in range(2)], writes=["qT_o"])
                S.dma("sp", lambda e: e.dma_start(out=kT_o.rearrange("c p t -> p c t"), in_=qko[:, 8:16, :]), "outA",
                      reads=[("qko", c, th) for c in range(8, 16) for th in range(2)], writes=["kT_o"])
                S.dma("sp", lambda e: e.dma_start(out=v_o.rearrange("t p f -> p t f"), in_=vsb[:].rearrange("p t h e -> p t (h e)")), "outA",
                      reads=[("vsb", tb, s_) for tb in range(NT) for s_ in (6, 7)], writes=["v_o"])
        S.barrier()
        S.run()
    return nc


_PROGS = {}


def _prog(do_B, do_A):
    key = (do_B, do_A)
    if key not in _PROGS:
        _PROGS[key] = build_program(do_B, do_A)
    return _PROGS[key]


def _const_tables():
    o = np.arange(-8 * 128 - 127, 8 * 128 + 128)
    w = ((np.abs(o) <= 64).astype(np.float32) + ((o % 4 == 0) & (np.abs(o) <= 256)) + ((o % 16 == 0) & (np.abs(o) <= 1024)))
    wmap = dict(zip(o.tolist(), w.tolist()))
    k = np.arange(128)[:, None, None]
    rel = np.arange(-8, 9)[None, :, None]
    q = np.arange(128)[None, None, :]
    off = rel * 128 + k - q
    masks = np.vectorize(wmap.get)(off).astype(np.float32).reshape(128, 17 * 128).astype(ml_dtypes.bfloat16)
    pm = np.zeros((128, 128), np.float32)
    for hh in range(2):
        for m in range(8):
            pm[hh * 64 + m + 8, hh * 64 + m] = -1.0
            pm[hh * 64 + m, hh * 64 + m + 8] = 1.0
    return masks, pm


def _rope_tables(c):
    pos = (np.arange(T) + c * T).astype(np.float32)
    inv = (np.float32(500000.0) ** (-np.arange(0, 16, 2, dtype=np.float32) / np.float32(16))).astype(np.float32)
    ang = (pos[:, None] * inv[None, :]).astype(np.float32)
    cs, sn = np.cos(ang).astype(np.float32), np.sin(ang).astype(np.float32)
    C = np.ones((128, T), np.float32)
    Sn = np.zeros((128, T), np.float32)
    for hh in range(2):
        for i in range(16):
            C[hh * 64 + i] = cs[:, i % 8]
            Sn[hh * 64 + i] = sn[:, i % 8]
    return C, Sn


def _corr_table(c):
    out = np.ones((4, 16), np.float32)
    idx = np.concatenate([np.arange(8), np.arange(T - 8, T)]) + c * T
    for gi, w in enumerate((2, 4, 8, 16)):
        lo = np.clip(idx - w // 2, 0, S_LEN)
        hi = np.clip(idx + w - w // 2, 0, S_LEN)
        out[gi] = w / (hi - lo).astype(np.float32)
    return np.ascontiguousarray(np.broadcast_to(out[None], (128, 4, 16)))


def _colmajor(v, n):
    return np.ascontiguousarray(v.reshape(n, 128).T)


def _run_step(step, xs, Aout, P, consts):
    f32 = np.float32
    masks, pm, ident, ropes, corrs = consts
    do_B = step > 0
    do_A = step < DEPTH
    lb, la = step - 1, step
    nc = _prog(do_B, do_A)
    maps = []
    if do_B:
        kT = np.concatenate([Aout[c]["kT_o"] for c in range(NCORES)], axis=2)
        kT = np.pad(kT, ((0, 0), (0, 0), (T, T)))
        vv = np.concatenate([Aout[c]["v_o"].reshape(T, 8, 130) for c in range(NCORES)], axis=0)
        vv = np.pad(vv, ((T, T), (0, 0), (0, 0)))
        hgf = np.pad(np.concatenate([Aout[c]["hg_o"] for c in range(NCORES)], axis=2), ((0, 0), (0, 0), (15, 15)))
        upf = np.pad(np.concatenate([Aout[c]["up_o"] for c in range(NCORES)], axis=2), ((0, 0), (0, 0), (8, 8)))
        cw = np.ascontiguousarray(np.asarray(P["conv_w"][lb], f32).T.reshape(4, 128, 31).transpose(1, 0, 2))
        vec4 = np.stack([_colmajor(np.asarray(P[k][lb], f32), 4) for k in ("conv_b", "conv_ln_g", "conv_ln_b", "pool_scale")], axis=1)
        gm = np.asarray(P["g_mix"][lb], f32)
    for c in range(NCORES):
        m = {"ident": ident, "x_in": xs[c]}
        if do_B:
            vh = vv[c * T:c * T + 3 * T].reshape(24, 128, 8, 130).transpose(2, 1, 0, 3)
            m.update({
                "qT": Aout[c]["qT_o"],
                "kTh": np.ascontiguousarray(kT[:, :, c * T:c * T + 3 * T]),
                "vh": np.ascontiguousarray(vh),
                "hgh": np.ascontiguousarray(hgf[:, :, c * T:c * T + T + 30]),
                "uh": np.ascontiguousarray(upf[:, :, c * T:c * T + T + 16]),
                "masks": masks, "corr": corrs[c], "cw": cw, "vec4": np.ascontiguousarray(vec4),
                "gmixT": _colmajor(gm, 16), "gmixB": np.ascontiguousarray(gm[None, 512:1536]),
                "poolw": np.asarray(P["pool_w"][lb], f32), "w_out": np.asarray(P["w_out"][lb], f32),
                "g_post_mix": np.asarray(P["g_post_mix"][lb], f32)[None], "g_pre_ffn": np.asarray(P["g_pre_ffn"][lb], f32)[None],
                "w_gate": np.asarray(P["w_gate"][lb], f32), "w_up": np.asarray(P["w_up"][lb], f32),
                "w_down": np.asarray(P["w_down"][lb], f32), "g_post_ffn": np.asarray(P["g_post_ffn"][lb], f32)[None],
            })
        if do_A:
            m.update({"w_in": np.asarray(P["w_in"][la], f32), "g_pre_mix": np.asarray(P["g_pre_mix"][la], f32)[None],
                      "rope_c": ropes[c][0], "rope_s": ropes[c][1], "pmat": pm})
        maps.append(m)
    res = run_bass_kernel_spmd(nc, maps, core_ids=list(range(NCORES)))
    Aout = res.results
    if do_B:
        xs = [np.asarray(Aout[c]["x_out"], f32) for c in range(NCORES)]
    return xs, Aout


def _consts():
    masks, pm = _const_tables()
    ident = np.eye(128, dtype=np.float32)
    ropes = [_rope_tables(c) for c in range(NCORES)]
    corrs = [_corr_table(c) for c in range(NCORES)]
    return masks, pm, ident, ropes, corrs


def kernel(x, w_in, conv_w, conv_b, conv_ln_g, conv_ln_b, pool_w, pool_scale, g_mix,
           w_out, g_pre_mix, g_post_mix, g_pre_ffn, g_post_ffn, w_gate, w_up, w_down):
    P = dict(w_in=w_in, conv_w=conv_w, conv_b=conv_b, conv_ln_g=conv_ln_g, conv_ln_b=conv_ln_b, pool_w=pool_w,
             pool_scale=pool_scale, g_mix=g_mix, w_out=w_out, g_pre_mix=g_pre_mix, g_post_mix=g_post_mix,
             g_pre_ffn=g_pre_ffn, g_post_ffn=g_post_ffn, w_gate=w_gate, w_up=w_up, w_down=w_down)
    x = np.asarray(x, np.float32)
    consts = _consts()
    xs = [np.ascontiguousarray(x[0, c * T:(c + 1) * T]) for c in range(NCORES)]
    Aout = None
    for step in range(DEPTH + 1):
        xs, Aout = _run_step(step, xs, Aout, P, consts)
    return np.concatenate(xs, axis=0)[None].astype(np.float32)
```

```python
import contextlib
import numpy as np
import ml_dtypes
import concourse.bass as bass
import concourse.mybir as mybir
from concourse.bass_utils import run_bass_kernel_spmd

F32 = mybir.dt.float32
BF16 = mybir.dt.bfloat16
AF = mybir.ActivationFunctionType
ALU = mybir.AluOpType

NCORES = 8
S_LEN = 8192
T = 1024
NT = 8
D = 2048
DEPTH = 4
INW = 4608
FF = 5632
NHC = FF // 128
EPS = 1e-6
ENGS = ("pe", "act", "dve", "pool", "sp")
import os
DEBUG = bool(os.environ.get("KDEBUG"))


class Sched:
    def __init__(self, nc):
        self.nc = nc
        self.ops = {e: [] for e in ENGS}
        self.cnt = {}
        self.waited = {e: {} for e in ENGS}
        self.last_w = {}
        self.readers = {}
        self.semnames = set(ENGS)
        self.sem_h = {}

    def _deps(self, eng, reads, writes):
        need = {}

        def add(k, v):
            if k == "pe" and eng == "pe":
                return
            if need.get(k, 0) < v:
                need[k] = v

        for k in reads:
            t = self.last_w.get(k)
            if t is not None:
                add(*t)
        for k in writes:
            t = self.last_w.get(k)
            if t is not None:
                add(*t)
            for kk, vv in self.readers.get(k, {}).items():
                add(kk, vv)
        return self._filter(eng, need)

    def _filter(self, eng, need):
        out = []
        w = self.waited[eng]
        for k, v in need.items():
            if w.get(k, 0) < v:
                w[k] = v
                out.append((k, v))
        return out

    def _track(self, tok, reads, writes):
        for k in writes:
            self.last_w[k] = tok
            self.readers[k] = {}
        for k in reads:
            r = self.readers.setdefault(k, {})
            if r.get(tok[0], 0) < tok[1]:
                r[tok[0]] = tok[1]

    def op(self, eng, fn, reads=(), writes=()):
        waits = self._deps(eng, reads, writes)
        self.cnt[eng] = self.cnt.get(eng, 0) + 1
        tok = (eng, self.cnt[eng])
        self._track(tok, reads, writes)
        self.ops[eng].append((waits, fn, eng, 1))
        return tok

    def dma(self, eng, fn, sem, reads=(), writes=()):
        if sem is None:
            self.nuniq = getattr(self, "nuniq", 0) + 1
            sem = f"u{self.nuniq}"
        self.semnames.add(sem)
        waits = self._deps(eng, reads, writes)
        self.cnt[sem] = self.cnt.get(sem, 0) + 16
        tok = (sem, self.cnt[sem])
        self._track(tok, reads, writes)
        self.ops[eng].append((waits, fn, sem, 16))
        return tok

    def barrier(self):
        for e in ENGS:
            waits = self._filter(e, dict(self.cnt))
            if waits:
                self.ops[e].append((waits, None, None, 0))

    def run(self):
        nc = self.nc
        with contextlib.ExitStack() as st:
            for name in sorted(self.semnames):
                self.sem_h[name] = st.enter_context(nc.semaphore("s_" + name))
            block = st.enter_context(nc.Block())
            sem_h = self.sem_h

            def replay(e, lst):
                for waits, fn, sem, inc in lst:
                    for k, v in waits:
                        e.wait_ge(sem_h[k], v)
                    if fn is not None:
                        fn(e).then_inc(sem_h[sem], inc)

            @block.tensor
            def _(e):
                replay(e, self.ops["pe"])

            @block.scalar
            def _(e):
                replay(e, self.ops["act"])

            @block.vector
            def _(e):
                replay(e, self.ops["dve"])

            @block.gpsimd
            def _(e):
                replay(e, self.ops["pool"])

            @block.sync
            def _(e):
                replay(e, self.ops["sp"])


class Arena:
    def __init__(self, t, nbytes, start=0):
        self.t, self.n, self.off = t, nbytes, start

    def take(self, shape, dt):
        esz = 2 if dt == BF16 else 4
        n = esz
        for d in shape:
            n *= d
        n = (n + 31) // 32 * 32
        assert self.off + n <= self.n, (self.off, n, self.n)
        ap = self.t[:, self.off // 4:(self.off + n) // 4]
        if dt != F32:
            ap = ap.bitcast(dt)
        tot = 1
        for d in shape:
            tot *= d
        ap = ap[:, 0:tot]
        if len(shape) == 2:
            ap = ap.rearrange("p (a b) -> p a b", a=shape[0])
        elif len(shape) == 3:
            ap = ap.rearrange("p (a b c) -> p a b c", a=shape[0], b=shape[1])
        self.off += n
        return ap


def build_program(do_B, do_A):
    nc = bass.Bass("TRN2", target_bir_lowering=False)
    S = Sched(nc)

    def din(name, shape, dt=F32):
        return nc.dram_tensor(name, list(shape), dt, kind="ExternalInput").ap()

    def dout(name, shape, dt=F32):
        return nc.dram_tensor(name, list(shape), dt, kind="ExternalOutput").ap()

    ident_d = din("ident", [128, 128])
    x_in = din("x_in", [T, D])
    if do_B:
        qT_d = din("qT", [8, 128, T], BF16)
        kT_d = din("kTh", [8, 128, 3 * T], BF16)
        v_d = din("vh", [8, 128, 24, 130], BF16)
        hg_d = din("hgh", [4, 128, T + 30])
        up_d = din("uh", [4, 128, T + 16])
        mask_d = din("masks", [128, 17 * 128], BF16)
        corr_d = din("corr", [128, 4, 16])
        cw_d = din("cw", [128, 4, 31])
        vec4_d = din("vec4", [128, 4, 4])
        gmixT_d = din("gmixT", [128, 16])
        gmixB_d = din("gmixB", [1, 1024])
        poolw_d = din("poolw", [4, 128, 128])
        wout_d = din("w_out", [D, D])
        gpm_d = din("g_post_mix", [1, D])
        gpf_d = din("g_pre_ffn", [1, D])
        wg_d = din("w_gate", [D, FF])
        wu_d = din("w_up", [D, FF])
        wd_d = din("w_down", [FF, D])
        gpo_d = din("g_post_ffn", [1, D])
        x_out = dout("x_out", [T, D])
        if DEBUG:
            x1_d = dout("x1_dbg", [T, D])
            yT_dbg = dout("yT_dbg", [16, 128, T], BF16)
        else:
            x1_d = nc.dram_tensor("x1_scratch", [T, D], F32).ap()
    if do_A:
        win_d = din("w_in", [D, INW])
        gpre_d = din("g_pre_mix", [1, D])
        ropec_d = din("rope_c", [128, T])
        ropes_d = din("rope_s", [128, T])
        pmat_d = din("pmat", [128, 128])
        qT_o = dout("qT_o", [8, 128, T], BF16)
        kT_o = dout("kT_o", [8, 128, T], BF16)
        v_o = dout("v_o", [NT, 128, 16 * 65], BF16)
        hg_o = dout("hg_o", [4, 128, T])
        up_o = dout("up_o", [4, 128, T])

    with contextlib.ExitStack() as top:
        def sbt(st, name, shape, dt):
            return st.enter_context(nc.sbuf_tensor("sb_" + name, list(shape), dt))

        ps = [top.enter_context(nc.psum_tensor(f"ps{i}", [128, 512], F32)) for i in range(8)]
        ident = sbt(top, "ident_sb", [128, 128], BF16)
        onesf = sbt(top, "onesf", [128, 128], F32)
        R1 = sbt(top, "R1", [128, 16384], F32)
        R2 = sbt(top, "R2", [128, 22528], F32)
        R1B, R2B = 65536, 90112
        h2T = Arena(R1, R1B).take([16, T], BF16)
        o_all = Arena(R1, R1B).take([NT, D], F32)
        aT = Arena(R2, R2B).take([NHC, T], BF16)
        hT = Arena(R2, R2B).take([16, T], BF16)
        ss = sbt(top, "ss", [128, 16], F32)
        rs = sbt(top, "rs", [128, 16], F32)
        ptr = ps[7][:].bitcast(BF16).rearrange("p (j t) -> p j t", j=8)

        S.dma("pool", lambda e: e.dma_start(out=ident[:], in_=ident_d), None, writes=["ident"])
        S.op("dve", lambda e: e.memset(onesf[:], 1.0 / 512.0), writes=["onesf"])

        def norm_transpose(xblk, xkey, gt, gkey, xs, xskey, sq, sqkey, dstT, dkey, tb, slot):
            c = slot
            S.op("act", lambda e: e.activation(out=sq, in_=xblk, func=AF.Square, accum_out=ss[:, c:c + 1]),
                 reads=[xkey], writes=[sqkey, ("ss", c)])
            S.op("dve", lambda e: e.tensor_scalar(out=rs[:, c:c + 1], in0=ss[:, c:c + 1], scalar1=1.0 / D, scalar2=EPS,
                                                  op0=ALU.mult, op1=ALU.add), reads=[("ss", c)], writes=[("rs", c)])
            S.op("act", lambda e: e.activation(out=rs[:, c:c + 1], in_=rs[:, c:c + 1], func=AF.Sqrt),
                 reads=[("rs", c)], writes=[("rs", c)])
            S.op("dve", lambda e: e.reciprocal(out=rs[:, c:c + 1], in_=rs[:, c:c + 1]), reads=[("rs", c)], writes=[("rs", c)])
            S.op("dve", lambda e: e.scalar_tensor_tensor(out=xs, in0=xblk, scalar=rs[:, c:c + 1], in1=gt,
                                                         op0=ALU.mult, op1=ALU.mult),
                 reads=[xkey, gkey, ("rs", c)], writes=[xskey])
            for half in range(2):
                def tr(e, half=half):
                    for j in range(8):
                        kc = half * 8 + j
                        i = e.transpose(out=ptr[:, j, :], in_=xs[:, kc * 128:(kc + 1) * 128], identity=ident[:])
                    return i
                S.op("pe", tr, reads=[xskey, "ident"], writes=[("ps", 7)])
                S.op("act", lambda e, half=half: e.activation(out=dstT[:, half * 8:(half + 1) * 8, tb * 128:(tb + 1) * 128],
                                                              in_=ptr[:], func=AF.Copy),
                     reads=[("ps", 7)], writes=[(dkey, tb, half)])

        if do_B:
            with contextlib.ExitStack() as sB:
                gmixT = sbt(sB, "gmixT", [128, 16], F32)
                vec4 = sbt(sB, "vec4", [128, 4, 4], F32)
                sY = contextlib.ExitStack()
                yT = sbt(sY, "yT", [128, 16, T], BF16)
                S.dma("sp", lambda e: e.dma_start(out=gmixT[:], in_=gmixT_d), None, writes=["gmixT"])
                S.dma("sp", lambda e: e.dma_start(out=vec4[:], in_=vec4_d), None, writes=["vec4"])

                with contextlib.ExitStack() as s1:
                    a2 = Arena(R2, R2B)
                    yb = a2.take([NT, 1024], F32)
                    hg = a2.take([4, T + 30], F32)
                    acc = a2.take([4, T], F32)
                    uh = a2.take([4, T + 16], F32)
                    masks = a2.take([17 * 128], BF16)
                    ybs = a2.take([1024], BF16)
                    a1 = Arena(R1, R1B)
                    pw = a1.take([3, T + 16], F32)
                    pooled = a1.take([4, T], BF16)
                    sqb = a1.take([1024], BF16)
                    gmixB = a1.take([1024], F32)
                    tmpA = a1.take([512], F32)
                    tmpB = a1.take([512], F32)
                    meansb = a1.take([512], F32)
                    rstdsb = a1.take([512], F32)
                    qkv_v = [a1.take([24, 130], BF16) for i in range(2)]
                    qkv_q = [a1.take([T], BF16) for i in range(2)]
                    qkv_k = [a1.take([3 * T], BF16) for i in range(2)]
                    ya = acc
                    yc = hg
                    cw = sbt(s1, "cw", [128, 4, 31], F32)
                    corr = sbt(s1, "corr", [128, 4, 16], F32)
                    poolw = sbt(s1, "poolw", [128, 4, 128], BF16)
                    pT = [sbt(s1, f"pT{i}", [128, 512], BF16) for i in range(4)]
                    rec = sbt(s1, "rec", [128, 2, 2], F32)
                    pwin = sbt(s1, "pwin", [128, 4, T], F32)

                    S.dma("sp", lambda e: e.dma_start(out=hg, in_=hg_d.rearrange("c p t -> p c t")), None, writes=["hg"])
                    S.dma("sp", lambda e: e.dma_start(out=uh, in_=up_d.rearrange("c p t -> p c t")), None, writes=["uh"])
                    S.dma("sp", lambda e: e.dma_start(out=cw[:], in_=cw_d), None, writes=["cw"])
                    S.dma("sp", lambda e: e.dma_start(out=corr[:], in_=corr_d), None, writes=["corr"])
                    S.dma("sp", lambda e: e.dma_start(out=masks, in_=mask_d), None, writes=["masks"])
                    S.dma("sp", lambda e: e.dma_start(out=gmixB, in_=gmixB_d.partition_broadcast(128)), None, writes=["gmixB"])
                    S.dma("pool", lambda e: e.dma_start(out=poolw[:], in_=poolw_d.rearrange("g c d -> c g d")), None, writes=["poolw"])

                    def load_qkv(hp):
                        sl = hp % 2
                        S.dma("sp", lambda e: e.dma_start(out=qkv_q[sl], in_=qT_d[hp]), f"q{sl}", writes=[("q", sl)])
                        S.dma("sp", lambda e: e.dma_start(out=qkv_k[sl], in_=kT_d[hp]), f"k{sl}", writes=[("k", sl)])
                        S.dma("sp", lambda e: e.dma_start(out=qkv_v[sl], in_=v_d[hp]), f"v{sl}", writes=[("v", sl)])

                    load_qkv(0)

                    bg_ops = []
                    for c in range(4):
                        bg_ops.append(lambda c=c: S.op("dve", lambda e: e.tensor_scalar(out=acc[:, c, :], in0=hg[:, c, 0:T], scalar1=cw[:, c, 0:1],
                                                                                      scalar2=vec4[:, 0, c:c + 1], op0=ALU.mult, op1=ALU.add),
                                                       reads=["hg", "cw", "vec4"], writes=[("acc", c)]))
                        for j in range(1, 31):
                            bg_ops.append(lambda c=c, j=j: S.op("dve", lambda e: e.scalar_tensor_tensor(
                                out=acc[:, c, :], in0=hg[:, c, j:j + T], scalar=cw[:, c, j:j + 1], in1=acc[:, c, :], op0=ALU.mult, op1=ALU.add),
                                reads=["hg", "cw", ("acc", c)], writes=[("acc", c)]))
                    for gi in range(4):
                        w = 2 << gi
                        src = uh[:, gi, :]
                        L = T + 16
                        step = 1
                        lvl = 0
                        while step < w:
                            L2 = L - step
                            dst = pw[:, lvl % 3, :]
                            S.op("pool", lambda e, src=src, dst=dst, L2=L2, step=step: e.tensor_add(
                                out=dst[:, 0:L2], in0=src[:, 0:L2], in1=src[:, step:step + L2]),
                                reads=["uh", ("pw", (lvl + 2) % 3)], writes=[("pw", lvl % 3)])
                            src = dst
                            L = L2
                            step *= 2
                            lvl += 1
                        off = 8 - w // 2
                        win = src[:, off:off + T]
                        lastkey = ("pw", (lvl - 1) % 3)
                        S.op("pool", lambda e, win=win, gi=gi: e.tensor_mul(out=win[:, 0:8], in0=win[:, 0:8], in1=corr[:, gi, 0:8]),
                             reads=[lastkey, "corr"], writes=[lastkey])
                        S.op("pool", lambda e, win=win, gi=gi: e.tensor_mul(out=win[:, T - 8:T], in0=win[:, T - 8:T], in1=corr[:, gi, 8:16]),
                             reads=[lastkey], writes=[lastkey])
                        pwk = ("pwin", gi)
                        S.op("pool", lambda e, win=win, gi=gi: e.tensor_copy(out=pwin[:, gi, :], in_=win), reads=[lastkey], writes=[pwk])
                        bg_ops.append(lambda gi=gi, w=w, pwk=pwk: S.op("dve", lambda e: e.scalar_tensor_tensor(
                            out=pooled[:, gi, :], in0=pwin[:, gi, :], scalar=1.0 / w, in1=uh[:, gi, 8:8 + T], op0=ALU.mult, op1=ALU.subtract),
                            reads=[pwk, "uh"], writes=[("pooled", gi)]))

                    units = []
                    for hp in range(8):
                        for qb in range(NT):
                            for h in range(2):
                                for g, (r0, n) in enumerate(((-8, 4), (-4, 4), (0, 4), (4, 4), (8, 1))):
                                    units.append((hp, qb, h, g, r0, n))
                    NU = len(units)

                    def emit_qk(i):
                        hp, qb, h, g, r0, n = units[i]
                        sl = hp % 2
                        bank = i % 3
                        kb0 = qb + 8 + r0

                        def f(e):
                            for j in range(n):
                                ins = e.matmul(ps[bank][:, j * 128:(j + 1) * 128],
                                               lhsT=qkv_k[sl][h * 64:(h + 1) * 64, (kb0 + j) * 128:(kb0 + j + 1) * 128],
                                               rhs=qkv_q[sl][h * 64:(h + 1) * 64, qb * 128:(qb + 1) * 128], start=True, stop=True)
                            return ins
                        S.op("pe", f, reads=[("q", sl), ("k", sl)], writes=[("ps", bank)])
                        slot = i % 4
                        S.op("act", lambda e: e.activation(out=pT[slot][:, 0:n * 128], in_=ps[bank][:, 0:n * 128], func=AF.Exp, scale=0.125),
                             reads=[("ps", bank)], writes=[("pT", slot)])
                        S.op("dve", lambda e: e.tensor_mul(out=pT[slot][:, 0:n * 128], in0=pT[slot][:, 0:n * 128],
                                                           in1=masks[:, (r0 + 8) * 128:(r0 + 8 + n) * 128]),
                             reads=[("pT", slot), "masks"], writes=[("pT", slot)])

                    def emit_pv(i):
                        hp, qb, h, g, r0, n = units[i]
                        sl = hp % 2
                        slot = i % 4
                        ob = 3 + ((hp * NT + qb) % 2)
                        kb0 = qb + 8 + r0

                        def f(e):
                            for j in range(n):
                                ins = e.matmul(ps[ob][:, h * 128:h * 128 + 65], lhsT=pT[slot][:, j * 128:(j + 1) * 128],
                                               rhs=qkv_v[sl][:, kb0 + j, h * 65:(h + 1) * 65],
                                               start=(g == 0 and j == 0), stop=(g == 4))
                            return ins
                        S.op("pe", f, reads=[("pT", slot), ("v", sl)], writes=[("ps", ob)])
                        if h == 1 and g == 4:
                            S.op("dve", lambda e: e.reciprocal(out=rec[:, (hp * NT + qb) % 2, :], in_=ps[ob][:, 64:256:128]),
                                 reads=[("ps", ob)], writes=[("rec", ob)])
                            for hh in range(2):
                                S.op("dve", lambda e, hh=hh: e.tensor_scalar(
                                    out=yb[:, qb, hp * 128 + hh * 64:hp * 128 + hh * 64 + 64], in0=ps[ob][:, hh * 128:hh * 128 + 64],
                                    scalar1=rec[:, (hp * NT + qb) % 2, hh:hh + 1], scalar2=None, op0=ALU.mult),
                                    reads=[("ps", ob), ("rec", ob)], writes=[("yb", qb, hp, hh)])

                    LAG = 2
                    bg_ops.reverse()
                    for i in range(NU + LAG):
                        if i % 4 == 3 and bg_ops:
                            bg_ops.pop()()
                        if i < NU:
                            emit_qk(i)
                        if i - LAG >= 0:
                            emit_pv(i - LAG)
                        if i < NU:
                            hp_, qb_, h_, g_ = units[i][:4]
                            if qb_ == 0 and h_ == 0 and g_ == LAG and hp_ + 1 < 8:
                                load_qkv(hp_ + 1)

                    while bg_ops:
                        bg_ops.pop()()
                    ybkeys = [[("yb", qb, hp, hh) for hp in range(8) for hh in range(2)] for qb in range(NT)]
                    for qb in range(NT):
                        c = 8 + (qb % 2)
                        S.op("act", lambda e, qb=qb, c=c: e.activation(out=sqb, in_=yb[:, qb, :], func=AF.Square, accum_out=ss[:, c:c + 1]),
                             reads=ybkeys[qb], writes=["sqb", ("ss", c)])
                        S.op("dve", lambda e, c=c: e.tensor_scalar(out=rs[:, c:c + 1], in0=ss[:, c:c + 1], scalar1=1.0 / 1024.0, scalar2=EPS,
                                                                   op0=ALU.mult, op1=ALU.add), reads=[("ss", c)], writes=[("rs", c)])
                        S.op("act", lambda e, c=c: e.activation(out=rs[:, c:c + 1], in_=rs[:, c:c + 1], func=AF.Sqrt),
                             reads=[("rs", c)], writes=[("rs", c)])
                        S.op("dve", lambda e, c=c: e.reciprocal(out=rs[:, c:c + 1], in_=rs[:, c:c + 1]), reads=[("rs", c)], writes=[("rs", c)])
                        S.op("dve", lambda e, qb=qb, c=c: e.scalar_tensor_tensor(out=ybs, in0=yb[:, qb, :], scalar=rs[:, c:c + 1], in1=gmixB,
                                                                                 op0=ALU.mult, op1=ALU.mult),
                             reads=ybkeys[qb] + [("rs", c), "gmixB"], writes=["ybs"])

                        def tr(e):
                            for j in range(8):
                                i_ = e.transpose(out=ptr[:, j, :], in_=ybs[:, j * 128:(j + 1) * 128], identity=ident[:])
                            return i_
                        S.op("pe", tr, reads=["ybs", "ident"], writes=[("ps", 7)])
                        S.op("act", lambda e, qb=qb: e.activation(out=yT[:, 4:12, qb * 128:(qb + 1) * 128], in_=ptr[:], func=AF.Copy),
                             reads=[("ps", 7)], writes=[("yT", "b", qb)])

                    def rms_feat(src, srckeys, base, tagk):
                        for th in range(2):
                            tsl = slice(th * 512, (th + 1) * 512)
                            for c in range(4):
                                S.op("act", lambda e, c=c, tsl=tsl: e.activation(out=tmpA, in_=src[:, c, tsl], func=AF.Square),
                                     reads=[srckeys(c, th)], writes=["tmpA"])
                                S.op("pe", lambda e, c=c: e.matmul(ps[0][:], lhsT=onesf[:], rhs=tmpA, start=(c == 0), stop=(c == 3)),
                                     reads=["tmpA", "onesf"], writes=[("ps", 0)])
                            S.op("dve", lambda e: e.tensor_scalar(out=rstdsb, in0=ps[0][:], scalar1=EPS, scalar2=None, op0=ALU.add),
                                 reads=[("ps", 0)], writes=["rstdsb"])
                            S.op("act", lambda e: e.activation(out=rstdsb, in_=rstdsb, func=AF.Sqrt), reads=["rstdsb"], writes=["rstdsb"])
                            S.op("dve", lambda e: e.reciprocal(out=rstdsb, in_=rstdsb), reads=["rstdsb"], writes=["rstdsb"])
                            for c in range(4):
                                S.op("dve", lambda e, c=c, tsl=tsl: e.scalar_tensor_tensor(out=yT[:, base + c, tsl], in0=src[:, c, tsl],
                                                                                  scalar=gmixT[:, base + c:base + c + 1], in1=rstdsb,
                                                                                  op0=ALU.mult, op1=ALU.mult),
                                     reads=[srckeys(c, th), "rstdsb", "gmixT"], writes=[("yT", tagk, c, th)])

                    for th in range(2):
                        tsl = slice(th * 512, (th + 1) * 512)
                        for c in range(4):
                            S.op("pe", lambda e, c=c, tsl=tsl: e.matmul(ps[1][:], lhsT=onesf[:], rhs=acc[:, c, tsl], start=(c == 0), stop=(c == 3)),
                                 reads=[("acc", c), "onesf"], writes=[("ps", 1)])
                        for c in range(4):
                            S.op("act", lambda e, c=c, tsl=tsl: e.activation(out=tmpA, in_=acc[:, c, tsl], func=AF.Square),
                                 reads=[("acc", c)], writes=["tmpA"])
                            S.op("pe", lambda e, c=c: e.matmul(ps[2][:], lhsT=onesf[:], rhs=tmpA, start=(c == 0), stop=(c == 3)),
                                 reads=["tmpA", "onesf"], writes=[("ps", 2)])
                        S.op("act", lambda e: e.activation(out=meansb, in_=ps[1][:], func=AF.Copy), reads=[("ps", 1)], writes=["meansb"])
                        S.op("dve", lambda e: e.tensor_mul(out=tmpB, in0=meansb, in1=meansb), reads=["meansb"], writes=["tmpB"])
                        S.op("dve", lambda e: e.tensor_sub(out=rstdsb, in0=ps[2][:], in1=tmpB), reads=[("ps", 2), "tmpB"], writes=["rstdsb"])
                        S.op("dve", lambda e: e.tensor_scalar(out=rstdsb, in0=rstdsb, scalar1=EPS, scalar2=None, op0=ALU.add),
                             reads=["rstdsb"], writes=["rstdsb"])
                        S.op("act", lambda e: e.activation(out=rstdsb, in_=rstdsb, func=AF.Sqrt), reads=["rstdsb"], writes=["rstdsb"])
                        S.op("dve", lambda e: e.reciprocal(out=rstdsb, in_=rstdsb), reads=["rstdsb"], writes=["rstdsb"])
                        for c in range(4):
                            S.op("dve", lambda e, c=c, tsl=tsl: e.tensor_sub(out=tmpB, in0=acc[:, c, tsl], in1=meansb),
                                 reads=[("acc", c), "meansb"], writes=["tmpB"])
                            S.op("dve", lambda e: e.tensor_mul(out=tmpB, in0=tmpB, in1=rstdsb), reads=["tmpB", "rstdsb"], writes=["tmpB"])
                            S.op("act", lambda e, c=c, tsl=tsl: e.activation(out=ya[:, c, tsl], in_=tmpB, func=AF.Silu,
                                                                    scale=vec4[:, 1, c:c + 1], bias=vec4[:, 2, c:c + 1]),
                                 reads=["tmpB", "vec4"], writes=[("ya", c, th)])
                    rms_feat(ya, lambda c, th: ("ya", c, th), 0, "a")

                    for gi in range(4):
                        for th in range(2):
                            tsl = slice(th * 512, (th + 1) * 512)
                            S.op("pe", lambda e, gi=gi, tsl=tsl: e.matmul(ps[1][:], lhsT=poolw[:, gi, :], rhs=pooled[:, gi, tsl], start=True, stop=True),
                                 reads=[("pooled", gi), "poolw"], writes=[("ps", 1)])
                            S.op("act", lambda e, gi=gi, tsl=tsl: e.activation(out=yc[:, gi, tsl], in_=ps[1][:], func=AF.Copy, scale=vec4[:, 3, gi:gi + 1]),
                                 reads=[("ps", 1), "vec4"], writes=[("yc", gi, th)])
                    rms_feat(yc, lambda c, th: ("yc", c, th), 12, "c")
                    if DEBUG:
                        S.barrier()
                        S.dma("sp", lambda e: e.dma_start(out=yT_dbg.rearrange("c p t -> p c t"), in_=yT[:]), "dbg", writes=["dbg"])
                S.barrier()

                with contextlib.ExitStack() as s4:
                    a2 = Arena(R2, R2B)
                    wo = [a2.take([16, 512], BF16) for i in range(2)]
                    omix = a2.take([4, D], F32)
                    gt1 = a2.take([D], F32)
                    gt2 = a2.take([D], F32)
                    xb = a2.take([D], F32)
                    sq4 = sbt(s4, "sq4", [128, D], BF16)[:]
                    xs4 = sbt(s4, "xs4", [128, D], BF16)[:]
                    S.dma("sp", lambda e: e.dma_start(out=gt1, in_=gpm_d.partition_broadcast(128)), None, writes=["gt1"])
                    S.dma("sp", lambda e: e.dma_start(out=gt2, in_=gpf_d.partition_broadcast(128)), None, writes=["gt2"])
                    wview = wout_d.rearrange("(kc p) n -> p kc n", p=128)
                    nslab = 0
                    for th in range(2):
                        for dc in range(4):
                            sl = nslab % 2
                            nslab += 1
                            S.dma("pool", lambda e, sl=sl, dc=dc: e.dma_start(out=wo[sl], in_=wview[:, :, dc * 512:(dc + 1) * 512]),
                                  f"wo{sl}", writes=[("wo", sl)])
                            for t4 in range(4):
                                tb = th * 4 + t4
                                bank = (dc * 4 + t4) % 4

                                def f(e, tb=tb, sl=sl, bank=bank):
                                    for kc in range(16):
                                        ins = e.matmul(ps[bank][:], lhsT=yT[:, kc, tb * 128:(tb + 1) * 128], rhs=wo[sl][:, kc, :],
                                                       start=(kc == 0), stop=(kc == 15))
                                    return ins
                                S.op("pe", f, reads=[("wo", sl)] + [("yT", "b", tb)] + [("yT", k, c, tb // 4) for k in ("a", "c") for c in range(4)],
                                     writes=[("ps", bank)])
                                S.op("act", lambda e, t4=t4, dc=dc, bank=bank: e.activation(out=omix[:, t4, dc * 512:(dc + 1) * 512], in_=ps[bank][:], func=AF.Copy),
                                     reads=[("ps", bank)], writes=[("omix", t4, dc)])
                        for t4 in range(4):
                            tb = th * 4 + t4
                            c = t4 % 2
                            okeys = [("omix", t4, dc) for dc in range(4)]
                            S.dma("sp", lambda e, tb=tb: e.dma_start(out=xb, in_=x_in[tb * 128:(tb + 1) * 128, :]), "xb", writes=["xb"])
                            S.op("act", lambda e, t4=t4, c=c: e.activation(out=sq4, in_=omix[:, t4, :], func=AF.Square, accum_out=ss[:, c:c + 1]),
                                 reads=okeys, writes=["sq4", ("ss", c)])
                            S.op("dve", lambda e, c=c: e.tensor_scalar(out=rs[:, c:c + 1], in0=ss[:, c:c + 1], scalar1=1.0 / D, scalar2=EPS,
                                                                       op0=ALU.mult, op1=ALU.add), reads=[("ss", c)], writes=[("rs", c)])
                            S.op("act", lambda e, c=c: e.activation(out=rs[:, c:c + 1], in_=rs[:, c:c + 1], func=AF.Sqrt), reads=[("rs", c)], writes=[("rs", c)])
                            S.op("dve", lambda e, c=c: e.reciprocal(out=rs[:, c:c + 1], in_=rs[:, c:c + 1]), reads=[("rs", c)], writes=[("rs", c)])
                            S.op("dve", lambda e, t4=t4, c=c: e.scalar_tensor_tensor(out=omix[:, t4, :], in0=omix[:, t4, :], scalar=rs[:, c:c + 1], in1=gt1,
                                                                                     op0=ALU.mult, op1=ALU.mult),
                                 reads=okeys + [("rs", c), "gt1"], writes=okeys)
                            S.op("dve", lambda e, t4=t4: e.tensor_add(out=xb, in0=xb, in1=omix[:, t4, :]), reads=okeys + ["xb"], writes=["xb"])
                            S.dma("sp", lambda e, tb=tb: e.dma_start(out=x1_d[tb * 128:(tb + 1) * 128, :], in_=xb), "x1w", reads=["xb"], writes=[("x1d", tb)])
                            norm_transpose(xb, "xb", gt2, "gt2", xs4, "xs4", sq4, "sq4", h2T, "h2T", tb, 2 + c)
                S.barrier()
                sY.close()

                with contextlib.ExitStack() as s5:
                    wg = [sbt(s5, f"wg{i}", [128, 16, 256], BF16) for i in range(2)]
                    wu = [sbt(s5, f"wu{i}", [128, 16, 256], BF16) for i in range(2)]
                    sg = [sbt(s5, f"sg{i}", [128, 512], F32) for i in range(2)]
                    gview = wg_d.rearrange("(kc p) n -> p kc n", p=128)
                    uview = wu_d.rearrange("(kc p) n -> p kc n", p=128)
                    h2keys = [("h2T", tb, half) for tb in range(NT) for half in range(2)]
                    n5 = 0
                    for hg2 in range(NHC // 2):
                        sl = hg2 % 2
                        S.dma("pool", lambda e, sl=sl, hg2=hg2: e.dma_start(out=wg[sl][:], in_=gview[:, :, hg2 * 256:(hg2 + 1) * 256]), f"wg{sl}", writes=[("wg", sl)])
                        S.dma("pool", lambda e, sl=sl, hg2=hg2: e.dma_start(out=wu[sl][:], in_=uview[:, :, hg2 * 256:(hg2 + 1) * 256]), f"wu{sl}", writes=[("wu", sl)])
                        for hh in range(2):
                            hc = hg2 * 2 + hh
                            for th in range(2):
                                bg = (n5 % 3) * 2
                                bu = bg + 1
                                sgs = n5 % 2
                                n5 += 1

                                def f(e, sl=sl, hh=hh, th=th, bg=bg, bu=bu):
                                    for kc in range(16):
                                        e.matmul(ps[bg][:], lhsT=wg[sl][:, kc, hh * 128:(hh + 1) * 128], rhs=h2T[:, kc, th * 512:(th + 1) * 512],
                                                 start=(kc == 0), stop=(kc == 15))
                                    for kc in range(16):
                                        ins = e.matmul(ps[bu][:], lhsT=wu[sl][:, kc, hh * 128:(hh + 1) * 128], rhs=h2T[:, kc, th * 512:(th + 1) * 512],
                                                       start=(kc == 0), stop=(kc == 15))
                                    return ins
                                S.op("pe", f, reads=[("wg", sl), ("wu", sl)] + h2keys, writes=[("ps", bg), ("ps", bu)])
                                S.op("act", lambda e, bg=bg, sgs=sgs: e.activation(out=sg[sgs][:], in_=ps[bg][:], func=AF.Silu),
                                     reads=[("ps", bg)], writes=[("sg", sgs)])
                                S.op("dve", lambda e, bu=bu, sgs=sgs, hc=hc, th=th: e.tensor_mul(out=aT[:, hc, th * 512:(th + 1) * 512], in0=sg[sgs][:], in1=ps[bu][:]),
                                     reads=[("sg", sgs), ("ps", bu)], writes=[("aT", hc, th)])
                S.barrier()

                with contextlib.ExitStack() as s6:
                    wdn = [sbt(s6, f"wd{i}", [128, 4, 512], BF16) for i in range(2)]
                    gt3 = sbt(s6, "gt3", [128, D], F32)[:]
                    xb6 = sbt(s6, "xb6", [128, D], F32)[:]
                    sq6 = sbt(s6, "sq6", [128, D], BF16)[:]
                    S.dma("sp", lambda e: e.dma_start(out=gt3, in_=gpo_d.partition_broadcast(128)), None, writes=["gt3"])
                    dview = wd_d.rearrange("(hc p) n -> p hc n", p=128)
                    n6 = 0
                    for dc in range(4):
                        for hq in range(11):
                            sl = n6 % 2
                            n6 += 1
                            S.dma("pool", lambda e, sl=sl, hq=hq, dc=dc: e.dma_start(out=wdn[sl][:], in_=dview[:, hq * 4:(hq + 1) * 4, dc * 512:(dc + 1) * 512]),
                                  f"wd{sl}", writes=[("wd", sl)])
                            for tb in range(NT):
                                def f(e, sl=sl, hq=hq, tb=tb):
                                    for j in range(4):
                                        hc = hq * 4 + j
                                        ins = e.matmul(ps[tb][:], lhsT=aT[:, hc, tb * 128:(tb + 1) * 128], rhs=wdn[sl][:, j, :],
                                                       start=(hc == 0), stop=(hc == NHC - 1))
                                    return ins
                                S.op("pe", f, reads=[("wd", sl)] + [("aT", hq * 4 + j, tb // 4) for j in range(4)], writes=[("ps", tb)])
                        for tb in range(NT):
                            S.op("act", lambda e, tb=tb, dc=dc: e.activation(out=o_all[:, tb, dc * 512:(dc + 1) * 512], in_=ps[tb][:], func=AF.Copy),
                                 reads=[("ps", tb)], writes=[("o", tb, dc)])
                    if do_A:
                        sA0 = contextlib.ExitStack()
                        gtA = sbt(sA0, "gtA", [128, D], F32)[:]
                        xsA = sbt(sA0, "xsA", [128, D], BF16)[:]
                        S.dma("sp", lambda e: e.dma_start(out=gtA, in_=gpre_d.partition_broadcast(128)), None, writes=["gtA"])
                    for tb in range(NT):
                        c = tb % 2
                        okeys = [("o", tb, dc) for dc in range(4)]
                        S.dma("sp", lambda e, tb=tb: e.dma_start(out=xb6, in_=x1_d[tb * 128:(tb + 1) * 128, :]), "xb", reads=[("x1d", tb)], writes=["xb6"])
                        S.op("act", lambda e, tb=tb, c=c: e.activation(out=sq6, in_=o_all[:, tb, :], func=AF.Square, accum_out=ss[:, c:c + 1]),
                             reads=okeys, writes=["sq6", ("ss", c)])
                        S.op("dve", lambda e, c=c: e.tensor_scalar(out=rs[:, c:c + 1], in0=ss[:, c:c + 1], scalar1=1.0 / D, scalar2=EPS,
                                                                   op0=ALU.mult, op1=ALU.add), reads=[("ss", c)], writes=[("rs", c)])
                        S.op("act", lambda e, c=c: e.activation(out=rs[:, c:c + 1], in_=rs[:, c:c + 1], func=AF.Sqrt), reads=[("rs", c)], writes=[("rs", c)])
                        S.op("dve", lambda e, c=c: e.reciprocal(out=rs[:, c:c + 1], in_=rs[:, c:c + 1]), reads=[("rs", c)], writes=[("rs", c)])
                        S.op("dve", lambda e, tb=tb, c=c: e.scalar_tensor_tensor(out=o_all[:, tb, :], in0=o_all[:, tb, :], scalar=rs[:, c:c + 1], in1=gt3,
                                                                                 op0=ALU.mult, op1=ALU.mult),
                             reads=okeys + [("rs", c), "gt3"], writes=okeys)
                        S.op("dve", lambda e, tb=tb: e.tensor_add(out=xb6, in0=xb6, in1=o_all[:, tb, :]), reads=okeys + ["xb6"], writes=["xb6"])
                        S.dma("sp", lambda e, tb=tb: e.dma_start(out=x_out[tb * 128:(tb + 1) * 128, :], in_=xb6), "xow", reads=["xb6"], writes=[("xout", tb)])
                        if do_A:
                            norm_transpose(xb6, "xb6", gtA, "gtA", xsA, "xsA", sq6, "sq6", hT, "hT", tb, 4 + c)
                    if do_A:
                        S.barrier()
                        sA0.close()
            S.barrier()

        if do_A:
            with contextlib.ExitStack() as sA:
                if not do_B:
                    a1 = Arena(R1, R1B)
                    gtA = a1.take([D], F32)
                    xsA = a1.take([D], BF16)
                    xbA = a1.take([D], F32)
                    sqA = a1.take([D], BF16)
                    S.dma("sp", lambda e: e.dma_start(out=gtA, in_=gpre_d.partition_broadcast(128)), None, writes=["gtA"])
                    for tb in range(NT):
                        S.dma("sp", lambda e, tb=tb: e.dma_start(out=xbA, in_=x_in[tb * 128:(tb + 1) * 128, :]), "xb", writes=["xbA"])
                        norm_transpose(xbA, "xbA", gtA, "gtA", xsA, "xsA", sqA, "sqA", hT, "hT", tb, 4 + tb % 2)
                    S.barrier()
                a1 = Arena(R1, R1B)
                qko = a1.take([16, T], BF16)
                asb = a1.take([4, T], F32)
                hgo = a1.take([4, T], F32)
                a2 = Arena(R2, R2B, start=32768)
                wi = [a2.take([16, 512], BF16) for i in range(2)]
                upo = a2.take([4, T], F32)
                ropec = a2.take([T], F32)
                ropes = a2.take([T], F32)
                pmat = sbt(sA, "pmat_sb", [128, 128], F32)
                sgA = [sbt(sA, f"sgA{i}", [128, 512], F32) for i in range(2)]
                qf = [sbt(sA, f"qf{i}", [128, 512], F32) for i in range(2)]
                t1 = [sbt(sA, f"t1{i}", [128, 512], F32) for i in range(2)]
                t2 = [sbt(sA, f"t2{i}", [128, 512], F32) for i in range(2)]
                vsb = sbt(sA, "vsb", [128, NT, 16, 65], BF16)
                S.dma("sp", lambda e: e.dma_start(out=ropec, in_=ropec_d), None, writes=["ropec"])
                S.dma("sp", lambda e: e.dma_start(out=ropes, in_=ropes_d), None, writes=["ropes"])
                S.dma("sp", lambda e: e.dma_start(out=pmat[:], in_=pmat_d), None, writes=["pmat"])
                S.op("dve", lambda e: e.memset(vsb[:], 1.0), writes=["vsb_ones"])
                hkeys = [("hT", tb, half) for tb in range(NT) for half in range(2)]
                wiv = win_d.rearrange("(kc p) n -> p kc n", p=128)
                nmm = 0
                nr = 0
                for s_ in range(9):
                    sl = s_ % 2
                    S.dma("pool", lambda e, sl=sl, s_=s_: e.dma_start(out=wi[sl], in_=wiv[:, :, s_ * 512:(s_ + 1) * 512]), f"wi{sl}", writes=[("wi", sl)])
                    if s_ in (6, 7):
                        for tb in range(NT):
                            bank = nmm % 4
                            nmm += 1

                            def f(e, sl=sl, tb=tb, bank=bank):
                                for kc in range(16):
                                    ins = e.matmul(ps[bank][:], lhsT=hT[:, kc, tb * 128:(tb + 1) * 128], rhs=wi[sl][:, kc, :], start=(kc == 0), stop=(kc == 15))
                                return ins
                            S.op("pe", f, reads=[("wi", sl)] + hkeys, writes=[("ps", bank)])
                            h0 = (s_ - 6) * 8
                            S.op("act", lambda e, tb=tb, bank=bank, h0=h0: e.activation(out=vsb[:, tb, h0:h0 + 8, 0:64],
                                                                                        in_=ps[bank][:].rearrange("p (h e) -> p h e", h=8), func=AF.Copy),
                                 reads=[("ps", bank), "vsb_ones"], writes=[("vsb", tb, s_)])
                        continue
                    for oc in range(4):
                        for th in range(2):
                            tsl = slice(th * 512, (th + 1) * 512)
                            bank = nmm % 4
                            nmm += 1

                            def f(e, sl=sl, oc=oc, th=th, bank=bank):
                                for kc in range(16):
                                    ins = e.matmul(ps[bank][:], lhsT=wi[sl][:, kc, oc * 128:(oc + 1) * 128], rhs=hT[:, kc, th * 512:(th + 1) * 512],
                                                   start=(kc == 0), stop=(kc == 15))
                                return ins
                            S.op("pe", f, reads=[("wi", sl)] + hkeys, writes=[("ps", bank)])
                            if s_ == 0:
                                S.op("act", lambda e, oc=oc, tsl=tsl, bank=bank: e.activation(out=asb[:, oc, tsl], in_=ps[bank][:], func=AF.Copy),
                                     reads=[("ps", bank)], writes=[("asb", oc, th)])
                            elif s_ == 1:
                                k2 = nmm % 2
                                S.op("act", lambda e, bank=bank, k2=k2: e.activation(out=sgA[k2][:], in_=ps[bank][:], func=AF.Sigmoid),
                                     reads=[("ps", bank)], writes=[("sgA", k2)])
                                S.op("dve", lambda e, oc=oc, tsl=tsl, k2=k2: e.tensor_mul(out=hgo[:, oc, tsl], in0=asb[:, oc, tsl], in1=sgA[k2][:]),
                                     reads=[("sgA", k2), ("asb", oc, th)], writes=[("hgo", oc, th)])
                            elif s_ == 8:
                                S.op("act", lambda e, oc=oc, tsl=tsl, bank=bank: e.activation(out=upo[:, oc, tsl], in_=ps[bank][:], func=AF.Copy),
                                     reads=[("ps", bank)], writes=[("upo", oc, th)])
                            else:
                                ch = (s_ - 2) * 4 + oc
                                k2 = nr % 2
                                pb = 4 + k2
                                nr += 1
                                S.op("act", lambda e, bank=bank, k2=k2: e.activation(out=qf[k2][:], in_=ps[bank][:], func=AF.Copy),
                                     reads=[("ps", bank)], writes=[("qf", k2)])
                                S.op("pe", lambda e, k2=k2, pb=pb: e.matmul(ps[pb][:], lhsT=pmat[:], rhs=qf[k2][:], start=True, stop=True),
                                     reads=[("qf", k2), "pmat"], writes=[("ps", pb)])
                                S.op("dve", lambda e, k2=k2, pb=pb, tsl=tsl: e.tensor_mul(out=t1[k2][:], in0=ps[pb][:], in1=ropes[:, tsl]),
                                     reads=[("ps", pb), "ropes"], writes=[("t1", k2)])
                                S.op("pool", lambda e, k2=k2, tsl=tsl: e.tensor_mul(out=t2[k2][:], in0=qf[k2][:], in1=ropec[:, tsl]),
                                     reads=[("qf", k2), "ropec"], writes=[("t2", k2)])
                                S.op("dve", lambda e, k2=k2, ch=ch, tsl=tsl: e.tensor_add(out=qko[:, ch, tsl], in0=t1[k2][:], in1=t2[k2][:]),
                                     reads=[("t1", k2), ("t2", k2)], writes=[("qko", ch, th)])
                S.dma("sp", lambda e: e.dma_start(out=hg_o.rearrange("c p t -> p c t"), in_=hgo), "outA",
                      reads=[("hgo", c, th) for c in range(4) for th in range(2)], writes=["hg_o"])
                S.dma("sp", lambda e: e.dma_start(out=up_o.rearrange("c p t -> p c t"), in_=upo), "outA",
                      reads=[("upo", c, th) for c in range(4) for th in range(2)], writes=["up_o"])
                S.dma("sp", lambda e: e.dma_start(out=qT_o.rearrange("c p t -> p c t"), in_=qko[:, 0:8, :]), "outA",
                      reads=[("qko", c, th) for c in range(8) for th in range(2)], writes=["qT_o"])
                S.dma("sp", lambda e: e.dma_start(out=kT_o.rearrange("c p t -> p c t"), in_=qko[:, 8:16, :]), "outA",
                      reads=[("qko", c, th) for c in range(8, 16) for th in range(2)], writes=["kT_o"])
                S.dma("sp", lambda e: e.dma_start(out=v_o.rearrange("t p f -> p t f"), in_=vsb[:].rearrange("p t h e -> p t (h e)")), "outA",
                      reads=[("vsb", tb, s_) for tb in range(NT) for s_ in (6, 7)], writes=["v_o"])
        S.barrier()
        S.run()
    return nc


_PROGS = {}


def _prog(do_B, do_A):
    key = (do_B, do_A)
    if key not in _PROGS:
        _PROGS[key] = build_program(do_B, do_A)
    return _PROGS[key]


def _const_tables():
    o = np.arange(-8 * 128 - 127, 8 * 128 + 128)
    w = ((np.abs(o) <= 64).astype(np.float32) + ((o % 4 == 0) & (np.abs(o) <= 256)) + ((o % 16 == 0) & (np.abs(o) <= 1024)))
    wmap = dict(zip(o.tolist(), w.tolist()))
    k = np.arange(128)[:, None, None]
    rel = np.arange(-8, 9)[None, :, None]
    q = np.arange(128)[None, None, :]
    off = rel * 128 + k - q
    masks = np.vectorize(wmap.get)(off).astype(np.float32).reshape(128, 17 * 128).astype(ml_dtypes.bfloat16)
    pm = np.zeros((128, 128), np.float32)
    for hh in range(2):
        for m in range(8):
            pm[hh * 64 + m + 8, hh * 64 + m] = -1.0
            pm[hh * 64 + m, hh * 64 + m + 8] = 1.0
    return masks, pm


def _rope_tables(c):
    pos = (np.arange(T) + c * T).astype(np.float32)
    inv = (np.float32(500000.0) ** (-np.arange(0, 16, 2, dtype=np.float32) / np.float32(16))).astype(np.float32)
    ang = (pos[:, None] * inv[None, :]).astype(np.float32)
    cs, sn = np.cos(ang).astype(np.float32), np.sin(ang).astype(np.float32)
    C = np.ones((128, T), np.float32)
    Sn = np.zeros((128, T), np.float32)
    for hh in range(2):
        for i in range(16):
            C[hh * 64 + i] = cs[:, i % 8]
            Sn[hh * 64 + i] = sn[:, i % 8]
    return C, Sn


def _corr_table(c):
    out = np.ones((4, 16), np.float32)
    idx = np.concatenate([np.arange(8), np.arange(T - 8, T)]) + c * T
    for gi, w in enumerate((2, 4, 8, 16)):
        lo = np.clip(idx - w // 2, 0, S_LEN)
        hi = np.clip(idx + w - w // 2, 0, S_LEN)
        out[gi] = w / (hi - lo).astype(np.float32)
    return np.ascontiguousarray(np.broadcast_to(out[None], (128, 4, 16)))


def _colmajor(v, n):
    return np.ascontiguousarray(v.reshape(n, 128).T)


def _run_step(step, xs, Aout, P, consts):
    f32 = np.float32
    masks, pm, ident, ropes, corrs = consts
    do_B = step > 0
    do_A = step < DEPTH
    lb, la = step - 1, step
    nc = _prog(do_B, do_A)
    maps = []
    if do_B:
        kT = np.concatenate([Aout[c]["kT_o"] for c in range(NCORES)], axis=2)
        kT = np.pad(kT, ((0, 0), (0, 0), (T, T)))
        vv = np.concatenate([Aout[c]["v_o"].reshape(T, 8, 130) for c in range(NCORES)], axis=0)
        vv = np.pad(vv, ((T, T), (0, 0), (0, 0)))
        hgf = np.pad(np.concatenate([Aout[c]["hg_o"] for c in range(NCORES)], axis=2), ((0, 0), (0, 0), (15, 15)))
        upf = np.pad(np.concatenate([Aout[c]["up_o"] for c in range(NCORES)], axis=2), ((0, 0), (0, 0), (8, 8)))
        cw = np.ascontiguousarray(np.asarray(P["conv_w"][lb], f32).T.reshape(4, 128, 31).transpose(1, 0, 2))
        vec4 = np.stack([_colmajor(np.asarray(P[k][lb], f32), 4) for k in ("conv_b", "conv_ln_g", "conv_ln_b", "pool_scale")], axis=1)
        gm = np.asarray(P["g_mix"][lb], f32)
    for c in range(NCORES):
        m = {"ident": ident, "x_in": xs[c]}
        if do_B:
            vh = vv[c * T:c * T + 3 * T].reshape(24, 128, 8, 130).transpose(2, 1, 0, 3)
            m.update({
                "qT": Aout[c]["qT_o"],
                "kTh": np.ascontiguousarray(kT[:, :, c * T:c * T + 3 * T]),
                "vh": np.ascontiguousarray(vh),
                "hgh": np.ascontiguousarray(hgf[:, :, c * T:c * T + T + 30]),
                "uh": np.ascontiguousarray(upf[:, :, c * T:c * T + T + 16]),
                "masks": masks, "corr": corrs[c], "cw": cw, "vec4": np.ascontiguousarray(vec4),
                "gmixT": _colmajor(gm, 16), "gmixB": np.ascontiguousarray(gm[None, 512:1536]),
                "poolw": np.asarray(P["pool_w"][lb], f32), "w_out": np.asarray(P["w_out"][lb], f32),
                "g_post_mix": np.asarray(P["g_post_mix"][lb], f32)[None], "g_pre_ffn": np.asarray(P["g_pre_ffn"][lb], f32)[None],
                "w_gate": np.asarray(P["w_gate"][lb], f32), "w_up": np.asarray(P["w_up"][lb], f32),
                "w_down": np.asarray(P["w_down"][lb], f32), "g_post_ffn": np.asarray(P["g_post_ffn"][lb], f32)[None],
            })
        if do_A:
            m.update({"w_in": np.asarray(P["w_in"][la], f32), "g_pre_mix": np.asarray(P["g_pre_mix"][la], f32)[None],
                      "rope_c": ropes[c][0], "rope_s": ropes[c][1], "pmat": pm})
        maps.append(m)
    res = run_bass_kernel_spmd(nc, maps, core_ids=list(range(NCORES)))
    Aout = res.results
    if do_B:
        xs = [np.asarray(Aout[c]["x_out"], f32) for c in range(NCORES)]
    return xs, Aout


def _consts():
    masks, pm = _const_tables()
    ident = np.eye(128, dtype=np.float32)
    ropes = [_rope_tables(c) for c in range(NCORES)]
    corrs = [_corr_table(c) for c in range(NCORES)]
    return masks, pm, ident, ropes, corrs


def kernel(x, w_in, conv_w, conv_b, conv_ln_g, conv_ln_b, pool_w, pool_scale, g_mix,
           w_out, g_pre_mix, g_post_mix, g_pre_ffn, g_post_ffn, w_gate, w_up, w_down):
    P = dict(w_in=w_in, conv_w=conv_w, conv_b=conv_b, conv_ln_g=conv_ln_g, conv_ln_b=conv_ln_b, pool_w=pool_w,
             pool_scale=pool_scale, g_mix=g_mix, w_out=w_out, g_pre_mix=g_pre_mix, g_post_mix=g_post_mix,
             g_pre_ffn=g_pre_ffn, g_post_ffn=g_post_ffn, w_gate=w_gate, w_up=w_up, w_down=w_down)
    x = np.asarray(x, np.float32)
    consts = _consts()
    xs = [np.ascontiguousarray(x[0, c * T:(c + 1) * T]) for c in range(NCORES)]
    Aout = None
    for step in range(DEPTH + 1):
        xs, Aout = _run_step(step, xs, Aout, P, consts)
    return np.concatenate(xs, axis=0)[None].astype(np.float32)
```

```python
import contextlib
import numpy as np
import ml_dtypes
import concourse.bass as bass
import concourse.mybir as mybir
from concourse.bass_utils import run_bass_kernel_spmd

F32 = mybir.dt.float32
BF16 = mybir.dt.bfloat16
AF = mybir.ActivationFunctionType
ALU = mybir.AluOpType

NCORES = 8
S_LEN = 8192
T = 1024
NT = 8
D = 2048
DEPTH = 4
INW = 4608
FF = 5632
NHC = FF // 128
EPS = 1e-6
ENGS = ("pe", "act", "dve", "pool", "sp")
import os
DEBUG = bool(os.environ.get("KDEBUG"))


class Sched:
    def __init__(self, nc):
        self.nc = nc
        self.ops = {e: [] for e in ENGS}
        self.cnt = {}
        self.waited = {e: {} for e in ENGS}
        self.last_w = {}
        self.readers = {}
        self.semnames = set(ENGS)
        self.sem_h = {}

    def _deps(self, eng, reads, writes):
        need = {}

        def add(k, v):
            if k == "pe" and eng == "pe":
                return
            if need.get(k, 0) < v:
                need[k] = v

        for k in reads:
            t = self.last_w.get(k)
            if t is not None:
                add(*t)
        for k in writes:
            t = self.last_w.get(k)
            if t is not None:
                add(*t)
            for kk, vv in self.readers.get(k, {}).items():
                add(kk, vv)
        return self._filter(eng, need)

    def _filter(self, eng, need):
        out = []
        w = self.waited[eng]
        for k, v in need.items():
            if w.get(k, 0) < v:
                w[k] = v
                out.append((k, v))
        return out

    def _track(self, tok, reads, writes):
        for k in writes:
            self.last_w[k] = tok
            self.readers[k] = {}
        for k in reads:
            r = self.readers.setdefault(k, {})
            if r.get(tok[0], 0) < tok[1]:
                r[tok[0]] = tok[1]

    def op(self, eng, fn, reads=(), writes=()):
        waits = self._deps(eng, reads, writes)
        self.cnt[eng] = self.cnt.get(eng, 0) + 1
        tok = (eng, self.cnt[eng])
        self._track(tok, reads, writes)
        self.ops[eng].append((waits, fn, eng, 1))
        return tok

    def dma(self, eng, fn, sem, reads=(), writes=()):
        if sem is None:
            self.nuniq = getattr(self, "nuniq", 0) + 1
            sem = f"u{self.nuniq}"
        self.semnames.add(sem)
        waits = self._deps(eng, reads, writes)
        self.cnt[sem] = self.cnt.get(sem, 0) + 16
        tok = (sem, self.cnt[sem])
        self._track(tok, reads, writes)
        self.ops[eng].append((waits, fn, sem, 16))
        return tok

    def barrier(self):
        for e in ENGS:
            waits = self._filter(e, dict(self.cnt))
            if waits:
                self.ops[e].append((waits, None, None, 0))

    def run(self):
        nc = self.nc
        with contextlib.ExitStack() as st:
            for name in sorted(self.semnames):
                self.sem_h[name] = st.enter_context(nc.semaphore("s_" + name))
            block = st.enter_context(nc.Block())
            sem_h = self.sem_h

            def replay(e, lst):
                for waits, fn, sem, inc in lst:
                    for k, v in waits:
                        e.wait_ge(sem_h[k], v)
                    if fn is not None:
                        fn(e).then_inc(sem_h[sem], inc)

            @block.tensor
            def _(e):
                replay(e, self.ops["pe"])

            @block.scalar
            def _(e):
                replay(e, self.ops["act"])

            @block.vector
            def _(e):
                replay(e, self.ops["dve"])

            @block.gpsimd
            def _(e):
                replay(e, self.ops["pool"])

            @block.sync
            def _(e):
                replay(e, self.ops["sp"])


class Arena:
    def __init__(self, t, nbytes, start=0):
        self.t, self.n, self.off = t, nbytes, start

    def take(self, shape, dt):
        esz = 2 if dt == BF16 else 4
        n = esz
        for d in shape:
            n *= d
        n = (n + 31) // 32 * 32
        assert self.off + n <= self.n, (self.off, n, self.n)
        ap = self.t[:, self.off // 4:(self.off + n) // 4]
        if dt != F32:
            ap = ap.bitcast(dt)
        tot = 1
        for d in shape:
            tot *= d
        ap = ap[:, 0:tot]
        if len(shape) == 2:
            ap = ap.rearrange("p (a b) -> p a b", a=shape[0])
        elif len(shape) == 3:
            ap = ap.rearrange("p (a b c) -> p a b c", a=shape[0], b=shape[1])
        self.off += n
        return ap


def build_program(do_B, do_A):
    nc = bass.Bass("TRN2", target_bir_lowering=False)
    S = Sched(nc)

    def din(name, shape, dt=F32):
        return nc.dram_tensor(name, list(shape), dt, kind="ExternalInput").ap()

    def dout(name, shape, dt=F32):
        return nc.dram_tensor(name, list(shape), dt, kind="ExternalOutput").ap()

    ident_d = din("ident", [128, 128])
    x_in = din("x_in", [T, D])
    if do_B:
        qT_d = din("qT", [8, 128, T], BF16)
        kT_d = din("kTh", [8, 128, 3 * T], BF16)
        v_d = din("vh", [8, 128, 24, 130], BF16)
        hg_d = din("hgh", [4, 128, T + 30])
        up_d = din("uh", [4, 128, T + 16])
        mask_d = din("masks", [128, 17 * 128], BF16)
        corr_d = din("corr", [128, 4, 16])
        cw_d = din("cw", [128, 4, 31])
        vec4_d = din("vec4", [128, 4, 4])
        gmixT_d = din("gmixT", [128, 16])
        gmixB_d = din("gmixB", [1, 1024])
        poolw_d = din("poolw", [4, 128, 128])
        wout_d = din("w_out", [D, D])
        gpm_d = din("g_post_mix", [1, D])
        gpf_d = din("g_pre_ffn", [1, D])
        wg_d = din("w_gate", [D, FF])
        wu_d = din("w_up", [D, FF])
        wd_d = din("w_down", [FF, D])
        gpo_d = din("g_post_ffn", [1, D])
        x_out = dout("x_out", [T, D])
        if DEBUG:
            x1_d = dout("x1_dbg", [T, D])
            yT_dbg = dout("yT_dbg", [16, 128, T], BF16)
        else:
            x1_d = nc.dram_tensor("x1_scratch", [T, D], F32).ap()
    if do_A:
        win_d = din("w_in", [D, INW])
        gpre_d = din("g_pre_mix", [1, D])
        ropec_d = din("rope_c", [128, T])
        ropes_d = din("rope_s", [128, T])
        pmat_d = din("pmat", [128, 128])
        qT_o = dout("qT_o", [8, 128, T], BF16)
        kT_o = dout("kT_o", [8, 128, T], BF16)
        v_o = dout("v_o", [NT, 128, 16 * 65], BF16)
        hg_o = dout("hg_o", [4, 128, T])
        up_o = dout("up_o", [4, 128, T])

    with contextlib.ExitStack() as top:
        def sbt(st, name, shape, dt):
            return st.enter_context(nc.sbuf_tensor("sb_" + name, list(shape), dt))

        ps = [top.enter_context(nc.psum_tensor(f"ps{i}", [128, 512], F32)) for i in range(8)]
        ident = sbt(top, "ident_sb", [128, 128], BF16)
        onesf = sbt(top, "onesf", [128, 128], F32)
        R1 = sbt(top, "R1", [128, 16384], F32)
        R2 = sbt(top, "R2", [128, 22528], F32)
        R1B, R2B = 65536, 90112
        h2T = Arena(R1, R1B).take([16, T], BF16)
        o_all = Arena(R1, R1B).take([NT, D], F32)
        aT = Arena(R2, R2B).take([NHC, T], BF16)
        hT = Arena(R2, R2B).take([16, T], BF16)
        ss = sbt(top, "ss", [128, 16], F32)
        rs = sbt(top, "rs", [128, 16], F32)
        ptr = ps[7][:].bitcast(BF16).rearrange("p (j t) -> p j t", j=8)

        S.dma("pool", lambda e: e.dma_start(out=ident[:], in_=ident_d), None, writes=["ident"])
        S.op("dve", lambda e: e.memset(onesf[:], 1.0 / 512.0), writes=["onesf"])

        def norm_transpose_ops(xblk, xkey, gt, gkey, xs, xskey, sq, sqkey, dstT, dkey, tb, slot):
            c = slot
            ops = []
            ops.append(lambda: S.op("act", lambda e: e.activation(out=sq, in_=xblk, func=AF.Square, accum_out=ss[:, c:c + 1]),
                                    reads=[xkey], writes=[sqkey, ("ss", c)]))
            ops.append(lambda: S.op("dve", lambda e: e.tensor_scalar(out=rs[:, c:c + 1], in0=ss[:, c:c + 1], scalar1=1.0 / D, scalar2=EPS,
                                                                     op0=ALU.mult, op1=ALU.add), reads=[("ss", c)], writes=[("rs", c)]))
            ops.append(lambda: S.op("act", lambda e: e.activation(out=rs[:, c:c + 1], in_=rs[:, c:c + 1], func=AF.Sqrt),
                                    reads=[("rs", c)], writes=[("rs", c)]))
            ops.append(lambda: S.op("dve", lambda e: e.reciprocal(out=rs[:, c:c + 1], in_=rs[:, c:c + 1]), reads=[("rs", c)], writes=[("rs", c)]))
            ops.append(lambda: S.op("dve", lambda e: e.scalar_tensor_tensor(out=xs, in0=xblk, scalar=rs[:, c:c + 1], in1=gt,
                                                                            op0=ALU.mult, op1=ALU.mult),
                                    reads=[xkey, gkey, ("rs", c)], writes=[xskey]))
            for half in range(2):
                def tr(e, half=half):
                    for j in range(8):
                        kc = half * 8 + j
                        i = e.transpose(out=ptr[:, j, :], in_=xs[:, kc * 128:(kc + 1) * 128], identity=ident[:])
                    return i
                ops.append(lambda tr=tr: S.op("pe", tr, reads=[xskey, "ident"], writes=[("ps", 7)]))
                ops.append(lambda half=half: S.op("act", lambda e: e.activation(out=dstT[:, half * 8:(half + 1) * 8, tb * 128:(tb + 1) * 128],
                                                                               in_=ptr[:], func=AF.Copy),
                                                  reads=[("ps", 7)], writes=[(dkey, tb, half)]))
            return ops

        def norm_transpose(*a):
            for o in norm_transpose_ops(*a):
                o()

        def pipeline(chains, stagger):
            n = max(len(c) for c in chains) + stagger * (len(chains) - 1)
            for step in range(n):
                for i, ch in enumerate(chains):
                    k = step - i * stagger
                    if 0 <= k < len(ch):
                        ch[k]()

        if do_B:
            with contextlib.ExitStack() as sB:
                gmixT = sbt(sB, "gmixT", [128, 16], F32)
                vec4 = sbt(sB, "vec4", [128, 4, 4], F32)
                sY = contextlib.ExitStack()
                yT = sbt(sY, "yT", [128, 16, T], BF16)
                S.dma("sp", lambda e: e.dma_start(out=gmixT[:], in_=gmixT_d), None, writes=["gmixT"])
                S.dma("sp", lambda e: e.dma_start(out=vec4[:], in_=vec4_d), None, writes=["vec4"])

                with contextlib.ExitStack() as s1:
                    a2 = Arena(R2, R2B)
                    yb = a2.take([NT, 1024], F32)
                    hg = a2.take([4, T + 30], F32)
                    acc = a2.take([4, T], F32)
                    uh = a2.take([4, T + 16], F32)
                    masks = a2.take([17 * 128], BF16)
                    ybs = a2.take([1024], BF16)
                    a1 = Arena(R1, R1B)
                    pw = a1.take([3, T + 16], F32)
                    pooled = a1.take([4, T], BF16)
                    sqb = a1.take([1024], BF16)
                    gmixB = a1.take([1024], F32)
                    tmpA = a1.take([512], F32)
                    tmpB = a1.take([512], F32)
                    meansb = a1.take([512], F32)
                    rstdsb = a1.take([512], F32)
                    qkv_v = [a1.take([24, 130], BF16) for i in range(2)]
                    qkv_q = [a1.take([T], BF16) for i in range(2)]
                    qkv_k = [a1.take([3 * T], BF16) for i in range(2)]
                    ya = acc
                    yc = hg
                    cw = sbt(s1, "cw", [128, 4, 31], F32)
                    corr = sbt(s1, "corr", [128, 4, 16], F32)
                    poolw = sbt(s1, "poolw", [128, 4, 128], BF16)
                    pT = [sbt(s1, f"pT{i}", [128, 512], BF16) for i in range(4)]
                    rec = sbt(s1, "rec", [128, 2, 2], F32)
                    pwin = sbt(s1, "pwin", [128, 4, T], F32)

                    S.dma("sp", lambda e: e.dma_start(out=hg, in_=hg_d.rearrange("c p t -> p c t")), None, writes=["hg"])
                    S.dma("sp", lambda e: e.dma_start(out=uh, in_=up_d.rearrange("c p t -> p c t")), None, writes=["uh"])
                    S.dma("sp", lambda e: e.dma_start(out=cw[:], in_=cw_d), None, writes=["cw"])
                    S.dma("sp", lambda e: e.dma_start(out=corr[:], in_=corr_d), None, writes=["corr"])
                    S.dma("sp", lambda e: e.dma_start(out=masks, in_=mask_d), None, writes=["masks"])
                    S.dma("sp", lambda e: e.dma_start(out=gmixB, in_=gmixB_d.partition_broadcast(128)), None, writes=["gmixB"])
                    S.dma("pool", lambda e: e.dma_start(out=poolw[:], in_=poolw_d.rearrange("g c d -> c g d")), None, writes=["poolw"])

                    def load_qkv(hp):
                        sl = hp % 2
                        S.dma("sp", lambda e: e.dma_start(out=qkv_q[sl], in_=qT_d[hp]), f"q{sl}", writes=[("q", sl)])
                        S.dma("sp", lambda e: e.dma_start(out=qkv_k[sl], in_=kT_d[hp]), f"k{sl}", writes=[("k", sl)])
                        S.dma("sp", lambda e: e.dma_start(out=qkv_v[sl], in_=v_d[hp]), f"v{sl}", writes=[("v", sl)])

                    load_qkv(0)

                    bg_ops = []
                    for c in range(4):
                        bg_ops.append(lambda c=c: S.op("dve", lambda e: e.tensor_scalar(out=acc[:, c, :], in0=hg[:, c, 0:T], scalar1=cw[:, c, 0:1],
                                                                                      scalar2=vec4[:, 0, c:c + 1], op0=ALU.mult, op1=ALU.add),
                                                       reads=["hg", "cw", "vec4"], writes=[("acc", c)]))
                        for j in range(1, 31):
                            bg_ops.append(lambda c=c, j=j: S.op("dve", lambda e: e.scalar_tensor_tensor(
                                out=acc[:, c, :], in0=hg[:, c, j:j + T], scalar=cw[:, c, j:j + 1], in1=acc[:, c, :], op0=ALU.mult, op1=ALU.add),
                                reads=["hg", "cw", ("acc", c)], writes=[("acc", c)]))
                    for gi in range(4):
                        w = 2 << gi
                        src = uh[:, gi, :]
                        L = T + 16
                        step = 1
                        lvl = 0
                        while step < w:
                            L2 = L - step
                            dst = pw[:, lvl % 3, :]
                            S.op("pool", lambda e, src=src, dst=dst, L2=L2, step=step: e.tensor_add(
                                out=dst[:, 0:L2], in0=src[:, 0:L2], in1=src[:, step:step + L2]),
                                reads=["uh", ("pw", (lvl + 2) % 3)], writes=[("pw", lvl % 3)])
                            src = dst
                            L = L2
                            step *= 2
                            lvl += 1
                        off = 8 - w // 2
                        win = src[:, off:off + T]
                        lastkey = ("pw", (lvl - 1) % 3)
                        S.op("pool", lambda e, win=win, gi=gi: e.tensor_mul(out=win[:, 0:8], in0=win[:, 0:8], in1=corr[:, gi, 0:8]),
                             reads=[lastkey, "corr"], writes=[lastkey])
                        S.op("pool", lambda e, win=win, gi=gi: e.tensor_mul(out=win[:, T - 8:T], in0=win[:, T - 8:T], in1=corr[:, gi, 8:16]),
                             reads=[lastkey], writes=[lastkey])
                        pwk = ("pwin", gi)
                        S.op("pool", lambda e, win=win, gi=gi: e.tensor_copy(out=pwin[:, gi, :], in_=win), reads=[lastkey], writes=[pwk])
                        bg_ops.append(lambda gi=gi, w=w, pwk=pwk: S.op("dve", lambda e: e.scalar_tensor_tensor(
                            out=pooled[:, gi, :], in0=pwin[:, gi, :], scalar=1.0 / w, in1=uh[:, gi, 8:8 + T], op0=ALU.mult, op1=ALU.subtract),
                            reads=[pwk, "uh"], writes=[("pooled", gi)]))

                    units = []
                    for hp in range(8):
                        for qb in range(NT):
                            for h in range(2):
                                for g, (r0, n) in enumerate(((-8, 4), (-4, 4), (0, 4), (4, 4), (8, 1))):
                                    units.append((hp, qb, h, g, r0, n))
                    NU = len(units)

                    def emit_qk(i):
                        hp, qb, h, g, r0, n = units[i]
                        sl = hp % 2
                        bank = i % 3
                        kb0 = qb + 8 + r0

                        def f(e):
                            for j in range(n):
                                ins = e.matmul(ps[bank][:, j * 128:(j + 1) * 128],
                                               lhsT=qkv_k[sl][h * 64:(h + 1) * 64, (kb0 + j) * 128:(kb0 + j + 1) * 128],
                                               rhs=qkv_q[sl][h * 64:(h + 1) * 64, qb * 128:(qb + 1) * 128], start=True, stop=True)
                            return ins
                        S.op("pe", f, reads=[("q", sl), ("k", sl)], writes=[("ps", bank)])
                        slot = i % 4
                        S.op("act", lambda e: e.activation(out=pT[slot][:, 0:n * 128], in_=ps[bank][:, 0:n * 128], func=AF.Exp, scale=0.125),
                             reads=[("ps", bank)], writes=[("pT", slot)])
                        S.op("dve", lambda e: e.tensor_mul(out=pT[slot][:, 0:n * 128], in0=pT[slot][:, 0:n * 128],
                                                           in1=masks[:, (r0 + 8) * 128:(r0 + 8 + n) * 128]),
                             reads=[("pT", slot), "masks"], writes=[("pT", slot)])

                    def emit_pv(i):
                        hp, qb, h, g, r0, n = units[i]
                        sl = hp % 2
                        slot = i % 4
                        ob = 3 + ((hp * NT + qb) % 2)
                        kb0 = qb + 8 + r0

                        def f(e):
                            for j in range(n):
                                ins = e.matmul(ps[ob][:, h * 128:h * 128 + 65], lhsT=pT[slot][:, j * 128:(j + 1) * 128],
                                               rhs=qkv_v[sl][:, kb0 + j, h * 65:(h + 1) * 65],
                                               start=(g == 0 and j == 0), stop=(g == 4))
                            return ins
                        S.op("pe", f, reads=[("pT", slot), ("v", sl)], writes=[("ps", ob)])
                        if h == 1 and g == 4:
                            S.op("dve", lambda e: e.reciprocal(out=rec[:, (hp * NT + qb) % 2, :], in_=ps[ob][:, 64:256:128]),
                                 reads=[("ps", ob)], writes=[("rec", ob)])
                            for hh in range(2):
                                S.op("dve", lambda e, hh=hh: e.tensor_scalar(
                                    out=yb[:, qb, hp * 128 + hh * 64:hp * 128 + hh * 64 + 64], in0=ps[ob][:, hh * 128:hh * 128 + 64],
                                    scalar1=rec[:, (hp * NT + qb) % 2, hh:hh + 1], scalar2=None, op0=ALU.mult),
                                    reads=[("ps", ob), ("rec", ob)], writes=[("yb", qb, hp, hh)])

                    LAG = 2
                    bg_ops.reverse()
                    for i in range(NU + LAG):
                        if i % 4 == 3 and bg_ops:
                            bg_ops.pop()()
                        if i < NU:
                            emit_qk(i)
                        if i - LAG >= 0:
                            emit_pv(i - LAG)
                        if i < NU:
                            hp_, qb_, h_, g_ = units[i][:4]
                            if qb_ == 0 and h_ == 0 and g_ == LAG and hp_ + 1 < 8:
                                load_qkv(hp_ + 1)

                    while bg_ops:
                        bg_ops.pop()()
                    ybkeys = [[("yb", qb, hp, hh) for hp in range(8) for hh in range(2)] for qb in range(NT)]
                    for qb in range(NT):
                        c = 8 + (qb % 2)
                        S.op("act", lambda e, qb=qb, c=c: e.activation(out=sqb, in_=yb[:, qb, :], func=AF.Square, accum_out=ss[:, c:c + 1]),
                             reads=ybkeys[qb], writes=["sqb", ("ss", c)])
                        S.op("dve", lambda e, c=c: e.tensor_scalar(out=rs[:, c:c + 1], in0=ss[:, c:c + 1], scalar1=1.0 / 1024.0, scalar2=EPS,
                                                                   op0=ALU.mult, op1=ALU.add), reads=[("ss", c)], writes=[("rs", c)])
                        S.op("act", lambda e, c=c: e.activation(out=rs[:, c:c + 1], in_=rs[:, c:c + 1], func=AF.Sqrt),
                             reads=[("rs", c)], writes=[("rs", c)])
                        S.op("dve", lambda e, c=c: e.reciprocal(out=rs[:, c:c + 1], in_=rs[:, c:c + 1]), reads=[("rs", c)], writes=[("rs", c)])
                        S.op("dve", lambda e, qb=qb, c=c: e.scalar_tensor_tensor(out=ybs, in0=yb[:, qb, :], scalar=rs[:, c:c + 1], in1=gmixB,
                                                                                 op0=ALU.mult, op1=ALU.mult),
                             reads=ybkeys[qb] + [("rs", c), "gmixB"], writes=["ybs"])

                        def tr(e):
                            for j in range(8):
                                i_ = e.transpose(out=ptr[:, j, :], in_=ybs[:, j * 128:(j + 1) * 128], identity=ident[:])
                            return i_
                        S.op("pe", tr, reads=["ybs", "ident"], writes=[("ps", 7)])
                        S.op("act", lambda e, qb=qb: e.activation(out=yT[:, 4:12, qb * 128:(qb + 1) * 128], in_=ptr[:], func=AF.Copy),
                             reads=[("ps", 7)], writes=[("yT", "b", qb)])

                    def rms_feat(src, srckeys, base, tagk):
                        for th in range(2):
                            tsl = slice(th * 512, (th + 1) * 512)
                            for c in range(4):
                                S.op("act", lambda e, c=c, tsl=tsl: e.activation(out=tmpA, in_=src[:, c, tsl], func=AF.Square),
                                     reads=[srckeys(c, th)], writes=["tmpA"])
                                S.op("pe", lambda e, c=c: e.matmul(ps[0][:], lhsT=onesf[:], rhs=tmpA, start=(c == 0), stop=(c == 3)),
                                     reads=["tmpA", "onesf"], writes=[("ps", 0)])
                            S.op("dve", lambda e: e.tensor_scalar(out=rstdsb, in0=ps[0][:], scalar1=EPS, scalar2=None, op0=ALU.add),
                                 reads=[("ps", 0)], writes=["rstdsb"])
                            S.op("act", lambda e: e.activation(out=rstdsb, in_=rstdsb, func=AF.Sqrt), reads=["rstdsb"], writes=["rstdsb"])
                            S.op("dve", lambda e: e.reciprocal(out=rstdsb, in_=rstdsb), reads=["rstdsb"], writes=["rstdsb"])
                            for c in range(4):
                                S.op("dve", lambda e, c=c, tsl=tsl: e.scalar_tensor_tensor(out=yT[:, base + c, tsl], in0=src[:, c, tsl],
                                                                                  scalar=gmixT[:, base + c:base + c + 1], in1=rstdsb,
                                                                                  op0=ALU.mult, op1=ALU.mult),
                                     reads=[srckeys(c, th), "rstdsb", "gmixT"], writes=[("yT", tagk, c, th)])

                    for th in range(2):
                        tsl = slice(th * 512, (th + 1) * 512)
                        for c in range(4):
                            S.op("pe", lambda e, c=c, tsl=tsl: e.matmul(ps[1][:], lhsT=onesf[:], rhs=acc[:, c, tsl], start=(c == 0), stop=(c == 3)),
                                 reads=[("acc", c), "onesf"], writes=[("ps", 1)])
                        for c in range(4):
                            S.op("act", lambda e, c=c, tsl=tsl: e.activation(out=tmpA, in_=acc[:, c, tsl], func=AF.Square),
                                 reads=[("acc", c)], writes=["tmpA"])
                            S.op("pe", lambda e, c=c: e.matmul(ps[2][:], lhsT=onesf[:], rhs=tmpA, start=(c == 0), stop=(c == 3)),
                                 reads=["tmpA", "onesf"], writes=[("ps", 2)])
                        S.op("act", lambda e: e.activation(out=meansb, in_=ps[1][:], func=AF.Copy), reads=[("ps", 1)], writes=["meansb"])
                        S.op("dve", lambda e: e.tensor_mul(out=tmpB, in0=meansb, in1=meansb), reads=["meansb"], writes=["tmpB"])
                        S.op("dve", lambda e: e.tensor_sub(out=rstdsb, in0=ps[2][:], in1=tmpB), reads=[("ps", 2), "tmpB"], writes=["rstdsb"])
                        S.op("dve", lambda e: e.tensor_scalar(out=rstdsb, in0=rstdsb, scalar1=EPS, scalar2=None, op0=ALU.add),
                             reads=["rstdsb"], writes=["rstdsb"])
                        S.op("act", lambda e: e.activation(out=rstdsb, in_=rstdsb, func=AF.Sqrt), reads=["rstdsb"], writes=["rstdsb"])
                        S.op("dve", lambda e: e.reciprocal(out=rstdsb, in_=rstdsb), reads=["rstdsb"], writes=["rstdsb"])
                        for c in range(4):
                            S.op("dve", lambda e, c=c, tsl=tsl: e.tensor_sub(out=tmpB, in0=acc[:, c, tsl], in1=meansb),
                                 reads=[("acc", c), "meansb"], writes=["tmpB"])
                            S.op("dve", lambda e: e.tensor_mul(out=tmpB, in0=tmpB, in1=rstdsb), reads=["tmpB", "rstdsb"], writes=["tmpB"])
                            S.op("act", lambda e, c=c, tsl=tsl: e.activation(out=ya[:, c, tsl], in_=tmpB, func=AF.Silu,
                                                                    scale=vec4[:, 1, c:c + 1], bias=vec4[:, 2, c:c + 1]),
                                 reads=["tmpB", "vec4"], writes=[("ya", c, th)])
                    rms_feat(ya, lambda c, th: ("ya", c, th), 0, "a")

                    for gi in range(4):
                        for th in range(2):
                            tsl = slice(th * 512, (th + 1) * 512)
                            S.op("pe", lambda e, gi=gi, tsl=tsl: e.matmul(ps[1][:], lhsT=poolw[:, gi, :], rhs=pooled[:, gi, tsl], start=True, stop=True),
                                 reads=[("pooled", gi), "poolw"], writes=[("ps", 1)])
                            S.op("act", lambda e, gi=gi, tsl=tsl: e.activation(out=yc[:, gi, tsl], in_=ps[1][:], func=AF.Copy, scale=vec4[:, 3, gi:gi + 1]),
                                 reads=[("ps", 1), "vec4"], writes=[("yc", gi, th)])
                    rms_feat(yc, lambda c, th: ("yc", c, th), 12, "c")
                    if DEBUG:
                        S.barrier()
                        S.dma("sp", lambda e: e.dma_start(out=yT_dbg.rearrange("c p t -> p c t"), in_=yT[:]), "dbg", writes=["dbg"])
                S.barrier()

                with contextlib.ExitStack() as s4:
                    a2 = Arena(R2, R2B)
                    omix = a2.take([NT, D], F32)
                    gt1 = a2.take([D], F32)
                    gt2 = a2.take([D], F32)
                    xbs = [a2.take([D], F32), sbt(s4, "xb_b", [128, D], F32)[:]]
                    a1u = Arena(R1, R1B, start=32768)
                    wo = [a1u.take([16, 512], BF16) for i in range(2)]
                    sq4 = sbt(s4, "sq4", [128, D], BF16)[:]
                    xs4s = [sbt(s4, f"xs4{i}", [128, D], BF16)[:] for i in range(2)]
                    S.dma("sp", lambda e: e.dma_start(out=gt1, in_=gpm_d.partition_broadcast(128)), None, writes=["gt1"])
                    S.dma("sp", lambda e: e.dma_start(out=gt2, in_=gpf_d.partition_broadcast(128)), None, writes=["gt2"])
                    wview = wout_d.rearrange("(kc p) n -> p kc n", p=128)
                    for dc in range(4):
                        sl = dc % 2
                        S.dma("pool", lambda e, sl=sl, dc=dc: e.dma_start(out=wo[sl], in_=wview[:, :, dc * 512:(dc + 1) * 512]),
                              f"wo{sl}", writes=[("wo", sl)])
                        for tb in range(NT):
                            bank = tb % 4

                            def f(e, tb=tb, sl=sl, bank=bank):
                                for kc in range(16):
                                    ins = e.matmul(ps[bank][:], lhsT=yT[:, kc, tb * 128:(tb + 1) * 128], rhs=wo[sl][:, kc, :],
                                                   start=(kc == 0), stop=(kc == 15))
                                return ins
                            S.op("pe", f, reads=[("wo", sl)] + [("yT", "b", tb)] + [("yT", k, c, tb // 4) for k in ("a", "c") for c in range(4)],
                                 writes=[("ps", bank)])
                            S.op("act", lambda e, tb=tb, dc=dc, bank=bank: e.activation(out=omix[:, tb, dc * 512:(dc + 1) * 512], in_=ps[bank][:], func=AF.Copy),
                                 reads=[("ps", bank)], writes=[("omix", tb, dc)])
                    chains = []
                    for tb in range(NT):
                        b = tb % 2
                        c = tb % 2
                        xb, xbk = xbs[b], ("xb", b)
                        okeys = [("omix", tb, dc) for dc in range(4)]
                        ch = []
                        ch.append(lambda tb=tb, xb=xb, xbk=xbk, b=b: S.dma("sp", lambda e: e.dma_start(out=xb, in_=x_in[tb * 128:(tb + 1) * 128, :]), f"xb{b}", writes=[xbk]))
                        ch.append(lambda tb=tb, c=c, okeys=okeys: S.op("act", lambda e: e.activation(out=sq4, in_=omix[:, tb, :], func=AF.Square, accum_out=ss[:, c:c + 1]),
                                                                       reads=okeys, writes=["sq4", ("ss", c)]))
                        ch.append(lambda c=c: S.op("dve", lambda e: e.tensor_scalar(out=rs[:, c:c + 1], in0=ss[:, c:c + 1], scalar1=1.0 / D, scalar2=EPS,
                                                                                    op0=ALU.mult, op1=ALU.add), reads=[("ss", c)], writes=[("rs", c)]))
                        ch.append(lambda c=c: S.op("act", lambda e: e.activation(out=rs[:, c:c + 1], in_=rs[:, c:c + 1], func=AF.Sqrt), reads=[("rs", c)], writes=[("rs", c)]))
                        ch.append(lambda c=c: S.op("dve", lambda e: e.reciprocal(out=rs[:, c:c + 1], in_=rs[:, c:c + 1]), reads=[("rs", c)], writes=[("rs", c)]))
                        ch.append(lambda tb=tb, c=c, okeys=okeys: S.op("dve", lambda e: e.scalar_tensor_tensor(out=omix[:, tb, :], in0=omix[:, tb, :], scalar=rs[:, c:c + 1], in1=gt1,
                                                                                                          op0=ALU.mult, op1=ALU.mult),
                                                                       reads=okeys + [("rs", c), "gt1"], writes=okeys))
                        ch.append(lambda tb=tb, xb=xb, xbk=xbk, okeys=okeys: S.op("dve", lambda e: e.tensor_add(out=xb, in0=xb, in1=omix[:, tb, :]), reads=okeys + [xbk], writes=[xbk]))
                        ch.append(lambda tb=tb, xb=xb, xbk=xbk, b=b: S.dma("sp", lambda e: e.dma_start(out=x1_d[tb * 128:(tb + 1) * 128, :], in_=xb), f"x1w{b}", reads=[xbk], writes=[("x1d", tb)]))
                        ch += norm_transpose_ops(xb, xbk, gt2, "gt2", xs4s[b], ("xs4", b), sq4, "sq4", h2T, "h2T", tb, 2 + c)
                        chains.append(ch)
                    pipeline(chains, 8)
                S.barrier()
                sY.close()

                with contextlib.ExitStack() as s5:
                    wg = [sbt(s5, f"wg{i}", [128, 16, 256], BF16) for i in range(2)]
                    wu = [sbt(s5, f"wu{i}", [128, 16, 256], BF16) for i in range(2)]
                    sg = [sbt(s5, f"sg{i}", [128, 512], F32) for i in range(2)]
                    gview = wg_d.rearrange("(kc p) n -> p kc n", p=128)
                    uview = wu_d.rearrange("(kc p) n -> p kc n", p=128)
                    h2keys = [("h2T", tb, half) for tb in range(NT) for half in range(2)]
                    n5 = 0
                    for hg2 in range(NHC // 2):
                        sl = hg2 % 2
                        S.dma("pool", lambda e, sl=sl, hg2=hg2: e.dma_start(out=wg[sl][:], in_=gview[:, :, hg2 * 256:(hg2 + 1) * 256]), f"wg{sl}", writes=[("wg", sl)])
                        S.dma("pool", lambda e, sl=sl, hg2=hg2: e.dma_start(out=wu[sl][:], in_=uview[:, :, hg2 * 256:(hg2 + 1) * 256]), f"wu{sl}", writes=[("wu", sl)])
                        for hh in range(2):
                            hc = hg2 * 2 + hh
                            for th in range(2):
                                bg = (n5 % 3) * 2
                                bu = bg + 1
                                sgs = n5 % 2
                                n5 += 1

                                def f(e, sl=sl, hh=hh, th=th, bg=bg, bu=bu):
                                    for kc in range(16):
                                        e.matmul(ps[bg][:], lhsT=wg[sl][:, kc, hh * 128:(hh + 1) * 128], rhs=h2T[:, kc, th * 512:(th + 1) * 512],
                                                 start=(kc == 0), stop=(kc == 15))
                                    for kc in range(16):
                                        ins = e.matmul(ps[bu][:], lhsT=wu[sl][:, kc, hh * 128:(hh + 1) * 128], rhs=h2T[:, kc, th * 512:(th + 1) * 512],
                                                       start=(kc == 0), stop=(kc == 15))
                                    return ins
                                S.op("pe", f, reads=[("wg", sl), ("wu", sl)] + [("h2T", tb, half) for tb in range(th * 4, th * 4 + 4) for half in range(2)],
                                     writes=[("ps", bg), ("ps", bu)])
                                S.op("act", lambda e, bg=bg, sgs=sgs: e.activation(out=sg[sgs][:], in_=ps[bg][:], func=AF.Silu),
                                     reads=[("ps", bg)], writes=[("sg", sgs)])
                                S.op("dve", lambda e, bu=bu, sgs=sgs, hc=hc, th=th: e.tensor_mul(out=aT[:, hc, th * 512:(th + 1) * 512], in0=sg[sgs][:], in1=ps[bu][:]),
                                     reads=[("sg", sgs), ("ps", bu)], writes=[("aT", hc, th)])
                S.barrier()

                with contextlib.ExitStack() as s6:
                    wdn = [sbt(s6, f"wd{i}", [128, 4, 512], BF16) for i in range(2)]
                    gt3 = sbt(s6, "gt3", [128, D], F32)[:]
                    xb6 = sbt(s6, "xb6", [128, D], F32)[:]
                    sq6 = sbt(s6, "sq6", [128, D], BF16)[:]
                    S.dma("sp", lambda e: e.dma_start(out=gt3, in_=gpo_d.partition_broadcast(128)), None, writes=["gt3"])
                    dview = wd_d.rearrange("(hc p) n -> p hc n", p=128)
                    n6 = 0
                    for dc in range(4):
                        for hq in range(11):
                            sl = n6 % 2
                            n6 += 1
                            S.dma("pool", lambda e, sl=sl, hq=hq, dc=dc: e.dma_start(out=wdn[sl][:], in_=dview[:, hq * 4:(hq + 1) * 4, dc * 512:(dc + 1) * 512]),
                                  f"wd{sl}", writes=[("wd", sl)])
                            for tb in range(NT):
                                def f(e, sl=sl, hq=hq, tb=tb):
                                    for j in range(4):
                                        hc = hq * 4 + j
                                        ins = e.matmul(ps[tb][:], lhsT=aT[:, hc, tb * 128:(tb + 1) * 128], rhs=wdn[sl][:, j, :],
                                                       start=(hc == 0), stop=(hc == NHC - 1))
                                    return ins
                                S.op("pe", f, reads=[("wd", sl)] + [("aT", hq * 4 + j, tb // 4) for j in range(4)], writes=[("ps", tb)])
                        for tb in range(NT):
                            S.op("act", lambda e, tb=tb, dc=dc: e.activation(out=o_all[:, tb, dc * 512:(dc + 1) * 512], in_=ps[tb][:], func=AF.Copy),
                                 reads=[("ps", tb)], writes=[("o", tb, dc)])
                    xb6s = [xb6, sbt(s6, "xb6_b", [128, D], F32)[:]]
                    if do_A:
                        sA0 = contextlib.ExitStack()
                        gtA = sbt(sA0, "gtA", [128, D], F32)[:]
                        xsAs = [sbt(sA0, f"xsA{i}", [128, D], BF16)[:] for i in range(2)]
                        S.dma("sp", lambda e: e.dma_start(out=gtA, in_=gpre_d.partition_broadcast(128)), None, writes=["gtA"])
                    chains = []
                    for tb in range(NT):
                        c = tb % 2
                        b = tb % 2
                        xb, xbk = xb6s[b], ("xb6", b)
                        okeys = [("o", tb, dc) for dc in range(4)]
                        ch = []
                        ch.append(lambda tb=tb, xb=xb, xbk=xbk, b=b: S.dma("sp", lambda e: e.dma_start(out=xb, in_=x1_d[tb * 128:(tb + 1) * 128, :]), f"xb{b}", reads=[("x1d", tb)], writes=[xbk]))
                        ch.append(lambda tb=tb, c=c, okeys=okeys: S.op("act", lambda e: e.activation(out=sq6, in_=o_all[:, tb, :], func=AF.Square, accum_out=ss[:, c:c + 1]),
                                                                       reads=okeys, writes=["sq6", ("ss", c)]))
                        ch.append(lambda c=c: S.op("dve", lambda e: e.tensor_scalar(out=rs[:, c:c + 1], in0=ss[:, c:c + 1], scalar1=1.0 / D, scalar2=EPS,
                                                                                    op0=ALU.mult, op1=ALU.add), reads=[("ss", c)], writes=[("rs", c)]))
                        ch.append(lambda c=c: S.op("act", lambda e: e.activation(out=rs[:, c:c + 1], in_=rs[:, c:c + 1], func=AF.Sqrt), reads=[("rs", c)], writes=[("rs", c)]))
                        ch.append(lambda c=c: S.op("dve", lambda e: e.reciprocal(out=rs[:, c:c + 1], in_=rs[:, c:c + 1]), reads=[("rs", c)], writes=[("rs", c)]))
                        ch.append(lambda tb=tb, c=c, okeys=okeys: S.op("dve", lambda e: e.scalar_tensor_tensor(out=o_all[:, tb, :], in0=o_all[:, tb, :], scalar=rs[:, c:c + 1], in1=gt3,
                                                                                                          op0=ALU.mult, op1=ALU.mult),
                                                                       reads=okeys + [("rs", c), "gt3"], writes=okeys))
                        ch.append(lambda tb=tb, xb=xb, xbk=xbk, okeys=okeys: S.op("dve", lambda e: e.tensor_add(out=xb, in0=xb, in1=o_all[:, tb, :]), reads=okeys + [xbk], writes=[xbk]))
                        ch.append(lambda tb=tb, xb=xb, xbk=xbk, b=b: S.dma("sp", lambda e: e.dma_start(out=x_out[tb * 128:(tb + 1) * 128, :], in_=xb), f"xow{b}", reads=[xbk], writes=[("xout", tb)]))
                        if do_A:
                            ch += norm_transpose_ops(xb, xbk, gtA, "gtA", xsAs[b], ("xsA", b), sq6, "sq6", hT, "hT", tb, 4 + c)
                        chains.append(ch)
                    pipeline(chains, 8)
                    if do_A:
                        S.barrier()
                        sA0.close()
            S.barrier()

        if do_A:
            with contextlib.ExitStack() as sA:
                if not do_B:
                    a1 = Arena(R1, R1B)
                    gtA = a1.take([D], F32)
                    xsA = a1.take([D], BF16)
                    xbA = a1.take([D], F32)
                    sqA = a1.take([D], BF16)
                    S.dma("sp", lambda e: e.dma_start(out=gtA, in_=gpre_d.partition_broadcast(128)), None, writes=["gtA"])
                    for tb in range(NT):
                        S.dma("sp", lambda e, tb=tb: e.dma_start(out=xbA, in_=x_in[tb * 128:(tb + 1) * 128, :]), "xb", writes=["xbA"])
                        norm_transpose(xbA, "xbA", gtA, "gtA", xsA, "xsA", sqA, "sqA", hT, "hT", tb, 4 + tb % 2)
                    S.barrier()
                a1 = Arena(R1, R1B)
                qko = a1.take([16, T], BF16)
                asb = a1.take([4, T], F32)
                hgo = a1.take([4, T], F32)
                a2 = Arena(R2, R2B, start=32768)
                wi = [a2.take([16, 512], BF16) for i in range(2)]
                upo = a2.take([4, T], F32)
                ropec = a2.take([T], F32)
                ropes = a2.take([T], F32)
                pmat = sbt(sA, "pmat_sb", [128, 128], F32)
                sgA = [sbt(sA, f"sgA{i}", [128, 512], F32) for i in range(2)]
                qf = [sbt(sA, f"qf{i}", [128, 512], F32) for i in range(2)]
                t1 = [sbt(sA, f"t1{i}", [128, 512], F32) for i in range(2)]
                t2 = [sbt(sA, f"t2{i}", [128, 512], F32) for i in range(2)]
                vsb = sbt(sA, "vsb", [128, NT, 16, 65], BF16)
                S.dma("sp", lambda e: e.dma_start(out=ropec, in_=ropec_d), None, writes=["ropec"])
                S.dma("sp", lambda e: e.dma_start(out=ropes, in_=ropes_d), None, writes=["ropes"])
                S.dma("sp", lambda e: e.dma_start(out=pmat[:], in_=pmat_d), None, writes=["pmat"])
                S.op("dve", lambda e: e.memset(vsb[:], 1.0), writes=["vsb_ones"])
                hkeys = [("hT", tb, half) for tb in range(NT) for half in range(2)]
                wiv = win_d.rearrange("(kc p) n -> p kc n", p=128)
                nmm = 0
                nr = 0
                def load_wi(s_):
                    sl = s_ % 2
                    S.dma("pool", lambda e: e.dma_start(out=wi[sl], in_=wiv[:, :, s_ * 512:(s_ + 1) * 512]), f"wi{sl}", writes=[("wi", sl)])
                load_wi(0)
                for s_ in range(9):
                    sl = s_ % 2
                    if s_ + 1 < 9:
                        load_wi(s_ + 1)
                    if s_ in (6, 7):
                        for tb in range(NT):
                            bank = nmm % 4
                            nmm += 1

                            def f(e, sl=sl, tb=tb, bank=bank):
                                for kc in range(16):
                                    ins = e.matmul(ps[bank][:], lhsT=hT[:, kc, tb * 128:(tb + 1) * 128], rhs=wi[sl][:, kc, :], start=(kc == 0), stop=(kc == 15))
                                return ins
                            S.op("pe", f, reads=[("wi", sl)] + hkeys, writes=[("ps", bank)])
                            h0 = (s_ - 6) * 8
                            S.op("act", lambda e, tb=tb, bank=bank, h0=h0: e.activation(out=vsb[:, tb, h0:h0 + 8, 0:64],
                                                                                        in_=ps[bank][:].rearrange("p (h e) -> p h e", h=8), func=AF.Copy),
                                 reads=[("ps", bank), "vsb_ones"], writes=[("vsb", tb, s_)])
                        continue
                    for oc in range(4):
                        for th in range(2):
                            tsl = slice(th * 512, (th + 1) * 512)
                            bank = nmm % 4
                            nmm += 1

                            def f(e, sl=sl, oc=oc, th=th, bank=bank):
                                for kc in range(16):
                                    ins = e.matmul(ps[bank][:], lhsT=wi[sl][:, kc, oc * 128:(oc + 1) * 128], rhs=hT[:, kc, th * 512:(th + 1) * 512],
                                                   start=(kc == 0), stop=(kc == 15))
                                return ins
                            S.op("pe", f, reads=[("wi", sl)] + hkeys, writes=[("ps", bank)])
                            if s_ == 0:
                                S.op("act", lambda e, oc=oc, tsl=tsl, bank=bank: e.activation(out=asb[:, oc, tsl], in_=ps[bank][:], func=AF.Copy),
                                     reads=[("ps", bank)], writes=[("asb", oc, th)])
                            elif s_ == 1:
                                k2 = nmm % 2
                                S.op("act", lambda e, bank=bank, k2=k2: e.activation(out=sgA[k2][:], in_=ps[bank][:], func=AF.Sigmoid),
                                     reads=[("ps", bank)], writes=[("sgA", k2)])
                                S.op("dve", lambda e, oc=oc, tsl=tsl, k2=k2: e.tensor_mul(out=hgo[:, oc, tsl], in0=asb[:, oc, tsl], in1=sgA[k2][:]),
                                     reads=[("sgA", k2), ("asb", oc, th)], writes=[("hgo", oc, th)])
                            elif s_ == 8:
                                S.op("act", lambda e, oc=oc, tsl=tsl, bank=bank: e.activation(out=upo[:, oc, tsl], in_=ps[bank][:], func=AF.Copy),
                                     reads=[("ps", bank)], writes=[("upo", oc, th)])
                            else:
                                ch = (s_ - 2) * 4 + oc
                                k2 = nr % 2
                                pb = 4 + k2
                                nr += 1
                                S.op("act", lambda e, bank=bank, k2=k2: e.activation(out=qf[k2][:], in_=ps[bank][:], func=AF.Copy),
                                     reads=[("ps", bank)], writes=[("qf", k2)])
                                S.op("pe", lambda e, k2=k2, pb=pb: e.matmul(ps[pb][:], lhsT=pmat[:], rhs=qf[k2][:], start=True, stop=True),
                                     reads=[("qf", k2), "pmat"], writes=[("ps", pb)])
                                S.op("dve", lambda e, k2=k2, pb=pb, tsl=tsl: e.tensor_mul(out=t1[k2][:], in0=ps[pb][:], in1=ropes[:, tsl]),
                                     reads=[("ps", pb), "ropes"], writes=[("t1", k2)])
                                S.op("dve", lambda e, k2=k2, tsl=tsl: e.tensor_mul(out=t2[k2][:], in0=qf[k2][:], in1=ropec[:, tsl]),
                                     reads=[("qf", k2), "ropec"], writes=[("t2", k2)])
                                S.op("dve", lambda e, k2=k2, ch=ch, tsl=tsl: e.tensor_add(out=qko[:, ch, tsl], in0=t1[k2][:], in1=t2[k2][:]),
                                     reads=[("t1", k2), ("t2", k2)], writes=[("qko", ch, th)])
                S.dma("sp", lambda e: e.dma_start(out=hg_o.rearrange("c p t -> p c t"), in_=hgo), "outA",
                      reads=[("hgo", c, th) for c in range(4) for th in range(2)], writes=["hg_o"])
                S.dma("sp", lambda e: e.dma_start(out=up_o.rearrange("c p t -> p c t"), in_=upo), "outA",
                      reads=[("upo", c, th) for c in range(4) for th in range(2)], writes=["up_o"])
                S.dma("sp", lambda e: e.dma_start(out=qT_o.rearrange("c p t -> p c t"), in_=qko[:, 0:8, :]), "outA",
                      reads=[("qko", c, th) for c in range(8) for th in range(2)], writes=["qT_o"])
                S.dma("sp", lambda e: e.dma_start(out=kT_o.rearrange("c p t -> p c t"), in_=qko[:, 8:16, :]), "outA",
                      reads=[("qko", c, th) for c in range(8, 16) for th in range(2)], writes=["kT_o"])
                S.dma("sp", lambda e: e.dma_start(out=v_o.rearrange("t p f -> p t f"), in_=vsb[:].rearrange("p t h e -> p t (h e)")), "outA",
                      reads=[("vsb", tb, s_) for tb in range(NT) for s_ in (6, 7)], writes=["v_o"])
        S.barrier()
        S.run()
    return nc


_PROGS = {}


def _prog(do_B, do_A):
    key = (do_B, do_A)
    if key not in _PROGS:
        _PROGS[key] = build_program(do_B, do_A)
    return _PROGS[key]


def _const_tables():
    o = np.arange(-8 * 128 - 127, 8 * 128 + 128)
    w = ((np.abs(o) <= 64).astype(np.float32) + ((o % 4 == 0) & (np.abs(o) <= 256)) + ((o % 16 == 0) & (np.abs(o) <= 1024)))
    wmap = dict(zip(o.tolist(), w.tolist()))
    k = np.arange(128)[:, None, None]
    rel = np.arange(-8, 9)[None, :, None]
    q = np.arange(128)[None, None, :]
    off = rel * 128 + k - q
    masks = np.vectorize(wmap.get)(off).astype(np.float32).reshape(128, 17 * 128).astype(ml_dtypes.bfloat16)
    pm = np.zeros((128, 128), np.float32)
    for hh in range(2):
        for m in range(8):
            pm[hh * 64 + m + 8, hh * 64 + m] = -1.0
            pm[hh * 64 + m, hh * 64 + m + 8] = 1.0
    return masks, pm


def _rope_tables(c):
    pos = (np.arange(T) + c * T).astype(np.float32)
    inv = (np.float32(500000.0) ** (-np.arange(0, 16, 2, dtype=np.float32) / np.float32(16))).astype(np.float32)
    ang = (pos[:, None] * inv[None, :]).astype(np.float32)
    cs, sn = np.cos(ang).astype(np.float32), np.sin(ang).astype(np.float32)
    C = np.ones((128, T), np.float32)
    Sn = np.zeros((128, T), np.float32)
    for hh in range(2):
        for i in range(16):
            C[hh * 64 + i] = cs[:, i % 8]
            Sn[hh * 64 + i] = sn[:, i % 8]
    return C, Sn


def _corr_table(c):
    out = np.ones((4, 16), np.float32)
    idx = np.concatenate([np.arange(8), np.arange(T - 8, T)]) + c * T
    for gi, w in enumerate((2, 4, 8, 16)):
        lo = np.clip(idx - w // 2, 0, S_LEN)
        hi = np.clip(idx + w - w // 2, 0, S_LEN)
        out[gi] = w / (hi - lo).astype(np.float32)
    return np.ascontiguousarray(np.broadcast_to(out[None], (128, 4, 16)))


def _colmajor(v, n):
    return np.ascontiguousarray(v.reshape(n, 128).T)


def _run_step(step, xs, Aout, P, consts):
    f32 = np.float32
    masks, pm, ident, ropes, corrs = consts
    do_B = step > 0
    do_A = step < DEPTH
    lb, la = step - 1, step
    nc = _prog(do_B, do_A)
    maps = []
    if do_B:
        kT = np.concatenate([Aout[c]["kT_o"] for c in range(NCORES)], axis=2)
        kT = np.pad(kT, ((0, 0), (0, 0), (T, T)))
        vv = np.concatenate([Aout[c]["v_o"].reshape(T, 8, 130) for c in range(NCORES)], axis=0)
        vv = np.pad(vv, ((T, T), (0, 0), (0, 0)))
        hgf = np.pad(np.concatenate([Aout[c]["hg_o"] for c in range(NCORES)], axis=2), ((0, 0), (0, 0), (15, 15)))
        upf = np.pad(np.concatenate([Aout[c]["up_o"] for c in range(NCORES)], axis=2), ((0, 0), (0, 0), (8, 8)))
        cw = np.ascontiguousarray(np.asarray(P["conv_w"][lb], f32).T.reshape(4, 128, 31).transpose(1, 0, 2))
        vec4 = np.stack([_colmajor(np.asarray(P[k][lb], f32), 4) for k in ("conv_b", "conv_ln_g", "conv_ln_b", "pool_scale")], axis=1)
        gm = np.asarray(P["g_mix"][lb], f32)
    for c in range(NCORES):
        m = {"ident": ident, "x_in": xs[c]}
        if do_B:
            vh = vv[c * T:c * T + 3 * T].reshape(24, 128, 8, 130).transpose(2, 1, 0, 3)
            m.update({
                "qT": Aout[c]["qT_o"],
                "kTh": np.ascontiguousarray(kT[:, :, c * T:c * T + 3 * T]),
                "vh": np.ascontiguousarray(vh),
                "hgh": np.ascontiguousarray(hgf[:, :, c * T:c * T + T + 30]),
                "uh": np.ascontiguousarray(upf[:, :, c * T:c * T + T + 16]),
                "masks": masks, "corr": corrs[c], "cw": cw, "vec4": np.ascontiguousarray(vec4),
                "gmixT": _colmajor(gm, 16), "gmixB": np.ascontiguousarray(gm[None, 512:1536]),
                "poolw": np.asarray(P["pool_w"][lb], f32), "w_out": np.asarray(P["w_out"][lb], f32),
                "g_post_mix": np.asarray(P["g_post_mix"][lb], f32)[None], "g_pre_ffn": np.asarray(P["g_pre_ffn"][lb], f32)[None],
                "w_gate": np.asarray(P["w_gate"][lb], f32), "w_up": np.asarray(P["w_up"][lb], f32),
                "w_down": np.asarray(P["w_down"][lb], f32), "g_post_ffn": np.asarray(P["g_post_ffn"][lb], f32)[None],
            })
        if do_A:
            m.update({"w_in": np.asarray(P["w_in"][la], f32), "g_pre_mix": np.asarray(P["g_pre_mix"][la], f32)[None],
                      "rope_c": ropes[c][0], "rope_s": ropes[c][1], "pmat": pm})
        maps.append(m)
    res = run_bass_kernel_spmd(nc, maps, core_ids=list(range(NCORES)))
    Aout = res.results
    if do_B:
        xs = [np.asarray(Aout[c]["x_out"], f32) for c in range(NCORES)]
    return xs, Aout


def _consts():
    masks, pm = _const_tables()
    ident = np.eye(128, dtype=np.float32)
    ropes = [_rope_tables(c) for c in range(NCORES)]
    corrs = [_corr_table(c) for c in range(NCORES)]
    return masks, pm, ident, ropes, corrs


def kernel(x, w_in, conv_w, conv_b, conv_ln_g, conv_ln_b, pool_w, pool_scale, g_mix,
           w_out, g_pre_mix, g_post_mix, g_pre_ffn, g_post_ffn, w_gate, w_up, w_down):
    P = dict(w_in=w_in, conv_w=conv_w, conv_b=conv_b, conv_ln_g=conv_ln_g, conv_ln_b=conv_ln_b, pool_w=pool_w,
             pool_scale=pool_scale, g_mix=g_mix, w_out=w_out, g_pre_mix=g_pre_mix, g_post_mix=g_post_mix,
             g_pre_ffn=g_pre_ffn, g_post_ffn=g_post_ffn, w_gate=w_gate, w_up=w_up, w_down=w_down)
    x = np.asarray(x, np.float32)
    consts = _consts()
    xs = [np.ascontiguousarray(x[0, c * T:(c + 1) * T]) for c in range(NCORES)]
    Aout = None
    for step in range(DEPTH + 1):
        xs, Aout = _run_step(step, xs, Aout, P, consts)
    return np.concatenate(xs, axis=0)[None].astype(np.float32)
```

```python
import contextlib
import numpy as np
import ml_dtypes
import concourse.bass as bass
import concourse.mybir as mybir
from concourse.bass_utils import run_bass_kernel_spmd

F32 = mybir.dt.float32
BF16 = mybir.dt.bfloat16
AF = mybir.ActivationFunctionType
ALU = mybir.AluOpType

NCORES = 8
S_LEN = 8192
T = 1024
NT = 8
D = 2048
DEPTH = 4
INW = 4608
FF = 5632
NHC = FF // 128
EPS = 1e-6
ENGS = ("pe", "act", "dve", "pool", "sp")
import os
DEBUG = bool(os.environ.get("KDEBUG"))


class Sched:
    def __init__(self, nc):
        self.nc = nc
        self.ops = {e: [] for e in ENGS}
        self.cnt = {}
        self.waited = {e: {} for e in ENGS}
        self.last_w = {}
        self.readers = {}
        self.semnames = set(ENGS)
        self.sem_h = {}

    def _deps(self, eng, reads, writes):
        need = {}

        def add(k, v):
            if k == "pe" and eng == "pe":
                return
            if need.get(k, 0) < v:
                need[k] = v

        for k in reads:
            t = self.last_w.get(k)
            if t is not None:
                add(*t)
        for k in writes:
            t = self.last_w.get(k)
            if t is not None:
                add(*t)
            for kk, vv in self.readers.get(k, {}).items():
                add(kk, vv)
        return self._filter(eng, need)

    def _filter(self, eng, need):
        out = []
        w = self.waited[eng]
        for k, v in need.items():
            if w.get(k, 0) < v:
                w[k] = v
                out.append((k, v))
        return out

    def _track(self, tok, reads, writes):
        for k in writes:
            self.last_w[k] = tok
            self.readers[k] = {}
        for k in reads:
            r = self.readers.setdefault(k, {})
            if r.get(tok[0], 0) < tok[1]:
                r[tok[0]] = tok[1]

    def op(self, eng, fn, reads=(), writes=()):
        waits = self._deps(eng, reads, writes)
        self.cnt[eng] = self.cnt.get(eng, 0) + 1
        tok = (eng, self.cnt[eng])
        self._track(tok, reads, writes)
        self.ops[eng].append((waits, fn, eng, 1))
        return tok

    def dma(self, eng, fn, sem, reads=(), writes=()):
        if sem is None:
            self.nuniq = getattr(self, "nuniq", 0) + 1
            sem = f"u{self.nuniq}"
        self.semnames.add(sem)
        waits = self._deps(eng, reads, writes)
        self.cnt[sem] = self.cnt.get(sem, 0) + 16
        tok = (sem, self.cnt[sem])
        self._track(tok, reads, writes)
        self.ops[eng].append((waits, fn, sem, 16))
        return tok

    def barrier(self):
        for e in ENGS:
            waits = self._filter(e, dict(self.cnt))
            if waits:
                self.ops[e].append((waits, None, None, 0))

    def run(self):
        nc = self.nc
        with contextlib.ExitStack() as st:
            for name in sorted(self.semnames):
                self.sem_h[name] = st.enter_context(nc.semaphore("s_" + name))
            block = st.enter_context(nc.Block())
            sem_h = self.sem_h

            def replay(e, lst):
                for waits, fn, sem, inc in lst:
                    for k, v in waits:
                        e.wait_ge(sem_h[k], v)
                    if fn is not None:
                        fn(e).then_inc(sem_h[sem], inc)

            @block.tensor
            def _(e):
                replay(e, self.ops["pe"])

            @block.scalar
            def _(e):
                replay(e, self.ops["act"])

            @block.vector
            def _(e):
                replay(e, self.ops["dve"])

            @block.gpsimd
            def _(e):
                replay(e, self.ops["pool"])

            @block.sync
            def _(e):
                replay(e, self.ops["sp"])


class Arena:
    def __init__(self, t, nbytes, start=0):
        self.t, self.n, self.off = t, nbytes, start

    def take(self, shape, dt):
        esz = 2 if dt == BF16 else 4
        n = esz
        for d in shape:
            n *= d
        n = (n + 31) // 32 * 32
        assert self.off + n <= self.n, (self.off, n, self.n)
        ap = self.t[:, self.off // 4:(self.off + n) // 4]
        if dt != F32:
            ap = ap.bitcast(dt)
        tot = 1
        for d in shape:
            tot *= d
        ap = ap[:, 0:tot]
        if len(shape) == 2:
            ap = ap.rearrange("p (a b) -> p a b", a=shape[0])
        elif len(shape) == 3:
            ap = ap.rearrange("p (a b c) -> p a b c", a=shape[0], b=shape[1])
        self.off += n
        return ap


def build_program(do_B, do_A):
    nc = bass.Bass("TRN2", target_bir_lowering=False)
    S = Sched(nc)

    def din(name, shape, dt=F32):
        return nc.dram_tensor(name, list(shape), dt, kind="ExternalInput").ap()

    def dout(name, shape, dt=F32):
        return nc.dram_tensor(name, list(shape), dt, kind="ExternalOutput").ap()

    ident_d = din("ident", [128, 128])
    x_in = din("x_in", [T, D])
    if do_B:
        qT_d = din("qT", [8, 128, T], BF16)
        kT_d = din("kTh", [8, 128, 3 * T], BF16)
        v_d = din("vh", [8, 128, 24, 130], BF16)
        hg_d = din("hgh", [4, 128, T + 30])
        up_d = din("uh", [4, 128, T + 16])
        mask_d = din("masks", [128, 10 * 128], BF16)
        fmask_d = din("fmask", [128, 2, 256], BF16)
        vc_d = din("vc", [8, 128, 16, 2, 130], BF16)
        kTc_d = din("kTc", [8, 128, 16 * 192], BF16)
        qTc_d = din("qTc", [8, 128, 16 * 64], BF16)
        far_d = nc.dram_tensor("far_scratch", [8, T, 130], F32).ap()
        corr_d = din("corr", [128, 4, 16])
        cw_d = din("cw", [128, 4, 31])
        vec4_d = din("vec4", [128, 4, 4])
        gmixT_d = din("gmixT", [128, 16])
        gmixB_d = din("gmixB", [1, 1024])
        poolw_d = din("poolw", [4, 128, 128])
        wout_d = din("w_out", [D, D])
        gpm_d = din("g_post_mix", [1, D])
        gpf_d = din("g_pre_ffn", [1, D])
        wg_d = din("w_gate", [D, FF])
        wu_d = din("w_up", [D, FF])
        wd_d = din("w_down", [FF, D])
        gpo_d = din("g_post_ffn", [1, D])
        x_out = dout("x_out", [T, D])
        if DEBUG:
            x1_d = dout("x1_dbg", [T, D])
            yT_dbg = dout("yT_dbg", [16, 128, T], BF16)
        else:
            x1_d = nc.dram_tensor("x1_scratch", [T, D], F32).ap()
    if do_A:
        win_d = din("w_in", [D, INW])
        gpre_d = din("g_pre_mix", [1, D])
        ropec_d = din("rope_c", [128, T])
        ropes_d = din("rope_s", [128, T])
        pmat_d = din("pmat", [128, 128])
        qT_o = dout("qT_o", [8, 128, T], BF16)
        kT_o = dout("kT_o", [8, 128, T], BF16)
        v_o = dout("v_o", [NT, 128, 16 * 65], BF16)
        hg_o = dout("hg_o", [4, 128, T])
        up_o = dout("up_o", [4, 128, T])

    with contextlib.ExitStack() as top:
        def sbt(st, name, shape, dt):
            return st.enter_context(nc.sbuf_tensor("sb_" + name, list(shape), dt))

        ps = [top.enter_context(nc.psum_tensor(f"ps{i}", [128, 512], F32)) for i in range(8)]
        ident = sbt(top, "ident_sb", [128, 128], BF16)
        onesf = sbt(top, "onesf", [128, 128], F32)
        R1 = sbt(top, "R1", [128, 16384], F32)
        R2 = sbt(top, "R2", [128, 22528], F32)
        R1B, R2B = 65536, 90112
        h2T = Arena(R1, R1B).take([16, T], BF16)
        o_all = Arena(R1, R1B).take([NT, D], F32)
        aT = Arena(R2, R2B).take([NHC, T], BF16)
        hT = Arena(R2, R2B).take([16, T], BF16)
        ss = sbt(top, "ss", [128, 16], F32)
        rs = sbt(top, "rs", [128, 16], F32)
        ptr = ps[7][:].bitcast(BF16).rearrange("p (j t) -> p j t", j=8)

        S.dma("pool", lambda e: e.dma_start(out=ident[:], in_=ident_d), None, writes=["ident"])
        S.op("dve", lambda e: e.memset(onesf[:], 1.0 / 512.0), writes=["onesf"])

        def norm_transpose_ops(xblk, xkey, gt, gkey, xs, xskey, sq, sqkey, dstT, dkey, tb, slot):
            c = slot
            ops = []
            ops.append(lambda: S.op("act", lambda e: e.activation(out=sq, in_=xblk, func=AF.Square, accum_out=ss[:, c:c + 1]),
                                    reads=[xkey], writes=[sqkey, ("ss", c)]))
            ops.append(lambda: S.op("dve", lambda e: e.tensor_scalar(out=rs[:, c:c + 1], in0=ss[:, c:c + 1], scalar1=1.0 / D, scalar2=EPS,
                                                                     op0=ALU.mult, op1=ALU.add), reads=[("ss", c)], writes=[("rs", c)]))
            ops.append(lambda: S.op("act", lambda e: e.activation(out=rs[:, c:c + 1], in_=rs[:, c:c + 1], func=AF.Sqrt),
                                    reads=[("rs", c)], writes=[("rs", c)]))
            ops.append(lambda: S.op("dve", lambda e: e.reciprocal(out=rs[:, c:c + 1], in_=rs[:, c:c + 1]), reads=[("rs", c)], writes=[("rs", c)]))
            ops.append(lambda: S.op("dve", lambda e: e.scalar_tensor_tensor(out=xs, in0=xblk, scalar=rs[:, c:c + 1], in1=gt,
                                                                            op0=ALU.mult, op1=ALU.mult),
                                    reads=[xkey, gkey, ("rs", c)], writes=[xskey]))
            for half in range(2):
                def tr(e, half=half):
                    for j in range(8):
                        kc = half * 8 + j
                        i = e.transpose(out=ptr[:, j, :], in_=xs[:, kc * 128:(kc + 1) * 128], identity=ident[:])
                    return i
                ops.append(lambda tr=tr: S.op("pe", tr, reads=[xskey, "ident"], writes=[("ps", 7)]))
                ops.append(lambda half=half: S.op("act", lambda e: e.activation(out=dstT[:, half * 8:(half + 1) * 8, tb * 128:(tb + 1) * 128],
                                                                               in_=ptr[:], func=AF.Copy),
                                                  reads=[("ps", 7)], writes=[(dkey, tb, half)]))
            return ops

        def norm_transpose(*a):
            for o in norm_transpose_ops(*a):
                o()

        def pipeline(chains, stagger):
            n = max(len(c) for c in chains) + stagger * (len(chains) - 1)
            for step in range(n):
                for i, ch in enumerate(chains):
                    k = step - i * stagger
                    if 0 <= k < len(ch):
                        ch[k]()

        if do_B:
            with contextlib.ExitStack() as sB:
                gmixT = sbt(sB, "gmixT", [128, 16], F32)
                vec4 = sbt(sB, "vec4", [128, 4, 4], F32)
                sY = contextlib.ExitStack()
                yT = sbt(sY, "yT", [128, 16, T], BF16)
                S.dma("sp", lambda e: e.dma_start(out=gmixT[:], in_=gmixT_d), None, writes=["gmixT"])
                S.dma("sp", lambda e: e.dma_start(out=vec4[:], in_=vec4_d), None, writes=["vec4"])

                with contextlib.ExitStack() as s1:
                    a2 = Arena(R2, R2B)
                    yb = a2.take([NT, 1024], F32)
                    hg = a2.take([4, T + 30], F32)
                    acc = a2.take([4, T], F32)
                    uh = a2.take([4, T + 16], F32)
                    masks = a2.take([10 * 128], BF16)
                    ybs = a2.take([1024], BF16)
                    a1 = Arena(R1, R1B)
                    pw = a1.take([3, T + 16], F32)
                    pooled = a1.take([4, T], BF16)
                    sqb = a1.take([1024], BF16)
                    gmixB = a1.take([1024], F32)
                    tmpA = a1.take([512], F32)
                    tmpB = a1.take([512], F32)
                    meansb = a1.take([512], F32)
                    rstdsb = a1.take([512], F32)
                    qkv_v = [a1.take([24, 130], BF16) for i in range(2)]
                    qkv_q = [a1.take([T], BF16) for i in range(2)]
                    qkv_k = [a1.take([3 * T], BF16) for i in range(2)]
                    aF = Arena(R2, R2B)
                    vcs = [aF.take([16, 2, 130], BF16) for i in range(2)]
                    farsb = aF.take([16, 130], F32)
                    pTA = [aF.take([256], BF16) for i in range(2)]
                    pTB = [aF.take([256], BF16) for i in range(2)]
                    fmask = aF.take([2, 256], BF16)
                    assert aF.off <= 32768
                    farnat = [sbt(s1, f"farnat{i}", [128, NT, 130], F32) for i in range(2)]
                    totsb = [sbt(s1, f"totsb{i}", [128, 2, 65], F32) for i in range(2)]
                    ya = acc
                    yc = hg
                    cw = sbt(s1, "cw", [128, 4, 31], F32)
                    corr = sbt(s1, "corr", [128, 4, 16], F32)
                    poolw = sbt(s1, "poolw", [128, 4, 128], BF16)
                    pT = [sbt(s1, f"pT{i}", [128, 512], BF16) for i in range(4)]
                    rec = sbt(s1, "rec", [128, 2, 2], F32)

                    S.dma("sp", lambda e: e.dma_start(out=hg, in_=hg_d.rearrange("c p t -> p c t")), None, writes=["hg"])
                    S.dma("sp", lambda e: e.dma_start(out=uh, in_=up_d.rearrange("c p t -> p c t")), None, writes=["uh"])
                    S.dma("sp", lambda e: e.dma_start(out=cw[:], in_=cw_d), None, writes=["cw"])
                    S.dma("sp", lambda e: e.dma_start(out=corr[:], in_=corr_d), None, writes=["corr"])
                    S.dma("sp", lambda e: e.dma_start(out=masks, in_=mask_d), None, writes=["masks"])
                    S.dma("sp", lambda e: e.dma_start(out=gmixB, in_=gmixB_d.partition_broadcast(128)), None, writes=["gmixB"])
                    S.dma("pool", lambda e: e.dma_start(out=poolw[:], in_=poolw_d.rearrange("g c d -> c g d")), None, writes=["poolw"])


                    bg_ops = []
                    for c in range(4):
                        bg_ops.append(lambda c=c: S.op("dve", lambda e: e.tensor_scalar(out=acc[:, c, :], in0=hg[:, c, 0:T], scalar1=cw[:, c, 0:1],
                                                                                      scalar2=vec4[:, 0, c:c + 1], op0=ALU.mult, op1=ALU.add),
                                                       reads=["hg", "cw", "vec4"], writes=[("acc", c)]))
                        for j in range(1, 31):
                            bg_ops.append(lambda c=c, j=j: S.op("dve", lambda e: e.scalar_tensor_tensor(
                                out=acc[:, c, :], in0=hg[:, c, j:j + T], scalar=cw[:, c, j:j + 1], in1=acc[:, c, :], op0=ALU.mult, op1=ALU.add),
                                reads=["hg", "cw", ("acc", c)], writes=[("acc", c)]))
                    for gi in range(4):
                        w = 2 << gi
                        src = uh[:, gi, :]
                        L = T + 16
                        step = 1
                        lvl = 0
                        while step < w:
                            L2 = L - step
                            dst = pw[:, lvl % 3, :]
                            S.op("pool", lambda e, src=src, dst=dst, L2=L2, step=step: e.tensor_add(
                                out=dst[:, 0:L2], in0=src[:, 0:L2], in1=src[:, step:step + L2]),
                                reads=["uh", ("pw", (lvl + 2) % 3)], writes=[("pw", lvl % 3)])
                            src = dst
                            L = L2
                            step *= 2
                            lvl += 1
                        off = 8 - w // 2
                        win = src[:, off:off + T]
                        lastkey = ("pw", (lvl - 1) % 3)
                        S.op("pool", lambda e, win=win, gi=gi: e.tensor_mul(out=win[:, 0:8], in0=win[:, 0:8], in1=corr[:, gi, 0:8]),
                             reads=[lastkey, "corr"], writes=[lastkey])
                        S.op("pool", lambda e, win=win, gi=gi: e.tensor_mul(out=win[:, T - 8:T], in0=win[:, T - 8:T], in1=corr[:, gi, 8:16]),
                             reads=[lastkey], writes=[lastkey])
                        S.op("pool", lambda e, win=win, w=w: e.tensor_scalar(out=win, in0=win, scalar1=1.0 / w, scalar2=None, op0=ALU.mult),
                             reads=[lastkey], writes=[lastkey])
                        S.op("pool", lambda e, win=win, gi=gi: e.tensor_sub(out=pooled[:, gi, :], in0=win, in1=uh[:, gi, 8:8 + T]),
                             reads=[lastkey, "uh"], writes=[("pooled", gi)])

                    S.dma("sp", lambda e: e.dma_start(out=fmask, in_=fmask_d), None, writes=["fmask"])

                    def load_far(hp):
                        sl = hp % 2
                        S.dma("sp", lambda e: e.dma_start(out=qkv_q[sl], in_=qTc_d[hp]), f"q{sl}", writes=[("q", sl)])
                        S.dma("sp", lambda e: e.dma_start(out=qkv_k[sl], in_=kTc_d[hp]), f"k{sl}", writes=[("k", sl)])
                        S.dma("sp", lambda e: e.dma_start(out=vcs[sl], in_=vc_d[hp]), f"v{sl}", writes=[("vc", sl)])

                    funits = [(hp, h, g4) for hp in range(8) for h in range(2) for g4 in range(4)]
                    NF = len(funits)
                    bg_ops.reverse()

                    def far_qk(i):
                        hp, h, g4 = funits[i]
                        sl = hp % 2
                        bA, bB = i % 2, 2 + i % 2

                        def f(e):
                            for c4 in range(4):
                                rho = 4 * g4 + c4
                                e.matmul(ps[bA][:, c4 * 64:(c4 + 1) * 64], lhsT=qkv_k[sl][h * 64:(h + 1) * 64, rho * 192:rho * 192 + 128],
                                         rhs=qkv_q[sl][h * 64:(h + 1) * 64, rho * 64:(rho + 1) * 64], start=True, stop=True)
                                ins = e.matmul(ps[bB][0:64, c4 * 64:(c4 + 1) * 64], lhsT=qkv_k[sl][h * 64:(h + 1) * 64, rho * 192 + 128:(rho + 1) * 192],
                                               rhs=qkv_q[sl][h * 64:(h + 1) * 64, rho * 64:(rho + 1) * 64], start=True, stop=True)
                            return ins
                        S.op("pe", f, reads=[("q", sl), ("k", sl)], writes=[("ps", bA), ("ps", bB)])
                        k2 = i % 2
                        S.op("act", lambda e: e.activation(out=pTA[k2], in_=ps[bA][:, 0:256], func=AF.Exp, scale=0.125), reads=[("ps", bA)], writes=[("pTA", k2)])
                        S.op("act", lambda e: e.activation(out=pTB[k2][0:64, :], in_=ps[bB][0:64, 0:256], func=AF.Exp, scale=0.125), reads=[("ps", bB)], writes=[("pTB", k2)])
                        S.op("dve", lambda e: e.tensor_mul(out=pTA[k2], in0=pTA[k2], in1=fmask[:, 0, :]), reads=[("pTA", k2), "fmask"], writes=[("pTA", k2)])
                        S.op("dve", lambda e: e.tensor_mul(out=pTB[k2][0:64, :], in0=pTB[k2][0:64, :], in1=fmask[0:64, 1, :]), reads=[("pTB", k2), "fmask"], writes=[("pTB", k2)])

                    def far_pv(i):
                        hp, h, g4 = funits[i]
                        sl = hp % 2
                        k2 = i % 2
                        bO = 4 + i % 2

                        def f(e):
                            for c4 in range(4):
                                rho = 4 * g4 + c4
                                e.matmul(ps[bO][0:64, c4 * 128:c4 * 128 + 65], lhsT=pTA[k2][:, c4 * 64:(c4 + 1) * 64],
                                         rhs=vcs[sl][:, rho, 0, h * 65:(h + 1) * 65], start=True, stop=False)
                                ins = e.matmul(ps[bO][0:64, c4 * 128:c4 * 128 + 65], lhsT=pTB[k2][0:64, c4 * 64:(c4 + 1) * 64],
                                               rhs=vcs[sl][0:64, rho, 1, h * 65:(h + 1) * 65], start=False, stop=True)
                            return ins
                        S.op("pe", f, reads=[("pTA", k2), ("pTB", k2), ("vc", sl)], writes=[("ps", bO)])
                        S.op("act", lambda e: e.activation(out=farsb[0:64, 4 * g4:4 * g4 + 4, h * 65:(h + 1) * 65],
                                                           in_=ps[bO][0:64, :].rearrange("p (c f) -> p c f", c=4)[:, :, 0:65], func=AF.Copy),
                             reads=[("ps", bO)], writes=[("farsb", h, g4)])
                        if h == 1 and g4 == 3:
                            S.dma("sp", lambda e: e.dma_start(out=far_d[hp].rearrange("(j r) f -> j r f", r=16), in_=farsb[0:64, :, :]), "fard",
                                  reads=[("farsb", hh, gg) for hh in range(2) for gg in range(4)], writes=[("fard", hp)])

                    load_far(0)
                    for i in range(0 if not os.environ.get("KSKIP_FAR") else NF + 1, NF + 1):
                        if i % 2 == 1 and bg_ops:
                            bg_ops.pop()()
                        if i < NF:
                            far_qk(i)
                        if i >= 1:
                            far_pv(i - 1)
                        if i < NF and funits[i][1] == 0 and funits[i][2] == 1 and funits[i][0] + 1 < 8:
                            load_far(funits[i][0] + 1)
                    S.barrier()

                    def load_qkv(hp):
                        sl = hp % 2
                        S.dma("sp", lambda e: e.dma_start(out=qkv_q[sl], in_=qT_d[hp]), f"q{sl}", writes=[("q", sl)])
                        S.dma("sp", lambda e: e.dma_start(out=qkv_k[sl], in_=kT_d[hp]), f"k{sl}", writes=[("k", sl)])
                        S.dma("sp", lambda e: e.dma_start(out=qkv_v[sl], in_=v_d[hp]), f"v{sl}", writes=[("v", sl)])
                        S.dma("sp", lambda e: e.dma_start(out=farnat[sl][:], in_=far_d[hp].rearrange("(qb p) f -> p qb f", p=128)), f"fn{sl}",
                              reads=[("fard", hp)], writes=[("farnat", sl)])

                    units = []
                    for hp in range(8):
                        for qb in range(NT):
                            for h in range(2):
                                for g, (r0, n) in enumerate(((-2, 4), (2, 1))):
                                    units.append((hp, qb, h, g, r0, n))
                    NU = len(units)

                    def emit_qk(i):
                        hp, qb, h, g, r0, n = units[i]
                        sl = hp % 2
                        bank = i % 3
                        kb0 = qb + 8 + r0

                        def f(e):
                            for j in range(n):
                                ins = e.matmul(ps[bank][:, j * 128:(j + 1) * 128],
                                               lhsT=qkv_k[sl][h * 64:(h + 1) * 64, (kb0 + j) * 128:(kb0 + j + 1) * 128],
                                               rhs=qkv_q[sl][h * 64:(h + 1) * 64, qb * 128:(qb + 1) * 128], start=True, stop=True)
                            return ins
                        S.op("pe", f, reads=[("q", sl), ("k", sl)], writes=[("ps", bank)])
                        slot = i % 4
                        S.op("act", lambda e: e.activation(out=pT[slot][:, 0:n * 128], in_=ps[bank][:, 0:n * 128], func=AF.Exp, scale=0.125),
                             reads=[("ps", bank)], writes=[("pT", slot)])
                        S.op("dve", lambda e: e.tensor_mul(out=pT[slot][:, 0:n * 128], in0=pT[slot][:, 0:n * 128],
                                                           in1=masks[:, (r0 + 2) * 128:(r0 + 2 + n) * 128]),
                             reads=[("pT", slot), "masks"], writes=[("pT", slot)])

                    def emit_pv(i):
                        hp, qb, h, g, r0, n = units[i]
                        sl = hp % 2
                        slot = i % 4
                        par = (hp * NT + qb) % 2
                        ob = 3 + par
                        kb0 = qb + 8 + r0

                        def f(e):
                            for j in range(n):
                                ins = e.matmul(ps[ob][:, h * 128:h * 128 + 65], lhsT=pT[slot][:, j * 128:(j + 1) * 128],
                                               rhs=qkv_v[sl][:, kb0 + j, h * 65:(h + 1) * 65],
                                               start=(g == 0 and j == 0), stop=(g == 1))
                            return ins
                        S.op("pe", f, reads=[("pT", slot), ("v", sl)], writes=[("ps", ob)])
                        if h == 1 and g == 1:
                            S.op("dve", lambda e: e.tensor_add(out=totsb[par][:], in0=ps[ob][:, 0:256].rearrange("p (h c) -> p h c", h=2)[:, :, 0:65],
                                                               in1=farnat[sl][:, qb, :].rearrange("p (h c) -> p h c", h=2)),
                                 reads=[("ps", ob), ("farnat", sl)], writes=[("totsb", par)])
                            S.op("dve", lambda e: e.reciprocal(out=rec[:, par, :], in_=totsb[par][:, :, 64]),
                                 reads=[("totsb", par)], writes=[("rec", par)])
                            for hh in range(2):
                                S.op("dve", lambda e, hh=hh: e.tensor_scalar(
                                    out=yb[:, qb, hp * 128 + hh * 64:hp * 128 + hh * 64 + 64], in0=totsb[par][:, hh, 0:64],
                                    scalar1=rec[:, par, hh:hh + 1], scalar2=None, op0=ALU.mult),
                                    reads=[("totsb", par), ("rec", par)], writes=[("yb", qb, hp, hh)])

                    load_qkv(0)
                    LAG = 2
                    for i in range(0 if not os.environ.get("KSKIP_NEAR") else NU + LAG, NU + LAG):
                        if i % 2 == 1 and bg_ops:
                            bg_ops.pop()()
                        if i < NU:
                            emit_qk(i)
                        if i - LAG >= 0:
                            emit_pv(i - LAG)
                        if i < NU:
                            hp_, qb_, h_, g_ = units[i][:4]
                            if qb_ == 0 and h_ == 1 and g_ == 0 and hp_ + 1 < 8:
                                load_qkv(hp_ + 1)
                    while bg_ops:
                        bg_ops.pop()()
                    ybkeys = [[("yb", qb, hp, hh) for hp in range(8) for hh in range(2)] for qb in range(NT)]
                    for qb in range(NT):
                        c = 8 + (qb % 2)
                        S.op("act", lambda e, qb=qb, c=c: e.activation(out=sqb, in_=yb[:, qb, :], func=AF.Square, accum_out=ss[:, c:c + 1]),
                             reads=ybkeys[qb], writes=["sqb", ("ss", c)])
                        S.op("dve", lambda e, c=c: e.tensor_scalar(out=rs[:, c:c + 1], in0=ss[:, c:c + 1], scalar1=1.0 / 1024.0, scalar2=EPS,
                                                                   op0=ALU.mult, op1=ALU.add), reads=[("ss", c)], writes=[("rs", c)])
                        S.op("act", lambda e, c=c: e.activation(out=rs[:, c:c + 1], in_=rs[:, c:c + 1], func=AF.Sqrt),
                             reads=[("rs", c)], writes=[("rs", c)])
                        S.op("dve", lambda e, c=c: e.reciprocal(out=rs[:, c:c + 1], in_=rs[:, c:c + 1]), reads=[("rs", c)], writes=[("rs", c)])
                        S.op("dve", lambda e, qb=qb, c=c: e.scalar_tensor_tensor(out=ybs, in0=yb[:, qb, :], scalar=rs[:, c:c + 1], in1=gmixB,
                                                                                 op0=ALU.mult, op1=ALU.mult),
                             reads=ybkeys[qb] + [("rs", c), "gmixB"], writes=["ybs"])

                        def tr(e):
                            for j in range(8):
                                i_ = e.transpose(out=ptr[:, j, :], in_=ybs[:, j * 128:(j + 1) * 128], identity=ident[:])
                            return i_
                        S.op("pe", tr, reads=["ybs", "ident"], writes=[("ps", 7)])
                        S.op("act", lambda e, qb=qb: e.activation(out=yT[:, 4:12, qb * 128:(qb + 1) * 128], in_=ptr[:], func=AF.Copy),
                             reads=[("ps", 7)], writes=[("yT", "b", qb)])

                    def rms_feat(src, srckeys, base, tagk):
                        for th in range(2):
                            tsl = slice(th * 512, (th + 1) * 512)
                            for c in range(4):
                                S.op("act", lambda e, c=c, tsl=tsl: e.activation(out=tmpA, in_=src[:, c, tsl], func=AF.Square),
                                     reads=[srckeys(c, th)], writes=["tmpA"])
                                S.op("pe", lambda e, c=c: e.matmul(ps[0][:], lhsT=onesf[:], rhs=tmpA, start=(c == 0), stop=(c == 3)),
                                     reads=["tmpA", "onesf"], writes=[("ps", 0)])
                            S.op("dve", lambda e: e.tensor_scalar(out=rstdsb, in0=ps[0][:], scalar1=EPS, scalar2=None, op0=ALU.add),
                                 reads=[("ps", 0)], writes=["rstdsb"])
                            S.op("act", lambda e: e.activation(out=rstdsb, in_=rstdsb, func=AF.Sqrt), reads=["rstdsb"], writes=["rstdsb"])
                            S.op("dve", lambda e: e.reciprocal(out=rstdsb, in_=rstdsb), reads=["rstdsb"], writes=["rstdsb"])
                            for c in range(4):
                                S.op("dve", lambda e, c=c, tsl=tsl: e.scalar_tensor_tensor(out=yT[:, base + c, tsl], in0=src[:, c, tsl],
                                                                                  scalar=gmixT[:, base + c:base + c + 1], in1=rstdsb,
                                                                                  op0=ALU.mult, op1=ALU.mult),
                                     reads=[srckeys(c, th), "rstdsb", "gmixT"], writes=[("yT", tagk, c, th)])

                    for th in range(2):
                        tsl = slice(th * 512, (th + 1) * 512)
                        for c in range(4):
                            S.op("pe", lambda e, c=c, tsl=tsl: e.matmul(ps[1][:], lhsT=onesf[:], rhs=acc[:, c, tsl], start=(c == 0), stop=(c == 3)),
                                 reads=[("acc", c), "onesf"], writes=[("ps", 1)])
                        for c in range(4):
                            S.op("act", lambda e, c=c, tsl=tsl: e.activation(out=tmpA, in_=acc[:, c, tsl], func=AF.Square),
                                 reads=[("acc", c)], writes=["tmpA"])
                            S.op("pe", lambda e, c=c: e.matmul(ps[2][:], lhsT=onesf[:], rhs=tmpA, start=(c == 0), stop=(c == 3)),
                                 reads=["tmpA", "onesf"], writes=[("ps", 2)])
                        S.op("act", lambda e: e.activation(out=meansb, in_=ps[1][:], func=AF.Copy), reads=[("ps", 1)], writes=["meansb"])
                        S.op("dve", lambda e: e.tensor_mul(out=tmpB, in0=meansb, in1=meansb), reads=["meansb"], writes=["tmpB"])
                        S.op("dve", lambda e: e.tensor_sub(out=rstdsb, in0=ps[2][:], in1=tmpB), reads=[("ps", 2), "tmpB"], writes=["rstdsb"])
                        S.op("dve", lambda e: e.tensor_scalar(out=rstdsb, in0=rstdsb, scalar1=EPS, scalar2=None, op0=ALU.add),
                             reads=["rstdsb"], writes=["rstdsb"])
                        S.op("act", lambda e: e.activation(out=rstdsb, in_=rstdsb, func=AF.Sqrt), reads=["rstdsb"], writes=["rstdsb"])
                        S.op("dve", lambda e: e.reciprocal(out=rstdsb, in_=rstdsb), reads=["rstdsb"], writes=["rstdsb"])
                        for c in range(4):
                            S.op("dve", lambda e, c=c, tsl=tsl: e.tensor_sub(out=tmpB, in0=acc[:, c, tsl], in1=meansb),
                                 reads=[("acc", c), "meansb"], writes=["tmpB"])
                            S.op("dve", lambda e: e.tensor_mul(out=tmpB, in0=tmpB, in1=rstdsb), reads=["tmpB", "rstdsb"], writes=["tmpB"])
                            S.op("act", lambda e, c=c, tsl=tsl: e.activation(out=ya[:, c, tsl], in_=tmpB, func=AF.Silu,
                                                                    scale=vec4[:, 1, c:c + 1], bias=vec4[:, 2, c:c + 1]),
                                 reads=["tmpB", "vec4"], writes=[("ya", c, th)])
                    rms_feat(ya, lambda c, th: ("ya", c, th), 0, "a")

                    for gi in range(4):
                        for th in range(2):
                            tsl = slice(th * 512, (th + 1) * 512)
                            S.op("pe", lambda e, gi=gi, tsl=tsl: e.matmul(ps[1][:], lhsT=poolw[:, gi, :], rhs=pooled[:, gi, tsl], start=True, stop=True),
                                 reads=[("pooled", gi), "poolw"], writes=[("ps", 1)])
                            S.op("act", lambda e, gi=gi, tsl=tsl: e.activation(out=yc[:, gi, tsl], in_=ps[1][:], func=AF.Copy, scale=vec4[:, 3, gi:gi + 1]),
                                 reads=[("ps", 1), "vec4"], writes=[("yc", gi, th)])
                    rms_feat(yc, lambda c, th: ("yc", c, th), 12, "c")
                    if DEBUG:
                        S.barrier()
                        S.dma("sp", lambda e: e.dma_start(out=yT_dbg.rearrange("c p t -> p c t"), in_=yT[:]), "dbg", writes=["dbg"])
                S.barrier()

                with contextlib.ExitStack() as s4:
                    a2 = Arena(R2, R2B)
                    omix = a2.take([NT, D], F32)
                    gt1 = a2.take([D], F32)
                    gt2 = a2.take([D], F32)
                    xbs = [a2.take([D], F32), sbt(s4, "xb_b", [128, D], F32)[:]]
                    a1u = Arena(R1, R1B, start=32768)
                    wo = [a1u.take([16, 512], BF16) for i in range(2)]
                    sq4 = sbt(s4, "sq4", [128, D], BF16)[:]
                    xs4s = [sbt(s4, f"xs4{i}", [128, D], BF16)[:] for i in range(2)]
                    S.dma("sp", lambda e: e.dma_start(out=gt1, in_=gpm_d.partition_broadcast(128)), None, writes=["gt1"])
                    S.dma("sp", lambda e: e.dma_start(out=gt2, in_=gpf_d.partition_broadcast(128)), None, writes=["gt2"])
                    wview = wout_d.rearrange("(kc p) n -> p kc n", p=128)
                    for dc in range(4):
                        sl = dc % 2
                        S.dma("pool", lambda e, sl=sl, dc=dc: e.dma_start(out=wo[sl], in_=wview[:, :, dc * 512:(dc + 1) * 512]),
                              f"wo{sl}", writes=[("wo", sl)])
                        for tb in range(NT):
                            bank = tb % 4

                            def f(e, tb=tb, sl=sl, bank=bank):
                                for kc in range(16):
                                    ins = e.matmul(ps[bank][:], lhsT=yT[:, kc, tb * 128:(tb + 1) * 128], rhs=wo[sl][:, kc, :],
                                                   start=(kc == 0), stop=(kc == 15))
                                return ins
                            S.op("pe", f, reads=[("wo", sl)] + [("yT", "b", tb)] + [("yT", k, c, tb // 4) for k in ("a", "c") for c in range(4)],
                                 writes=[("ps", bank)])
                            S.op("act", lambda e, tb=tb, dc=dc, bank=bank: e.activation(out=omix[:, tb, dc * 512:(dc + 1) * 512], in_=ps[bank][:], func=AF.Copy),
                                 reads=[("ps", bank)], writes=[("omix", tb, dc)])
                    chains = []
                    for tb in range(NT):
                        b = tb % 2
                        c = tb % 2
                        xb, xbk = xbs[b], ("xb", b)
                        okeys = [("omix", tb, dc) for dc in range(4)]
                        ch = []
                        ch.append(lambda tb=tb, xb=xb, xbk=xbk, b=b: S.dma("sp", lambda e: e.dma_start(out=xb, in_=x_in[tb * 128:(tb + 1) * 128, :]), f"xb{b}", writes=[xbk]))
                        ch.append(lambda tb=tb, c=c, okeys=okeys: S.op("act", lambda e: e.activation(out=sq4, in_=omix[:, tb, :], func=AF.Square, accum_out=ss[:, c:c + 1]),
                                                                       reads=okeys, writes=["sq4", ("ss", c)]))
                        ch.append(lambda c=c: S.op("dve", lambda e: e.tensor_scalar(out=rs[:, c:c + 1], in0=ss[:, c:c + 1], scalar1=1.0 / D, scalar2=EPS,
                                                                                    op0=ALU.mult, op1=ALU.add), reads=[("ss", c)], writes=[("rs", c)]))
                        ch.append(lambda c=c: S.op("act", lambda e: e.activation(out=rs[:, c:c + 1], in_=rs[:, c:c + 1], func=AF.Sqrt), reads=[("rs", c)], writes=[("rs", c)]))
                        ch.append(lambda c=c: S.op("dve", lambda e: e.reciprocal(out=rs[:, c:c + 1], in_=rs[:, c:c + 1]), reads=[("rs", c)], writes=[("rs", c)]))
                        ch.append(lambda tb=tb, c=c, okeys=okeys: S.op("dve", lambda e: e.scalar_tensor_tensor(out=omix[:, tb, :], in0=omix[:, tb, :], scalar=rs[:, c:c + 1], in1=gt1,
                                                                                                          op0=ALU.mult, op1=ALU.mult),
                                                                       reads=okeys + [("rs", c), "gt1"], writes=okeys))
                        ch.append(lambda tb=tb, xb=xb, xbk=xbk, okeys=okeys: S.op("dve", lambda e: e.tensor_add(out=xb, in0=xb, in1=omix[:, tb, :]), reads=okeys + [xbk], writes=[xbk]))
                        ch.append(lambda tb=tb, xb=xb, xbk=xbk, b=b: S.dma("sp", lambda e: e.dma_start(out=x1_d[tb * 128:(tb + 1) * 128, :], in_=xb), f"x1w{b}", reads=[xbk], writes=[("x1d", tb)]))
                        ch += norm_transpose_ops(xb, xbk, gt2, "gt2", xs4s[b], ("xs4", b), sq4, "sq4", h2T, "h2T", tb, 2 + c)
                        chains.append(ch)
                    pipeline(chains, 8)
                S.barrier()
                sY.close()

                with contextlib.ExitStack() as s5:
                    wg = [sbt(s5, f"wg{i}", [128, 16, 256], BF16) for i in range(2)]
                    wu = [sbt(s5, f"wu{i}", [128, 16, 256], BF16) for i in range(2)]
                    sg = [sbt(s5, f"sg{i}", [128, 512], F32) for i in range(2)]
                    gview = wg_d.rearrange("(kc p) n -> p kc n", p=128)
                    uview = wu_d.rearrange("(kc p) n -> p kc n", p=128)
                    h2keys = [("h2T", tb, half) for tb in range(NT) for half in range(2)]
                    n5 = 0
                    for hg2 in range(NHC // 2):
                        sl = hg2 % 2
                        S.dma("pool", lambda e, sl=sl, hg2=hg2: e.dma_start(out=wg[sl][:], in_=gview[:, :, hg2 * 256:(hg2 + 1) * 256]), f"wg{sl}", writes=[("wg", sl)])
                        S.dma("pool", lambda e, sl=sl, hg2=hg2: e.dma_start(out=wu[sl][:], in_=uview[:, :, hg2 * 256:(hg2 + 1) * 256]), f"wu{sl}", writes=[("wu", sl)])
                        for hh in range(2):
                            hc = hg2 * 2 + hh
                            for th in range(2):
                                bg = (n5 % 3) * 2
                                bu = bg + 1
                                sgs = n5 % 2
                                n5 += 1

                                def f(e, sl=sl, hh=hh, th=th, bg=bg, bu=bu):
                                    for kc in range(16):
                                        e.matmul(ps[bg][:], lhsT=wg[sl][:, kc, hh * 128:(hh + 1) * 128], rhs=h2T[:, kc, th * 512:(th + 1) * 512],
                                                 start=(kc == 0), stop=(kc == 15))
                                    for kc in range(16):
                                        ins = e.matmul(ps[bu][:], lhsT=wu[sl][:, kc, hh * 128:(hh + 1) * 128], rhs=h2T[:, kc, th * 512:(th + 1) * 512],
                                                       start=(kc == 0), stop=(kc == 15))
                                    return ins
                                S.op("pe", f, reads=[("wg", sl), ("wu", sl)] + [("h2T", tb, half) for tb in range(th * 4, th * 4 + 4) for half in range(2)],
                                     writes=[("ps", bg), ("ps", bu)])
                                S.op("act", lambda e, bg=bg, sgs=sgs: e.activation(out=sg[sgs][:], in_=ps[bg][:], func=AF.Silu),
                                     reads=[("ps", bg)], writes=[("sg", sgs)])
                                S.op("dve", lambda e, bu=bu, sgs=sgs, hc=hc, th=th: e.tensor_mul(out=aT[:, hc, th * 512:(th + 1) * 512], in0=sg[sgs][:], in1=ps[bu][:]),
                                     reads=[("sg", sgs), ("ps", bu)], writes=[("aT", hc, th)])
                S.barrier()

                with contextlib.ExitStack() as s6:
                    wdn = [sbt(s6, f"wd{i}", [128, 4, 512], BF16) for i in range(2)]
                    gt3 = sbt(s6, "gt3", [128, D], F32)[:]
                    xb6 = sbt(s6, "xb6", [128, D], F32)[:]
                    sq6 = sbt(s6, "sq6", [128, D], BF16)[:]
                    S.dma("sp", lambda e: e.dma_start(out=gt3, in_=gpo_d.partition_broadcast(128)), None, writes=["gt3"])
                    dview = wd_d.rearrange("(hc p) n -> p hc n", p=128)
                    n6 = 0
                    for dc in range(4):
                        for hq in range(11):
                            sl = n6 % 2
                            n6 += 1
                            S.dma("pool", lambda e, sl=sl, hq=hq, dc=dc: e.dma_start(out=wdn[sl][:], in_=dview[:, hq * 4:(hq + 1) * 4, dc * 512:(dc + 1) * 512]),
                                  f"wd{sl}", writes=[("wd", sl)])
                            for tb in range(NT):
                                def f(e, sl=sl, hq=hq, tb=tb):
                                    for j in range(4):
                                        hc = hq * 4 + j
                                        ins = e.matmul(ps[tb][:], lhsT=aT[:, hc, tb * 128:(tb + 1) * 128], rhs=wdn[sl][:, j, :],
                                                       start=(hc == 0), stop=(hc == NHC - 1))
                                    return ins
                                S.op("pe", f, reads=[("wd", sl)] + [("aT", hq * 4 + j, tb // 4) for j in range(4)], writes=[("ps", tb)])
                        for tb in range(NT):
                            S.op("act", lambda e, tb=tb, dc=dc: e.activation(out=o_all[:, tb, dc * 512:(dc + 1) * 512], in_=ps[tb][:], func=AF.Copy),
                                 reads=[("ps", tb)], writes=[("o", tb, dc)])
                    xb6s = [xb6, sbt(s6, "xb6_b", [128, D], F32)[:]]
                    if do_A:
                        sA0 = contextlib.ExitStack()
                        gtA = sbt(sA0, "gtA", [128, D], F32)[:]
                        xsAs = [sbt(sA0, f"xsA{i}", [128, D], BF16)[:] for i in range(2)]
                        S.dma("sp", lambda e: e.dma_start(out=gtA, in_=gpre_d.partition_broadcast(128)), None, writes=["gtA"])
                    chains = []
                    for tb in range(NT):
                        c = tb % 2
                        b = tb % 2
                        xb, xbk = xb6s[b], ("xb6", b)
                        okeys = [("o", tb, dc) for dc in range(4)]
                        ch = []
                        ch.append(lambda tb=tb, xb=xb, xbk=xbk, b=b: S.dma("sp", lambda e: e.dma_start(out=xb, in_=x1_d[tb * 128:(tb + 1) * 128, :]), f"xb{b}", reads=[("x1d", tb)], writes=[xbk]))
                        ch.append(lambda tb=tb, c=c, okeys=okeys: S.op("act", lambda e: e.activation(out=sq6, in_=o_all[:, tb, :], func=AF.Square, accum_out=ss[:, c:c + 1]),
                                                                       reads=okeys, writes=["sq6", ("ss", c)]))
                        ch.append(lambda c=c: S.op("dve", lambda e: e.tensor_scalar(out=rs[:, c:c + 1], in0=ss[:, c:c + 1], scalar1=1.0 / D, scalar2=EPS,
                                                                                    op0=ALU.mult, op1=ALU.add), reads=[("ss", c)], writes=[("rs", c)]))
                        ch.append(lambda c=c: S.op("act", lambda e: e.activation(out=rs[:, c:c + 1], in_=rs[:, c:c + 1], func=AF.Sqrt), reads=[("rs", c)], writes=[("rs", c)]))
                        ch.append(lambda c=c: S.op("dve", lambda e: e.reciprocal(out=rs[:, c:c + 1], in_=rs[:, c:c + 1]), reads=[("rs", c)], writes=[("rs", c)]))
                        ch.append(lambda tb=tb, c=c, okeys=okeys: S.op("dve", lambda e: e.scalar_tensor_tensor(out=o_all[:, tb, :], in0=o_all[:, tb, :], scalar=rs[:, c:c + 1], in1=gt3,
                                                                                                          op0=ALU.mult, op1=ALU.mult),
                                                                       reads=okeys + [("rs", c), "gt3"], writes=okeys))
                        ch.append(lambda tb=tb, xb=xb, xbk=xbk, okeys=okeys: S.op("dve", lambda e: e.tensor_add(out=xb, in0=xb, in1=o_all[:, tb, :]), reads=okeys + [xbk], writes=[xbk]))
                        ch.append(lambda tb=tb, xb=xb, xbk=xbk, b=b: S.dma("sp", lambda e: e.dma_start(out=x_out[tb * 128:(tb + 1) * 128, :], in_=xb), f"xow{b}", reads=[xbk], writes=[("xout", tb)]))
                        if do_A:
                            ch += norm_transpose_ops(xb, xbk, gtA, "gtA", xsAs[b], ("xsA", b), sq6, "sq6", hT, "hT", tb, 4 + c)
                        chains.append(ch)
                    pipeline(chains, 8)
                    if do_A:
                        S.barrier()
                        sA0.close()
            S.barrier()

        if do_A:
            with contextlib.ExitStack() as sA:
                if not do_B:
                    a1 = Arena(R1, R1B)
                    gtA = a1.take([D], F32)
                    xsA = a1.take([D], BF16)
                    xbA = a1.take([D], F32)
                    sqA = a1.take([D], BF16)
                    S.dma("sp", lambda e: e.dma_start(out=gtA, in_=gpre_d.partition_broadcast(128)), None, writes=["gtA"])
                    for tb in range(NT):
                        S.dma("sp", lambda e, tb=tb: e.dma_start(out=xbA, in_=x_in[tb * 128:(tb + 1) * 128, :]), "xb", writes=["xbA"])
                        norm_transpose(xbA, "xbA", gtA, "gtA", xsA, "xsA", sqA, "sqA", hT, "hT", tb, 4 + tb % 2)
                    S.barrier()
                a1 = Arena(R1, R1B)
                qko = a1.take([16, T], BF16)
                asb = a1.take([4, T], F32)
                hgo = a1.take([4, T], F32)
                a2 = Arena(R2, R2B, start=32768)
                wi = [a2.take([16, 512], BF16) for i in range(2)]
                upo = a2.take([4, T], F32)
                ropec = a2.take([T], F32)
                ropes = a2.take([T], F32)
                pmat = sbt(sA, "pmat_sb", [128, 128], F32)
                sgA = [sbt(sA, f"sgA{i}", [128, 512], F32) for i in range(2)]
                qf = [sbt(sA, f"qf{i}", [128, 512], F32) for i in range(2)]
                t1 = [sbt(sA, f"t1{i}", [128, 512], F32) for i in range(2)]
                t2 = [sbt(sA, f"t2{i}", [128, 512], F32) for i in range(2)]
                vsb = sbt(sA, "vsb", [128, NT, 16, 65], BF16)
                S.dma("sp", lambda e: e.dma_start(out=ropec, in_=ropec_d), None, writes=["ropec"])
                S.dma("sp", lambda e: e.dma_start(out=ropes, in_=ropes_d), None, writes=["ropes"])
                S.dma("sp", lambda e: e.dma_start(out=pmat[:], in_=pmat_d), None, writes=["pmat"])
                S.op("dve", lambda e: e.memset(vsb[:], 1.0), writes=["vsb_ones"])
                hkeys = [("hT", tb, half) for tb in range(NT) for half in range(2)]
                wiv = win_d.rearrange("(kc p) n -> p kc n", p=128)
                nmm = 0
                nr = 0
                def load_wi(s_):
                    sl = s_ % 2
                    S.dma("pool", lambda e: e.dma_start(out=wi[sl], in_=wiv[:, :, s_ * 512:(s_ + 1) * 512]), f"wi{sl}", writes=[("wi", sl)])
                load_wi(0)
                for s_ in range(9):
                    sl = s_ % 2
                    if s_ + 1 < 9:
                        load_wi(s_ + 1)
                    if s_ in (6, 7):
                        for tb in range(NT):
                            bank = nmm % 4
                            nmm += 1

                            def f(e, sl=sl, tb=tb, bank=bank):
                                for kc in range(16):
                                    ins = e.matmul(ps[bank][:], lhsT=hT[:, kc, tb * 128:(tb + 1) * 128], rhs=wi[sl][:, kc, :], start=(kc == 0), stop=(kc == 15))
                                return ins
                            S.op("pe", f, reads=[("wi", sl)] + hkeys, writes=[("ps", bank)])
                            h0 = (s_ - 6) * 8
                            S.op("act", lambda e, tb=tb, bank=bank, h0=h0: e.activation(out=vsb[:, tb, h0:h0 + 8, 0:64],
                                                                                        in_=ps[bank][:].rearrange("p (h e) -> p h e", h=8), func=AF.Copy),
                                 reads=[("ps", bank), "vsb_ones"], writes=[("vsb", tb, s_)])
                        continue
                    for oc in range(4):
                        for th in range(2):
                            tsl = slice(th * 512, (th + 1) * 512)
                            bank = nmm % 4
                            nmm += 1

                            def f(e, sl=sl, oc=oc, th=th, bank=bank):
                                for kc in range(16):
                                    ins = e.matmul(ps[bank][:], lhsT=wi[sl][:, kc, oc * 128:(oc + 1) * 128], rhs=hT[:, kc, th * 512:(th + 1) * 512],
                                                   start=(kc == 0), stop=(kc == 15))
                                return ins
                            S.op("pe", f, reads=[("wi", sl)] + hkeys, writes=[("ps", bank)])
                            if s_ == 0:
                                S.op("act", lambda e, oc=oc, tsl=tsl, bank=bank: e.activation(out=asb[:, oc, tsl], in_=ps[bank][:], func=AF.Copy),
                                     reads=[("ps", bank)], writes=[("asb", oc, th)])
                            elif s_ == 1:
                                k2 = nmm % 2
                                S.op("act", lambda e, bank=bank, k2=k2: e.activation(out=sgA[k2][:], in_=ps[bank][:], func=AF.Sigmoid),
                                     reads=[("ps", bank)], writes=[("sgA", k2)])
                                S.op("dve", lambda e, oc=oc, tsl=tsl, k2=k2: e.tensor_mul(out=hgo[:, oc, tsl], in0=asb[:, oc, tsl], in1=sgA[k2][:]),
                                     reads=[("sgA", k2), ("asb", oc, th)], writes=[("hgo", oc, th)])
                            elif s_ == 8:
                                S.op("act", lambda e, oc=oc, tsl=tsl, bank=bank: e.activation(out=upo[:, oc, tsl], in_=ps[bank][:], func=AF.Copy),
                                     reads=[("ps", bank)], writes=[("upo", oc, th)])
                            else:
                                ch = (s_ - 2) * 4 + oc
                                k2 = nr % 2
                                pb = 4 + k2
                                nr += 1
                                S.op("act", lambda e, bank=bank, k2=k2: e.activation(out=qf[k2][:], in_=ps[bank][:], func=AF.Copy),
                                     reads=[("ps", bank)], writes=[("qf", k2)])
                                S.op("pe", lambda e, k2=k2, pb=pb: e.matmul(ps[pb][:], lhsT=pmat[:], rhs=qf[k2][:], start=True, stop=True),
                                     reads=[("qf", k2), "pmat"], writes=[("ps", pb)])
                                S.op("dve", lambda e, k2=k2, pb=pb, tsl=tsl: e.tensor_mul(out=t1[k2][:], in0=ps[pb][:], in1=ropes[:, tsl]),
                                     reads=[("ps", pb), "ropes"], writes=[("t1", k2)])
                                S.op("dve", lambda e, k2=k2, tsl=tsl: e.tensor_mul(out=t2[k2][:], in0=qf[k2][:], in1=ropec[:, tsl]),
                                     reads=[("qf", k2), "ropec"], writes=[("t2", k2)])
                                S.op("dve", lambda e, k2=k2, ch=ch, tsl=tsl: e.tensor_add(out=qko[:, ch, tsl], in0=t1[k2][:], in1=t2[k2][:]),
                                     reads=[("t1", k2), ("t2", k2)], writes=[("qko", ch, th)])
                S.dma("sp", lambda e: e.dma_start(out=hg_o.rearrange("c p t -> p c t"), in_=hgo), "outA",
                      reads=[("hgo", c, th) for c in range(4) for th in range(2)], writes=["hg_o"])
                S.dma("sp", lambda e: e.dma_start(out=up_o.rearrange("c p t -> p c t"), in_=upo), "outA",
                      reads=[("upo", c, th) for c in range(4) for th in range(2)], writes=["up_o"])
                S.dma("sp", lambda e: e.dma_start(out=qT_o.rearrange("c p t -> p c t"), in_=qko[:, 0:8, :]), "outA",
                      reads=[("qko", c, th) for c in range(8) for th in range(2)], writes=["qT_o"])
                S.dma("sp", lambda e: e.dma_start(out=kT_o.rearrange("c p t -> p c t"), in_=qko[:, 8:16, :]), "outA",
                      reads=[("qko", c, th) for c in range(8, 16) for th in range(2)], writes=["kT_o"])
                S.dma("sp", lambda e: e.dma_start(out=v_o.rearrange("t p f -> p t f"), in_=vsb[:].rearrange("p t h e -> p t (h e)")), "outA",
                      reads=[("vsb", tb, s_) for tb in range(NT) for s_ in (6, 7)], writes=["v_o"])
        S.barrier()
        S.run()
    return nc


_PROGS = {}


def _prog(do_B, do_A):
    key = (do_B, do_A)
    if key not in _PROGS:
        _PROGS[key] = build_program(do_B, do_A)
    return _PROGS[key]


def _const_tables():
    o = np.arange(-2 * 128 - 127, 2 * 128 + 128)
    w = ((np.abs(o) <= 64).astype(np.float32) + ((o % 4 == 0) & (np.abs(o) <= 256)))
    wmap = dict(zip(o.tolist(), w.tolist()))
    k = np.arange(128)[:, None, None]
    rel = np.array([r for _ in range(2) for r in range(-2, 3)])[None, :, None]
    q = np.arange(128)[None, None, :]
    off = rel * 128 + k - q
    masks = np.vectorize(wmap.get)(off).astype(np.float32).reshape(128, 10 * 128).astype(ml_dtypes.bfloat16)
    kk = np.arange(128)[:, None]
    jj = np.arange(64)[None, :]
    mA = (kk >= jj).astype(np.float32)
    mB = ((kk <= jj) & (kk < 64)).astype(np.float32)
    fmask = np.stack([np.tile(mA, (1, 4)), np.tile(mB, (1, 4))], axis=1).astype(ml_dtypes.bfloat16)
    pm = np.zeros((128, 128), np.float32)
    for hh in range(2):
        for m in range(8):
            pm[hh * 64 + m + 8, hh * 64 + m] = -1.0
            pm[hh * 64 + m, hh * 64 + m + 8] = 1.0
    return (masks, fmask), pm


def _rope_tables(c):
    pos = (np.arange(T) + c * T).astype(np.float32)
    inv = (np.float32(500000.0) ** (-np.arange(0, 16, 2, dtype=np.float32) / np.float32(16))).astype(np.float32)
    ang = (pos[:, None] * inv[None, :]).astype(np.float32)
    cs, sn = np.cos(ang).astype(np.float32), np.sin(ang).astype(np.float32)
    C = np.ones((128, T), np.float32)
    Sn = np.zeros((128, T), np.float32)
    for hh in range(2):
        for i in range(16):
            C[hh * 64 + i] = cs[:, i % 8]
            Sn[hh * 64 + i] = sn[:, i % 8]
    return C, Sn


def _corr_table(c):
    out = np.ones((4, 16), np.float32)
    idx = np.concatenate([np.arange(8), np.arange(T - 8, T)]) + c * T
    for gi, w in enumerate((2, 4, 8, 16)):
        lo = np.clip(idx - w // 2, 0, S_LEN)
        hi = np.clip(idx + w - w // 2, 0, S_LEN)
        out[gi] = w / (hi - lo).astype(np.float32)
    return np.ascontiguousarray(np.broadcast_to(out[None], (128, 4, 16)))


def _colmajor(v, n):
    return np.ascontiguousarray(v.reshape(n, 128).T)


def _run_step(step, xs, Aout, P, consts):
    f32 = np.float32
    masks, pm, ident, ropes, corrs = consts
    do_B = step > 0
    do_A = step < DEPTH
    lb, la = step - 1, step
    nc = _prog(do_B, do_A)
    maps = []
    if do_B:
        kT = np.concatenate([Aout[c]["kT_o"] for c in range(NCORES)], axis=2)
        kT = np.pad(kT, ((0, 0), (0, 0), (T, T)))
        vv = np.concatenate([Aout[c]["v_o"].reshape(T, 8, 130) for c in range(NCORES)], axis=0)
        vv = np.pad(vv, ((T, T), (0, 0), (0, 0)))
        hgf = np.pad(np.concatenate([Aout[c]["hg_o"] for c in range(NCORES)], axis=2), ((0, 0), (0, 0), (15, 15)))
        upf = np.pad(np.concatenate([Aout[c]["up_o"] for c in range(NCORES)], axis=2), ((0, 0), (0, 0), (8, 8)))
        cw = np.ascontiguousarray(np.asarray(P["conv_w"][lb], f32).T.reshape(4, 128, 31).transpose(1, 0, 2))
        vec4 = np.stack([_colmajor(np.asarray(P[k][lb], f32), 4) for k in ("conv_b", "conv_ln_g", "conv_ln_b", "pool_scale")], axis=1)
        gm = np.asarray(P["g_mix"][lb], f32)
    for c in range(NCORES):
        m = {"ident": ident, "x_in": xs[c]}
        if do_B:
            seg = vv[c * T:c * T + 3 * T]
            vh = seg.reshape(24, 128, 8, 130).transpose(2, 1, 0, 3)
            sc = seg.reshape(192, 16, 8, 130)
            kseg = kT[:, :, c * T:c * T + 3 * T]
            kTc = np.ascontiguousarray(kseg.reshape(8, 128, 192, 16).transpose(0, 1, 3, 2)).reshape(8, 128, 16 * 192)
            qTc = np.ascontiguousarray(np.asarray(Aout[c]["qT_o"]).reshape(8, 128, 64, 16).transpose(0, 1, 3, 2)).reshape(8, 128, 16 * 64)
            vc = np.zeros((8, 128, 16, 2, 130), seg.dtype)
            vc[:, :, :, 0, :] = sc[0:128].transpose(2, 0, 1, 3)
            vc[:, 0:64, :, 1, :] = sc[128:192].transpose(2, 0, 1, 3)
            m.update({
                "qT": Aout[c]["qT_o"],
                "kTh": np.ascontiguousarray(kT[:, :, c * T:c * T + 3 * T]),
                "vh": np.ascontiguousarray(vh),
                "hgh": np.ascontiguousarray(hgf[:, :, c * T:c * T + T + 30]),
                "uh": np.ascontiguousarray(upf[:, :, c * T:c * T + T + 16]),
                "masks": masks[0], "fmask": masks[1], "vc": vc, "kTc": kTc, "qTc": qTc, "corr": corrs[c], "cw": cw, "vec4": np.ascontiguousarray(vec4),
                "gmixT": _colmajor(gm, 16), "gmixB": np.ascontiguousarray(gm[None, 512:1536]),
                "poolw": np.asarray(P["pool_w"][lb], f32), "w_out": np.asarray(P["w_out"][lb], f32),
                "g_post_mix": np.asarray(P["g_post_mix"][lb], f32)[None], "g_pre_ffn": np.asarray(P["g_pre_ffn"][lb], f32)[None],
                "w_gate": np.asarray(P["w_gate"][lb], f32), "w_up": np.asarray(P["w_up"][lb], f32),
                "w_down": np.asarray(P["w_down"][lb], f32), "g_post_ffn": np.asarray(P["g_post_ffn"][lb], f32)[None],
            })
        if do_A:
            m.update({"w_in": np.asarray(P["w_in"][la], f32), "g_pre_mix": np.asarray(P["g_pre_mix"][la], f32)[None],
                      "rope_c": ropes[c][0], "rope_s": ropes[c][1], "pmat": pm})
        maps.append(m)
    res = run_bass_kernel_spmd(nc, maps, core_ids=list(range(NCORES)))
    Aout = res.results
    if do_B:
        xs = [np.asarray(Aout[c]["x_out"], f32) for c in range(NCORES)]
    return xs, Aout


def _consts():
    masks, pm = _const_tables()
    ident = np.eye(128, dtype=np.float32)
    ropes = [_rope_tables(c) for c in range(NCORES)]
    corrs = [_corr_table(c) for c in range(NCORES)]
    return masks, pm, ident, ropes, corrs


def kernel(x, w_in, conv_w, conv_b, conv_ln_g, conv_ln_b, pool_w, pool_scale, g_mix,
           w_out, g_pre_mix, g_post_mix, g_pre_ffn, g_post_ffn, w_gate, w_up, w_down):
    P = dict(w_in=w_in, conv_w=conv_w, conv_b=conv_b, conv_ln_g=conv_ln_g, conv_ln_b=conv_ln_b, pool_w=pool_w,
             pool_scale=pool_scale, g_mix=g_mix, w_out=w_out, g_pre_mix=g_pre_mix, g_post_mix=g_post_mix,
             g_pre_ffn=g_pre_ffn, g_post_ffn=g_post_ffn, w_gate=w_gate, w_up=w_up, w_down=w_down)
    x = np.asarray(x, np.float32)
    consts = _consts()
    xs = [np.ascontiguousarray(x[0, c * T:(c + 1) * T]) for c in range(NCORES)]
    Aout = None
    for step in range(DEPTH + 1):
        xs, Aout = _run_step(step, xs, Aout, P, consts)
    return np.concatenate(xs, axis=0)[None].astype(np.float32)
```

```python
import contextlib
import numpy as np
import ml_dtypes
import concourse.bass as bass
import concourse.mybir as mybir
from concourse.bass_utils import run_bass_kernel_spmd

F32 = mybir.dt.float32
BF16 = mybir.dt.bfloat16
AF = mybir.ActivationFunctionType
ALU = mybir.AluOpType

NCORES = 8
S_LEN = 8192
T = 1024
NT = 8
D = 2048
DEPTH = 4
INW = 4608
FF = 5632
NHC = FF // 128
EPS = 1e-6
ENGS = ("pe", "act", "dve", "pool", "sp")
import os
DEBUG = bool(os.environ.get("KDEBUG"))


class Sched:
    def __init__(self, nc):
        self.nc = nc
        self.ops = {e: [] for e in ENGS}
        self.cnt = {}
        self.waited = {e: {} for e in ENGS}
        self.last_w = {}
        self.readers = {}
        self.semnames = set(ENGS)
        self.sem_h = {}

    def _deps(self, eng, reads, writes):
        need = {}

        def add(k, v):
            if k == "pe" and eng == "pe":
                return
            if need.get(k, 0) < v:
                need[k] = v

        for k in reads:
            t = self.last_w.get(k)
            if t is not None:
                add(*t)
        for k in writes:
            t = self.last_w.get(k)
            if t is not None:
                add(*t)
            for kk, vv in self.readers.get(k, {}).items():
                add(kk, vv)
        return self._filter(eng, need)

    def _filter(self, eng, need):
        out = []
        w = self.waited[eng]
        for k, v in need.items():
            if w.get(k, 0) < v:
                w[k] = v
                out.append((k, v))
        return out

    def _track(self, tok, reads, writes):
        for k in writes:
            self.last_w[k] = tok
            self.readers[k] = {}
        for k in reads:
            r = self.readers.setdefault(k, {})
            if r.get(tok[0], 0) < tok[1]:
                r[tok[0]] = tok[1]

    def op(self, eng, fn, reads=(), writes=()):
        waits = self._deps(eng, reads, writes)
        self.cnt[eng] = self.cnt.get(eng, 0) + 1
        tok = (eng, self.cnt[eng])
        self._track(tok, reads, writes)
        self.ops[eng].append((waits, fn, eng, 1))
        return tok

    def dma(self, eng, fn, sem, reads=(), writes=()):
        if sem is None:
            self.nuniq = getattr(self, "nuniq", 0) + 1
            sem = f"u{self.nuniq}"
        self.semnames.add(sem)
        waits = self._deps(eng, reads, writes)
        self.cnt[sem] = self.cnt.get(sem, 0) + 16
        tok = (sem, self.cnt[sem])
        self._track(tok, reads, writes)
        self.ops[eng].append((waits, fn, sem, 16))
        return tok

    def barrier(self):
        for e in ENGS:
            waits = self._filter(e, dict(self.cnt))
            if waits:
                self.ops[e].append((waits, None, None, 0))

    def run(self):
        nc = self.nc
        with contextlib.ExitStack() as st:
            for name in sorted(self.semnames):
                self.sem_h[name] = st.enter_context(nc.semaphore("s_" + name))
            block = st.enter_context(nc.Block())
            sem_h = self.sem_h

            def replay(e, lst):
                for waits, fn, sem, inc in lst:
                    for k, v in waits:
                        e.wait_ge(sem_h[k], v)
                    if fn is not None:
                        fn(e).then_inc(sem_h[sem], inc)

            @block.tensor
            def _(e):
                replay(e, self.ops["pe"])

            @block.scalar
            def _(e):
                replay(e, self.ops["act"])

            @block.vector
            def _(e):
                replay(e, self.ops["dve"])

            @block.gpsimd
            def _(e):
                replay(e, self.ops["pool"])

            @block.sync
            def _(e):
                replay(e, self.ops["sp"])


class Arena:
    def __init__(self, t, nbytes, start=0):
        self.t, self.n, self.off = t, nbytes, start

    def take(self, shape, dt):
        esz = 2 if dt == BF16 else 4
        n = esz
        for d in shape:
            n *= d
        n = (n + 31) // 32 * 32
        assert self.off + n <= self.n, (self.off, n, self.n)
        ap = self.t[:, self.off // 4:(self.off + n) // 4]
        if dt != F32:
            ap = ap.bitcast(dt)
        tot = 1
        for d in shape:
            tot *= d
        ap = ap[:, 0:tot]
        if len(shape) == 2:
            ap = ap.rearrange("p (a b) -> p a b", a=shape[0])
        elif len(shape) == 3:
            ap = ap.rearrange("p (a b c) -> p a b c", a=shape[0], b=shape[1])
        self.off += n
        return ap


def build_program(do_B, do_A):
    nc = bass.Bass("TRN2", target_bir_lowering=False)
    S = Sched(nc)

    def din(name, shape, dt=F32):
        return nc.dram_tensor(name, list(shape), dt, kind="ExternalInput").ap()

    def dout(name, shape, dt=F32):
        return nc.dram_tensor(name, list(shape), dt, kind="ExternalOutput").ap()

    ident_d = din("ident", [128, 128])
    x_in = din("x_in", [T, D])
    if do_B:
        qT_d = din("qT", [8, 128, T], BF16)
        kT_d = din("kTh", [8, 128, 3 * T], BF16)
        v_d = din("vh", [8, 128, 24, 130], BF16)
        hg_d = din("hgh", [4, 128, T + 30])
        up_d = din("uh", [4, 128, T + 16])
        mask_d = din("masks", [128, 10 * 128], BF16)
        fmask_d = din("fmask", [128, 2, 256], BF16)
        vc_d = din("vc", [8, 128, 16, 2, 130], BF16)
        kTc_d = din("kTc", [8, 128, 16 * 192], BF16)
        qTc_d = din("qTc", [8, 128, 16 * 64], BF16)
        far_d = nc.dram_tensor("far_scratch", [8, T, 130], F32).ap()
        corr_d = din("corr", [128, 4, 16])
        cw_d = din("cw", [128, 4, 31])
        vec4_d = din("vec4", [128, 4, 4])
        gmixT_d = din("gmixT", [128, 16])
        gmixB_d = din("gmixB", [1, 1024])
        poolw_d = din("poolw", [4, 128, 128])
        wout_d = din("w_out", [D, D])
        gpm_d = din("g_post_mix", [1, D])
        gpf_d = din("g_pre_ffn", [1, D])
        wg_d = din("w_gate", [D, FF])
        wu_d = din("w_up", [D, FF])
        wd_d = din("w_down", [FF, D])
        gpo_d = din("g_post_ffn", [1, D])
        x_out = dout("x_out", [T, D])
        if DEBUG:
            x1_d = dout("x1_dbg", [T, D])
            yT_dbg = dout("yT_dbg", [16, 128, T], BF16)
        else:
            x1_d = nc.dram_tensor("x1_scratch", [T, D], F32).ap()
    if do_A:
        win_d = din("w_in", [D, INW])
        gpre_d = din("g_pre_mix", [1, D])
        ropec_d = din("rope_c", [128, T])
        ropes_d = din("rope_s", [128, T])
        pmat_d = din("pmat", [128, 128])
        qT_o = dout("qT_o", [8, 128, T], BF16)
        kT_o = dout("kT_o", [8, 128, T], BF16)
        v_o = dout("v_o", [NT, 128, 16 * 65], BF16)
        hg_o = dout("hg_o", [4, 128, T])
        up_o = dout("up_o", [4, 128, T])

    with contextlib.ExitStack() as top:
        def sbt(st, name, shape, dt):
            return st.enter_context(nc.sbuf_tensor("sb_" + name, list(shape), dt))

        ps = [top.enter_context(nc.psum_tensor(f"ps{i}", [128, 512], F32)) for i in range(8)]
        ident = sbt(top, "ident_sb", [128, 128], BF16)
        onesf = sbt(top, "onesf", [128, 128], F32)
        R1 = sbt(top, "R1", [128, 16384], F32)
        R2 = sbt(top, "R2", [128, 22528], F32)
        R1B, R2B = 65536, 90112
        h2T = Arena(R1, R1B).take([16, T], BF16)
        o_all = Arena(R1, R1B).take([NT, D], F32)
        aT = Arena(R2, R2B).take([NHC, T], BF16)
        hT = Arena(R2, R2B).take([16, T], BF16)
        ss = sbt(top, "ss", [128, 16], F32)
        rs = sbt(top, "rs", [128, 16], F32)
        ptr = ps[7][:].bitcast(BF16).rearrange("p (j t) -> p j t", j=8)

        S.dma("pool", lambda e: e.dma_start(out=ident[:], in_=ident_d), None, writes=["ident"])
        S.op("dve", lambda e: e.memset(onesf[:], 1.0 / 512.0), writes=["onesf"])

        def norm_transpose_ops(xblk, xkey, gt, gkey, xs, xskey, sq, sqkey, dstT, dkey, tb, slot):
            c = slot
            ops = []
            ops.append(lambda: S.op("act", lambda e: e.activation(out=sq, in_=xblk, func=AF.Square, accum_out=ss[:, c:c + 1]),
                                    reads=[xkey], writes=[sqkey, ("ss", c)]))
            ops.append(lambda: S.op("dve", lambda e: e.tensor_scalar(out=rs[:, c:c + 1], in0=ss[:, c:c + 1], scalar1=1.0 / D, scalar2=EPS,
                                                                     op0=ALU.mult, op1=ALU.add), reads=[("ss", c)], writes=[("rs", c)]))
            ops.append(lambda: S.op("act", lambda e: e.activation(out=rs[:, c:c + 1], in_=rs[:, c:c + 1], func=AF.Sqrt),
                                    reads=[("rs", c)], writes=[("rs", c)]))
            ops.append(lambda: S.op("dve", lambda e: e.reciprocal(out=rs[:, c:c + 1], in_=rs[:, c:c + 1]), reads=[("rs", c)], writes=[("rs", c)]))
            ops.append(lambda: S.op("dve", lambda e: e.scalar_tensor_tensor(out=xs, in0=xblk, scalar=rs[:, c:c + 1], in1=gt,
                                                                            op0=ALU.mult, op1=ALU.mult),
                                    reads=[xkey, gkey, ("rs", c)], writes=[xskey]))
            for half in range(2):
                def tr(e, half=half):
                    for j in range(8):
                        kc = half * 8 + j
                        i = e.transpose(out=ptr[:, j, :], in_=xs[:, kc * 128:(kc + 1) * 128], identity=ident[:])
                    return i
                ops.append(lambda tr=tr: S.op("pe", tr, reads=[xskey, "ident"], writes=[("ps", 7)]))
                ops.append(lambda half=half: S.op("act", lambda e: e.activation(out=dstT[:, half * 8:(half + 1) * 8, tb * 128:(tb + 1) * 128],
                                                                               in_=ptr[:], func=AF.Copy),
                                                  reads=[("ps", 7)], writes=[(dkey, tb, half)]))
            return ops

        def norm_transpose(*a):
            for o in norm_transpose_ops(*a):
                o()

        def pipeline(chains, stagger):
            n = max(len(c) for c in chains) + stagger * (len(chains) - 1)
            for step in range(n):
                for i, ch in enumerate(chains):
                    k = step - i * stagger
                    if 0 <= k < len(ch):
                        ch[k]()

        if do_B:
            with contextlib.ExitStack() as sB:
                gmixT = sbt(sB, "gmixT", [128, 16], F32)
                vec4 = sbt(sB, "vec4", [128, 4, 4], F32)
                sY = contextlib.ExitStack()
                yT = sbt(sY, "yT", [128, 16, T], BF16)
                a1u = Arena(R1, R1B, start=32768)
                wo = [a1u.take([16, 512], BF16) for i in range(2)]
                wview = wout_d.rearrange("(kc p) n -> p kc n", p=128)
                S.dma("sp", lambda e: e.dma_start(out=gmixT[:], in_=gmixT_d), None, writes=["gmixT"])
                S.dma("sp", lambda e: e.dma_start(out=vec4[:], in_=vec4_d), None, writes=["vec4"])

                with contextlib.ExitStack() as s1:
                    a2 = Arena(R2, R2B)
                    yb = a2.take([NT, 1024], F32)
                    hg = a2.take([4, T + 30], F32)
                    acc = a2.take([4, T], F32)
                    uh = a2.take([4, T + 16], F32)
                    masks = a2.take([10 * 128], BF16)
                    ybs = a2.take([1024], BF16)
                    a1 = Arena(R1, R1B)
                    pw = a1.take([3, T + 16], F32)
                    pooled = a1.take([4, T], BF16)
                    meansb = a1.take([512], F32)
                    rstdsb = a1.take([512], F32)
                    tmpA = a1.take([512], F32)
                    tmpB = a1.take([512], F32)
                    sqb = a1.take([1024], BF16)
                    assert a1.off <= 32768
                    gmixB = a1.take([1024], F32)
                    qkv_v = [a1.take([24, 130], BF16) for i in range(2)]
                    qkv_q = [a1.take([T], BF16) for i in range(2)]
                    qkv_k = [a1.take([3 * T], BF16) for i in range(2)]
                    aF = Arena(R2, R2B)
                    vcs = [aF.take([16, 2, 130], BF16) for i in range(2)]
                    farsb = aF.take([16, 130], F32)
                    pTA = [aF.take([256], BF16) for i in range(2)]
                    pTB = [aF.take([256], BF16) for i in range(2)]
                    fmask = aF.take([2, 256], BF16)
                    assert aF.off <= 32768
                    farnat = [sbt(s1, f"farnat{i}", [128, NT, 130], F32) for i in range(2)]
                    totsb = [sbt(s1, f"totsb{i}", [128, 2, 65], F32) for i in range(2)]
                    ya = acc
                    yc = hg
                    cw = sbt(s1, "cw", [128, 4, 31], F32)
                    corr = sbt(s1, "corr", [128, 4, 16], F32)
                    poolw = sbt(s1, "poolw", [128, 4, 128], BF16)
                    pT = [sbt(s1, f"pT{i}", [128, 512], BF16) for i in range(6)]
                    rec = sbt(s1, "rec", [128, 2, 2], F32)

                    S.dma("sp", lambda e: e.dma_start(out=hg, in_=hg_d.rearrange("c p t -> p c t")), None, writes=["hg"])
                    S.dma("sp", lambda e: e.dma_start(out=uh, in_=up_d.rearrange("c p t -> p c t")), None, writes=["uh"])
                    S.dma("sp", lambda e: e.dma_start(out=cw[:], in_=cw_d), None, writes=["cw"])
                    S.dma("sp", lambda e: e.dma_start(out=corr[:], in_=corr_d), None, writes=["corr"])
                    S.dma("sp", lambda e: e.dma_start(out=masks, in_=mask_d), None, writes=["masks"])
                    S.dma("sp", lambda e: e.dma_start(out=gmixB, in_=gmixB_d.partition_broadcast(128)), None, writes=["gmixB"])
                    S.dma("pool", lambda e: e.dma_start(out=poolw[:], in_=poolw_d.rearrange("g c d -> c g d")), None, writes=["poolw"])


                    bg_ops = []
                    for c in range(4):
                        bg_ops.append(lambda c=c: S.op("dve", lambda e: e.tensor_scalar(out=acc[:, c, :], in0=hg[:, c, 0:T], scalar1=cw[:, c, 0:1],
                                                                                      scalar2=vec4[:, 0, c:c + 1], op0=ALU.mult, op1=ALU.add),
                                                       reads=["hg", "cw", "vec4"], writes=[("acc", c)]))
                        for j in range(1, 31):
                            bg_ops.append(lambda c=c, j=j: S.op("dve", lambda e: e.scalar_tensor_tensor(
                                out=acc[:, c, :], in0=hg[:, c, j:j + T], scalar=cw[:, c, j:j + 1], in1=acc[:, c, :], op0=ALU.mult, op1=ALU.add),
                                reads=["hg", "cw", ("acc", c)], writes=[("acc", c)]))
                    for gi in range(4):
                        w = 2 << gi
                        src = uh[:, gi, :]
                        L = T + 16
                        step = 1
                        lvl = 0
                        while step < w:
                            L2 = L - step
                            dst = pw[:, lvl % 3, :]
                            S.op("pool", lambda e, src=src, dst=dst, L2=L2, step=step: e.tensor_add(
                                out=dst[:, 0:L2], in0=src[:, 0:L2], in1=src[:, step:step + L2]),
                                reads=["uh", ("pw", (lvl + 2) % 3)], writes=[("pw", lvl % 3)])
                            src = dst
                            L = L2
                            step *= 2
                            lvl += 1
                        off = 8 - w // 2
                        win = src[:, off:off + T]
                        lastkey = ("pw", (lvl - 1) % 3)
                        S.op("pool", lambda e, win=win, gi=gi: e.tensor_mul(out=win[:, 0:8], in0=win[:, 0:8], in1=corr[:, gi, 0:8]),
                             reads=[lastkey, "corr"], writes=[lastkey])
                        S.op("pool", lambda e, win=win, gi=gi: e.tensor_mul(out=win[:, T - 8:T], in0=win[:, T - 8:T], in1=corr[:, gi, 8:16]),
                             reads=[lastkey], writes=[lastkey])
                        S.op("pool", lambda e, win=win, w=w: e.tensor_scalar(out=win, in0=win, scalar1=1.0 / w, scalar2=None, op0=ALU.mult),
                             reads=[lastkey], writes=[lastkey])
                        S.op("pool", lambda e, win=win, gi=gi: e.tensor_sub(out=pooled[:, gi, :], in0=win, in1=uh[:, gi, 8:8 + T]),
                             reads=[lastkey, "uh"], writes=[("pooled", gi)])

                    S.dma("sp", lambda e: e.dma_start(out=fmask, in_=fmask_d), None, writes=["fmask"])

                    def load_far(hp):
                        sl = hp % 2
                        S.dma("sp", lambda e: e.dma_start(out=qkv_q[sl], in_=qTc_d[hp]), f"q{sl}", writes=[("q", sl)])
                        S.dma("sp", lambda e: e.dma_start(out=qkv_k[sl], in_=kTc_d[hp]), f"k{sl}", writes=[("k", sl)])
                        S.dma("sp", lambda e: e.dma_start(out=vcs[sl], in_=vc_d[hp]), f"v{sl}", writes=[("vc", sl)])

                    funits = [(hp, h, g4) for hp in range(8) for h in range(2) for g4 in range(4)]
                    NF = len(funits)
                    bg_ops.reverse()

                    def far_qk(i):
                        hp, h, g4 = funits[i]
                        sl = hp % 2
                        bA, bB = i % 2, 2 + i % 2

                        def f(e):
                            for c4 in range(4):
                                rho = 4 * g4 + c4
                                e.matmul(ps[bA][:, c4 * 64:(c4 + 1) * 64], lhsT=qkv_k[sl][h * 64:(h + 1) * 64, rho * 192:rho * 192 + 128],
                                         rhs=qkv_q[sl][h * 64:(h + 1) * 64, rho * 64:(rho + 1) * 64], start=True, stop=True)
                                ins = e.matmul(ps[bB][0:64, c4 * 64:(c4 + 1) * 64], lhsT=qkv_k[sl][h * 64:(h + 1) * 64, rho * 192 + 128:(rho + 1) * 192],
                                               rhs=qkv_q[sl][h * 64:(h + 1) * 64, rho * 64:(rho + 1) * 64], start=True, stop=True)
                            return ins
                        S.op("pe", f, reads=[("q", sl), ("k", sl)], writes=[("ps", bA), ("ps", bB)])
                        k2 = i % 2
                        S.op("act", lambda e: e.activation(out=pTA[k2], in_=ps[bA][:, 0:256], func=AF.Exp, scale=0.125), reads=[("ps", bA)], writes=[("pTA", k2)])
                        S.op("act", lambda e: e.activation(out=pTB[k2][0:64, :], in_=ps[bB][0:64, 0:256], func=AF.Exp, scale=0.125), reads=[("ps", bB)], writes=[("pTB", k2)])
                        S.op("dve", lambda e: e.tensor_mul(out=pTA[k2], in0=pTA[k2], in1=fmask[:, 0, :]), reads=[("pTA", k2), "fmask"], writes=[("pTA", k2)])
                        S.op("dve", lambda e: e.tensor_mul(out=pTB[k2][0:64, :], in0=pTB[k2][0:64, :], in1=fmask[0:64, 1, :]), reads=[("pTB", k2), "fmask"], writes=[("pTB", k2)])

                    def far_pv(i):
                        hp, h, g4 = funits[i]
                        sl = hp % 2
                        k2 = i % 2
                        bO = 4 + i % 2

                        def f(e):
                            for c4 in range(4):
                                rho = 4 * g4 + c4
                                e.matmul(ps[bO][0:64, c4 * 128:c4 * 128 + 65], lhsT=pTA[k2][:, c4 * 64:(c4 + 1) * 64],
                                         rhs=vcs[sl][:, rho, 0, h * 65:(h + 1) * 65], start=True, stop=False)
                                ins = e.matmul(ps[bO][0:64, c4 * 128:c4 * 128 + 65], lhsT=pTB[k2][0:64, c4 * 64:(c4 + 1) * 64],
                                               rhs=vcs[sl][0:64, rho, 1, h * 65:(h + 1) * 65], start=False, stop=True)
                            return ins
                        S.op("pe", f, reads=[("pTA", k2), ("pTB", k2), ("vc", sl)], writes=[("ps", bO)])
                        S.op("act", lambda e: e.activation(out=farsb[0:64, 4 * g4:4 * g4 + 4, h * 65:(h + 1) * 65],
                                                           in_=ps[bO][0:64, :].rearrange("p (c f) -> p c f", c=4)[:, :, 0:65], func=AF.Copy),
                             reads=[("ps", bO)], writes=[("farsb", h, g4)])
                        if h == 1 and g4 == 3:
                            S.dma("sp", lambda e: e.dma_start(out=far_d[hp].rearrange("(j r) f -> j r f", r=16), in_=farsb[0:64, :, :]), "fard",
                                  reads=[("farsb", hh, gg) for hh in range(2) for gg in range(4)], writes=[("fard", hp)])

                    load_far(0)
                    for i in range(0 if not os.environ.get("KSKIP_FAR") else NF + 1, NF + 1):
                        if i % 2 == 1 and bg_ops:
                            bg_ops.pop()()
                        if i < NF:
                            far_qk(i)
                        if i >= 1:
                            far_pv(i - 1)
                        if i < NF and funits[i][1] == 0 and funits[i][2] == 1 and funits[i][0] + 1 < 8:
                            load_far(funits[i][0] + 1)
                    S.barrier()

                    def load_qkv(hp):
                        sl = hp % 2
                        S.dma("sp", lambda e: e.dma_start(out=qkv_q[sl], in_=qT_d[hp]), f"q{sl}", writes=[("q", sl)])
                        S.dma("sp", lambda e: e.dma_start(out=qkv_k[sl], in_=kT_d[hp]), f"k{sl}", writes=[("k", sl)])
                        S.dma("sp", lambda e: e.dma_start(out=qkv_v[sl], in_=v_d[hp]), f"v{sl}", writes=[("v", sl)])
                        S.dma("sp", lambda e: e.dma_start(out=farnat[sl][:], in_=far_d[hp].rearrange("(qb p) f -> p qb f", p=128)), f"fn{sl}",
                              reads=[("fard", hp)], writes=[("farnat", sl)])

                    units = []
                    for hp in range(8):
                        for qb in range(NT):
                            for h in range(2):
                                for g, (r0, n) in enumerate(((-2, 4), (2, 1))):
                                    units.append((hp, qb, h, g, r0, n))
                    NU = len(units)

                    def emit_qk(i):
                        hp, qb, h, g, r0, n = units[i]
                        sl = hp % 2
                        bank = (0, 1, 2, 5, 6)[i % 5]
                        kb0 = qb + 8 + r0

                        def f(e):
                            for j in range(n):
                                ins = e.matmul(ps[bank][:, j * 128:(j + 1) * 128],
                                               lhsT=qkv_k[sl][h * 64:(h + 1) * 64, (kb0 + j) * 128:(kb0 + j + 1) * 128],
                                               rhs=qkv_q[sl][h * 64:(h + 1) * 64, qb * 128:(qb + 1) * 128], start=True, stop=True)
                            return ins
                        S.op("pe", f, reads=[("q", sl), ("k", sl)], writes=[("ps", bank)])
                        slot = i % 6
                        S.op("act", lambda e: e.activation(out=pT[slot][:, 0:n * 128], in_=ps[bank][:, 0:n * 128], func=AF.Exp, scale=0.125),
                             reads=[("ps", bank)], writes=[("pT", slot)])
                        S.op("dve", lambda e: e.tensor_mul(out=pT[slot][:, 0:n * 128], in0=pT[slot][:, 0:n * 128],
                                                           in1=masks[:, (r0 + 2) * 128:(r0 + 2 + n) * 128]),
                             reads=[("pT", slot), "masks"], writes=[("pT", slot)])

                    def emit_pv(i):
                        hp, qb, h, g, r0, n = units[i]
                        sl = hp % 2
                        slot = i % 6
                        par = (hp * NT + qb) % 2
                        ob = 3 + par
                        kb0 = qb + 8 + r0

                        def f(e):
                            for j in range(n):
                                ins = e.matmul(ps[ob][:, h * 128:h * 128 + 65], lhsT=pT[slot][:, j * 128:(j + 1) * 128],
                                               rhs=qkv_v[sl][:, kb0 + j, h * 65:(h + 1) * 65],
                                               start=(g == 0 and j == 0), stop=(g == 1))
                            return ins
                        S.op("pe", f, reads=[("pT", slot), ("v", sl)], writes=[("ps", ob)])
                        if h == 1 and g == 1:
                            S.op("dve", lambda e: e.tensor_add(out=totsb[par][:], in0=ps[ob][:, 0:256].rearrange("p (h c) -> p h c", h=2)[:, :, 0:65],
                                                               in1=farnat[sl][:, qb, :].rearrange("p (h c) -> p h c", h=2)),
                                 reads=[("ps", ob), ("farnat", sl)], writes=[("totsb", par)])
                            S.op("dve", lambda e: e.reciprocal(out=rec[:, par, :], in_=totsb[par][:, :, 64]),
                                 reads=[("totsb", par)], writes=[("rec", par)])
                            for hh in range(2):
                                S.op("dve", lambda e, hh=hh: e.tensor_scalar(
                                    out=yb[:, qb, hp * 128 + hh * 64:hp * 128 + hh * 64 + 64], in0=totsb[par][:, hh, 0:64],
                                    scalar1=rec[:, par, hh:hh + 1], scalar2=None, op0=ALU.mult),
                                    reads=[("totsb", par), ("rec", par)], writes=[("yb", qb, hp, hh)])

                    load_qkv(0)
                    LAG = 4
                    for i in range(0 if not os.environ.get("KSKIP_NEAR") else NU + LAG, NU + LAG):
                        if i % 2 == 1 and bg_ops:
                            bg_ops.pop()()
                        if i < NU:
                            emit_qk(i)
                        if i - LAG >= 0:
                            emit_pv(i - LAG)
                        if i < NU:
                            hp_, qb_, h_, g_ = units[i][:4]
                            if qb_ == 0 and h_ == 1 and g_ == 1 and hp_ + 1 < 8:
                                load_qkv(hp_ + 1)
                    while bg_ops:
                        bg_ops.pop()()
                    ybkeys = [[("yb", qb, hp, hh) for hp in range(8) for hh in range(2)] for qb in range(NT)]
                    for qb in range(NT):
                        c = 8 + (qb % 2)
                        S.op("act", lambda e, qb=qb, c=c: e.activation(out=sqb, in_=yb[:, qb, :], func=AF.Square, accum_out=ss[:, c:c + 1]),
                             reads=ybkeys[qb], writes=["sqb", ("ss", c)])
                        S.op("dve", lambda e, c=c: e.tensor_scalar(out=rs[:, c:c + 1], in0=ss[:, c:c + 1], scalar1=1.0 / 1024.0, scalar2=EPS,
                                                                   op0=ALU.mult, op1=ALU.add), reads=[("ss", c)], writes=[("rs", c)])
                        S.op("act", lambda e, c=c: e.activation(out=rs[:, c:c + 1], in_=rs[:, c:c + 1], func=AF.Sqrt),
                             reads=[("rs", c)], writes=[("rs", c)])
                        S.op("dve", lambda e, c=c: e.reciprocal(out=rs[:, c:c + 1], in_=rs[:, c:c + 1]), reads=[("rs", c)], writes=[("rs", c)])
                        S.op("dve", lambda e, qb=qb, c=c: e.scalar_tensor_tensor(out=ybs, in0=yb[:, qb, :], scalar=rs[:, c:c + 1], in1=gmixB,
                                                                                 op0=ALU.mult, op1=ALU.mult),
                             reads=ybkeys[qb] + [("rs", c), "gmixB"], writes=["ybs"])

                        def tr(e):
                            for j in range(8):
                                i_ = e.transpose(out=ptr[:, j, :], in_=ybs[:, j * 128:(j + 1) * 128], identity=ident[:])
                            return i_
                        S.op("pe", tr, reads=["ybs", "ident"], writes=[("ps", 7)])
                        S.op("act", lambda e, qb=qb: e.activation(out=yT[:, 4:12, qb * 128:(qb + 1) * 128], in_=ptr[:], func=AF.Copy),
                             reads=[("ps", 7)], writes=[("yT", "b", qb)])

                    dead = ["gmixB"] + [(n_, i_) for n_ in ("v", "q", "k") for i_ in range(2)]
                    for sl in range(2):
                        S.dma("pool", lambda e, sl=sl: e.dma_start(out=wo[sl], in_=wview[:, :, sl * 512:(sl + 1) * 512]),
                              f"wo{sl}", writes=[("wo", sl)] + dead)
                    def rms_feat(src, srckeys, base, tagk):
                        for th in range(2):
                            tsl = slice(th * 512, (th + 1) * 512)
                            for c in range(4):
                                S.op("act", lambda e, c=c, tsl=tsl: e.activation(out=tmpA, in_=src[:, c, tsl], func=AF.Square),
                                     reads=[srckeys(c, th)], writes=["tmpA"])
                                S.op("pe", lambda e, c=c: e.matmul(ps[0][:], lhsT=onesf[:], rhs=tmpA, start=(c == 0), stop=(c == 3)),
                                     reads=["tmpA", "onesf"], writes=[("ps", 0)])
                            S.op("dve", lambda e: e.tensor_scalar(out=rstdsb, in0=ps[0][:], scalar1=EPS, scalar2=None, op0=ALU.add),
                                 reads=[("ps", 0)], writes=["rstdsb"])
                            S.op("act", lambda e: e.activation(out=rstdsb, in_=rstdsb, func=AF.Sqrt), reads=["rstdsb"], writes=["rstdsb"])
                            S.op("dve", lambda e: e.reciprocal(out=rstdsb, in_=rstdsb), reads=["rstdsb"], writes=["rstdsb"])
                            for c in range(4):
                                S.op("dve", lambda e, c=c, tsl=tsl: e.scalar_tensor_tensor(out=yT[:, base + c, tsl], in0=src[:, c, tsl],
                                                                                  scalar=gmixT[:, base + c:base + c + 1], in1=rstdsb,
                                                                                  op0=ALU.mult, op1=ALU.mult),
                                     reads=[srckeys(c, th), "rstdsb", "gmixT"], writes=[("yT", tagk, c, th)])

                    for th in range(2):
                        tsl = slice(th * 512, (th + 1) * 512)
                        for c in range(4):
                            S.op("pe", lambda e, c=c, tsl=tsl: e.matmul(ps[1][:], lhsT=onesf[:], rhs=acc[:, c, tsl], start=(c == 0), stop=(c == 3)),
                                 reads=[("acc", c), "onesf"], writes=[("ps", 1)])
                        for c in range(4):
                            S.op("act", lambda e, c=c, tsl=tsl: e.activation(out=tmpA, in_=acc[:, c, tsl], func=AF.Square),
                                 reads=[("acc", c)], writes=["tmpA"])
                            S.op("pe", lambda e, c=c: e.matmul(ps[2][:], lhsT=onesf[:], rhs=tmpA, start=(c == 0), stop=(c == 3)),
                                 reads=["tmpA", "onesf"], writes=[("ps", 2)])
                        S.op("act", lambda e: e.activation(out=meansb, in_=ps[1][:], func=AF.Copy), reads=[("ps", 1)], writes=["meansb"])
                        S.op("dve", lambda e: e.tensor_mul(out=tmpB, in0=meansb, in1=meansb), reads=["meansb"], writes=["tmpB"])
                        S.op("dve", lambda e: e.tensor_sub(out=rstdsb, in0=ps[2][:], in1=tmpB), reads=[("ps", 2), "tmpB"], writes=["rstdsb"])
                        S.op("dve", lambda e: e.tensor_scalar(out=rstdsb, in0=rstdsb, scalar1=EPS, scalar2=None, op0=ALU.add),
                             reads=["rstdsb"], writes=["rstdsb"])
                        S.op("act", lambda e: e.activation(out=rstdsb, in_=rstdsb, func=AF.Sqrt), reads=["rstdsb"], writes=["rstdsb"])
                        S.op("dve", lambda e: e.reciprocal(out=rstdsb, in_=rstdsb), reads=["rstdsb"], writes=["rstdsb"])
                        for c in range(4):
                            S.op("dve", lambda e, c=c, tsl=tsl: e.tensor_sub(out=tmpB, in0=acc[:, c, tsl], in1=meansb),
                                 reads=[("acc", c), "meansb"], writes=["tmpB"])
                            S.op("dve", lambda e: e.tensor_mul(out=tmpB, in0=tmpB, in1=rstdsb), reads=["tmpB", "rstdsb"], writes=["tmpB"])
                            S.op("act", lambda e, c=c, tsl=tsl: e.activation(out=ya[:, c, tsl], in_=tmpB, func=AF.Silu,
                                                                    scale=vec4[:, 1, c:c + 1], bias=vec4[:, 2, c:c + 1]),
                                 reads=["tmpB", "vec4"], writes=[("ya", c, th)])
                    rms_feat(ya, lambda c, th: ("ya", c, th), 0, "a")

                    for gi in range(4):
                        for th in range(2):
                            tsl = slice(th * 512, (th + 1) * 512)
                            S.op("pe", lambda e, gi=gi, tsl=tsl: e.matmul(ps[1][:], lhsT=poolw[:, gi, :], rhs=pooled[:, gi, tsl], start=True, stop=True),
                                 reads=[("pooled", gi), "poolw"], writes=[("ps", 1)])
                            S.op("act", lambda e, gi=gi, tsl=tsl: e.activation(out=yc[:, gi, tsl], in_=ps[1][:], func=AF.Copy, scale=vec4[:, 3, gi:gi + 1]),
                                 reads=[("ps", 1), "vec4"], writes=[("yc", gi, th)])
                    rms_feat(yc, lambda c, th: ("yc", c, th), 12, "c")
                    if DEBUG:
                        S.barrier()
                        S.dma("sp", lambda e: e.dma_start(out=yT_dbg.rearrange("c p t -> p c t"), in_=yT[:]), "dbg", writes=["dbg"])
                S.barrier()

                with contextlib.ExitStack() as s4:
                    a2 = Arena(R2, R2B)
                    omix = a2.take([NT, D], F32)
                    gt1 = a2.take([D], F32)
                    gt2 = a2.take([D], F32)
                    xbs = [a2.take([D], F32), sbt(s4, "xb_b", [128, D], F32)[:]]
                    sq4 = sbt(s4, "sq4", [128, D], BF16)[:]
                    xs4s = [sbt(s4, f"xs4{i}", [128, D], BF16)[:] for i in range(2)]
                    S.dma("sp", lambda e: e.dma_start(out=gt1, in_=gpm_d.partition_broadcast(128)), None, writes=["gt1"])
                    S.dma("sp", lambda e: e.dma_start(out=gt2, in_=gpf_d.partition_broadcast(128)), None, writes=["gt2"])
                    for dc in range(4):
                        sl = dc % 2
                        if dc >= 2:
                            S.dma("pool", lambda e, sl=sl, dc=dc: e.dma_start(out=wo[sl], in_=wview[:, :, dc * 512:(dc + 1) * 512]),
                                  f"wo{sl}", writes=[("wo", sl)])
                        for tb in range(NT):
                            bank = tb % 4

                            def f(e, tb=tb, sl=sl, bank=bank):
                                for kc in range(16):
                                    ins = e.matmul(ps[bank][:], lhsT=yT[:, kc, tb * 128:(tb + 1) * 128], rhs=wo[sl][:, kc, :],
                                                   start=(kc == 0), stop=(kc == 15))
                                return ins
                            S.op("pe", f, reads=[("wo", sl)] + [("yT", "b", tb)] + [("yT", k, c, tb // 4) for k in ("a", "c") for c in range(4)],
                                 writes=[("ps", bank)])
                            S.op("act", lambda e, tb=tb, dc=dc, bank=bank: e.activation(out=omix[:, tb, dc * 512:(dc + 1) * 512], in_=ps[bank][:], func=AF.Copy),
                                 reads=[("ps", bank)], writes=[("omix", tb, dc)])
                    chains = []
                    for tb in range(NT):
                        b = tb % 2
                        c = tb % 2
                        xb, xbk = xbs[b], ("xb", b)
                        okeys = [("omix", tb, dc) for dc in range(4)]
                        ch = []
                        ch.append(lambda tb=tb, xb=xb, xbk=xbk, b=b: S.dma("sp", lambda e: e.dma_start(out=xb, in_=x_in[tb * 128:(tb + 1) * 128, :]), f"xb{b}", writes=[xbk]))
                        ch.append(lambda tb=tb, c=c, okeys=okeys: S.op("act", lambda e: e.activation(out=sq4, in_=omix[:, tb, :], func=AF.Square, accum_out=ss[:, c:c + 1]),
                                                                       reads=okeys, writes=["sq4", ("ss", c)]))
                        ch.append(lambda c=c: S.op("dve", lambda e: e.tensor_scalar(out=rs[:, c:c + 1], in0=ss[:, c:c + 1], scalar1=1.0 / D, scalar2=EPS,
                                                                                    op0=ALU.mult, op1=ALU.add), reads=[("ss", c)], writes=[("rs", c)]))
                        ch.append(lambda c=c: S.op("act", lambda e: e.activation(out=rs[:, c:c + 1], in_=rs[:, c:c + 1], func=AF.Sqrt), reads=[("rs", c)], writes=[("rs", c)]))
                        ch.append(lambda c=c: S.op("dve", lambda e: e.reciprocal(out=rs[:, c:c + 1], in_=rs[:, c:c + 1]), reads=[("rs", c)], writes=[("rs", c)]))
                        ch.append(lambda tb=tb, c=c, okeys=okeys: S.op("dve", lambda e: e.scalar_tensor_tensor(out=omix[:, tb, :], in0=omix[:, tb, :], scalar=rs[:, c:c + 1], in1=gt1,
                                                                                                          op0=ALU.mult, op1=ALU.mult),
                                                                       reads=okeys + [("rs", c), "gt1"], writes=okeys))
                        ch.append(lambda tb=tb, xb=xb, xbk=xbk, okeys=okeys: S.op("dve", lambda e: e.tensor_add(out=xb, in0=xb, in1=omix[:, tb, :]), reads=okeys + [xbk], writes=[xbk]))
                        ch.append(lambda tb=tb, xb=xb, xbk=xbk, b=b: S.dma("sp", lambda e: e.dma_start(out=x1_d[tb * 128:(tb + 1) * 128, :], in_=xb), f"x1w{b}", reads=[xbk], writes=[("x1d", tb)]))
                        ch += norm_transpose_ops(xb, xbk, gt2, "gt2", xs4s[b], ("xs4", b), sq4, "sq4", h2T, "h2T", tb, 2 + c)
                        chains.append(ch)
                    pipeline(chains, 8)
                S.barrier()
                sY.close()

                with contextlib.ExitStack() as s5:
                    wg = [sbt(s5, f"wg{i}", [128, 16, 256], BF16) for i in range(2)]
                    wu = [sbt(s5, f"wu{i}", [128, 16, 256], BF16) for i in range(2)]
                    sg = [sbt(s5, f"sg{i}", [128, 512], F32) for i in range(2)]
                    gview = wg_d.rearrange("(kc p) n -> p kc n", p=128)
                    uview = wu_d.rearrange("(kc p) n -> p kc n", p=128)
                    h2keys = [("h2T", tb, half) for tb in range(NT) for half in range(2)]
                    n5 = 0
                    for hg2 in range(NHC // 2):
                        sl = hg2 % 2
                        S.dma("pool", lambda e, sl=sl, hg2=hg2: e.dma_start(out=wg[sl][:], in_=gview[:, :, hg2 * 256:(hg2 + 1) * 256]), f"wg{sl}", writes=[("wg", sl)])
                        S.dma("pool", lambda e, sl=sl, hg2=hg2: e.dma_start(out=wu[sl][:], in_=uview[:, :, hg2 * 256:(hg2 + 1) * 256]), f"wu{sl}", writes=[("wu", sl)])
                        for hh in range(2):
                            hc = hg2 * 2 + hh
                            for th in range(2):
                                bg = (n5 % 3) * 2
                                bu = bg + 1
                                sgs = n5 % 2
                                n5 += 1

                                def f(e, sl=sl, hh=hh, th=th, bg=bg, bu=bu):
                                    for kc in range(16):
                                        e.matmul(ps[bg][:], lhsT=wg[sl][:, kc, hh * 128:(hh + 1) * 128], rhs=h2T[:, kc, th * 512:(th + 1) * 512],
                                                 start=(kc == 0), stop=(kc == 15))
                                    for kc in range(16):
                                        ins = e.matmul(ps[bu][:], lhsT=wu[sl][:, kc, hh * 128:(hh + 1) * 128], rhs=h2T[:, kc, th * 512:(th + 1) * 512],
                                                       start=(kc == 0), stop=(kc == 15))
                                    return ins
                                S.op("pe", f, reads=[("wg", sl), ("wu", sl)] + [("h2T", tb, half) for tb in range(th * 4, th * 4 + 4) for half in range(2)],
                                     writes=[("ps", bg), ("ps", bu)])
                                S.op("act", lambda e, bg=bg, sgs=sgs: e.activation(out=sg[sgs][:], in_=ps[bg][:], func=AF.Silu),
                                     reads=[("ps", bg)], writes=[("sg", sgs)])
                                S.op("dve", lambda e, bu=bu, sgs=sgs, hc=hc, th=th: e.tensor_mul(out=aT[:, hc, th * 512:(th + 1) * 512], in0=sg[sgs][:], in1=ps[bu][:]),
                                     reads=[("sg", sgs), ("ps", bu)], writes=[("aT", hc, th)])
                S.barrier()

                with contextlib.ExitStack() as s6:
                    wdn = [sbt(s6, f"wd{i}", [128, 4, 512], BF16) for i in range(2)]
                    gt3 = sbt(s6, "gt3", [128, D], F32)[:]
                    xb6 = sbt(s6, "xb6", [128, D], F32)[:]
                    sq6 = sbt(s6, "sq6", [128, D], BF16)[:]
                    S.dma("sp", lambda e: e.dma_start(out=gt3, in_=gpo_d.partition_broadcast(128)), None, writes=["gt3"])
                    dview = wd_d.rearrange("(hc p) n -> p hc n", p=128)
                    n6 = 0
                    for dc in range(4):
                        for hq in range(11):
                            sl = n6 % 2
                            n6 += 1
                            S.dma("pool", lambda e, sl=sl, hq=hq, dc=dc: e.dma_start(out=wdn[sl][:], in_=dview[:, hq * 4:(hq + 1) * 4, dc * 512:(dc + 1) * 512]),
                                  f"wd{sl}", writes=[("wd", sl)])
                            for tb in range(NT):
                                def f(e, sl=sl, hq=hq, tb=tb):
                                    for j in range(4):
                                        hc = hq * 4 + j
                                        ins = e.matmul(ps[tb][:], lhsT=aT[:, hc, tb * 128:(tb + 1) * 128], rhs=wdn[sl][:, j, :],
                                                       start=(hc == 0), stop=(hc == NHC - 1))
                                    return ins
                                S.op("pe", f, reads=[("wd", sl)] + [("aT", hq * 4 + j, tb // 4) for j in range(4)], writes=[("ps", tb)])
                        for tb in range(NT):
                            S.op("act", lambda e, tb=tb, dc=dc: e.activation(out=o_all[:, tb, dc * 512:(dc + 1) * 512], in_=ps[tb][:], func=AF.Copy),
                                 reads=[("ps", tb)], writes=[("o", tb, dc)])
                    xb6s = [xb6, sbt(s6, "xb6_b", [128, D], F32)[:]]
                    if do_A:
                        sA0 = contextlib.ExitStack()
                        gtA = sbt(sA0, "gtA", [128, D], F32)[:]
                        xsAs = [sbt(sA0, f"xsA{i}", [128, D], BF16)[:] for i in range(2)]
                        S.dma("sp", lambda e: e.dma_start(out=gtA, in_=gpre_d.partition_broadcast(128)), None, writes=["gtA"])
                    chains = []
                    for tb in range(NT):
                        c = tb % 2
                        b = tb % 2
                        xb, xbk = xb6s[b], ("xb6", b)
                        okeys = [("o", tb, dc) for dc in range(4)]
                        ch = []
                        ch.append(lambda tb=tb, xb=xb, xbk=xbk, b=b: S.dma("sp", lambda e: e.dma_start(out=xb, in_=x1_d[tb * 128:(tb + 1) * 128, :]), f"xb{b}", reads=[("x1d", tb)], writes=[xbk]))
                        ch.append(lambda tb=tb, c=c, okeys=okeys: S.op("act", lambda e: e.activation(out=sq6, in_=o_all[:, tb, :], func=AF.Square, accum_out=ss[:, c:c + 1]),
                                                                       reads=okeys, writes=["sq6", ("ss", c)]))
                        ch.append(lambda c=c: S.op("dve", lambda e: e.tensor_scalar(out=rs[:, c:c + 1], in0=ss[:, c:c + 1], scalar1=1.0 / D, scalar2=EPS,
                                                                                    op0=ALU.mult, op1=ALU.add), reads=[("ss", c)], writes=[("rs", c)]))
                        ch.append(lambda c=c: S.op("act", lambda e: e.activation(out=rs[:, c:c + 1], in_=rs[:, c:c + 1], func=AF.Sqrt), reads=[("rs", c)], writes=[("rs", c)]))
                        ch.append(lambda c=c: S.op("dve", lambda e: e.reciprocal(out=rs[:, c:c + 1], in_=rs[:, c:c + 1]), reads=[("rs", c)], writes=[("rs", c)]))
                        ch.append(lambda tb=tb, c=c, okeys=okeys: S.op("dve", lambda e: e.scalar_tensor_tensor(out=o_all[:, tb, :], in0=o_all[:, tb, :], scalar=rs[:, c:c + 1], in1=gt3,
                                                                                                          op0=ALU.mult, op1=ALU.mult),
                                                                       reads=okeys + [("rs", c), "gt3"], writes=okeys))
                        ch.append(lambda tb=tb, xb=xb, xbk=xbk, okeys=okeys: S.op("dve", lambda e: e.tensor_add(out=xb, in0=xb, in1=o_all[:, tb, :]), reads=okeys + [xbk], writes=[xbk]))
                        ch.append(lambda tb=tb, xb=xb, xbk=xbk, b=b: S.dma("sp", lambda e: e.dma_start(out=x_out[tb * 128:(tb + 1) * 128, :], in_=xb), f"xow{b}", reads=[xbk], writes=[("xout", tb)]))
                        if do_A:
                            ch += norm_transpose_ops(xb, xbk, gtA, "gtA", xsAs[b], ("xsA", b), sq6, "sq6", hT, "hT", tb, 4 + c)
                        chains.append(ch)
                    pipeline(chains, 8)
                    if do_A:
                        S.barrier()
                        sA0.close()
            S.barrier()

        if do_A:
            with contextlib.ExitStack() as sA:
                if not do_B:
                    a1 = Arena(R1, R1B)
                    gtA = a1.take([D], F32)
                    xsA = a1.take([D], BF16)
                    xbA = a1.take([D], F32)
                    sqA = a1.take([D], BF16)
                    S.dma("sp", lambda e: e.dma_start(out=gtA, in_=gpre_d.partition_broadcast(128)), None, writes=["gtA"])
                    for tb in range(NT):
                        S.dma("sp", lambda e, tb=tb: e.dma_start(out=xbA, in_=x_in[tb * 128:(tb + 1) * 128, :]), "xb", writes=["xbA"])
                        norm_transpose(xbA, "xbA", gtA, "gtA", xsA, "xsA", sqA, "sqA", hT, "hT", tb, 4 + tb % 2)
                    S.barrier()
                a1 = Arena(R1, R1B)
                qko = a1.take([16, T], BF16)
                asb = a1.take([4, T], F32)
                hgo = a1.take([4, T], F32)
                a2 = Arena(R2, R2B, start=32768)
                wi = [a2.take([16, 512], BF16) for i in range(2)]
                upo = a2.take([4, T], F32)
                ropec = a2.take([T], F32)
                ropes = a2.take([T], F32)
                pmat = sbt(sA, "pmat_sb", [128, 128], F32)
                sgA = [sbt(sA, f"sgA{i}", [128, 512], F32) for i in range(2)]
                qf = [sbt(sA, f"qf{i}", [128, 512], F32) for i in range(2)]
                t1 = [sbt(sA, f"t1{i}", [128, 512], F32) for i in range(2)]
                t2 = [sbt(sA, f"t2{i}", [128, 512], F32) for i in range(2)]
                vsb = sbt(sA, "vsb", [128, NT, 16, 65], BF16)
                S.dma("sp", lambda e: e.dma_start(out=ropec, in_=ropec_d), None, writes=["ropec"])
                S.dma("sp", lambda e: e.dma_start(out=ropes, in_=ropes_d), None, writes=["ropes"])
                S.dma("sp", lambda e: e.dma_start(out=pmat[:], in_=pmat_d), None, writes=["pmat"])
                S.op("dve", lambda e: e.memset(vsb[:], 1.0), writes=["vsb_ones"])
                hkeys = [("hT", tb, half) for tb in range(NT) for half in range(2)]
                wiv = win_d.rearrange("(kc p) n -> p kc n", p=128)
                nmm = 0
                nr = 0
                def load_wi(s_):
                    sl = s_ % 2
                    S.dma("pool", lambda e: e.dma_start(out=wi[sl], in_=wiv[:, :, s_ * 512:(s_ + 1) * 512]), f"wi{sl}", writes=[("wi", sl)])
                load_wi(0)
                for s_ in range(9):
                    sl = s_ % 2
                    if s_ + 1 < 9:
                        load_wi(s_ + 1)
                    if s_ in (6, 7):
                        for tb in range(NT):
                            bank = nmm % 4
                            nmm += 1

                            def f(e, sl=sl, tb=tb, bank=bank):
                                for kc in range(16):
                                    ins = e.matmul(ps[bank][:], lhsT=hT[:, kc, tb * 128:(tb + 1) * 128], rhs=wi[sl][:, kc, :], start=(kc == 0), stop=(kc == 15))
                                return ins
                            S.op("pe", f, reads=[("wi", sl)] + hkeys, writes=[("ps", bank)])
                            h0 = (s_ - 6) * 8
                            S.op("act", lambda e, tb=tb, bank=bank, h0=h0: e.activation(out=vsb[:, tb, h0:h0 + 8, 0:64],
                                                                                        in_=ps[bank][:].rearrange("p (h e) -> p h e", h=8), func=AF.Copy),
                                 reads=[("ps", bank), "vsb_ones"], writes=[("vsb", tb, s_)])
                        continue
                    for oc in range(4):
                        for th in range(2):
                            tsl = slice(th * 512, (th + 1) * 512)
                            bank = nmm % 4
                            nmm += 1

                            def f(e, sl=sl, oc=oc, th=th, bank=bank):
                                for kc in range(16):
                                    ins = e.matmul(ps[bank][:], lhsT=wi[sl][:, kc, oc * 128:(oc + 1) * 128], rhs=hT[:, kc, th * 512:(th + 1) * 512],
                                                   start=(kc == 0), stop=(kc == 15))
                                return ins
                            S.op("pe", f, reads=[("wi", sl)] + hkeys, writes=[("ps", bank)])
                            if s_ == 0:
                                S.op("act", lambda e, oc=oc, tsl=tsl, bank=bank: e.activation(out=asb[:, oc, tsl], in_=ps[bank][:], func=AF.Copy),
                                     reads=[("ps", bank)], writes=[("asb", oc, th)])
                            elif s_ == 1:
                                k2 = nmm % 2
                                S.op("act", lambda e, bank=bank, k2=k2: e.activation(out=sgA[k2][:], in_=ps[bank][:], func=AF.Sigmoid),
                                     reads=[("ps", bank)], writes=[("sgA", k2)])
                                S.op("dve", lambda e, oc=oc, tsl=tsl, k2=k2: e.tensor_mul(out=hgo[:, oc, tsl], in0=asb[:, oc, tsl], in1=sgA[k2][:]),
                                     reads=[("sgA", k2), ("asb", oc, th)], writes=[("hgo", oc, th)])
                            elif s_ == 8:
                                S.op("act", lambda e, oc=oc, tsl=tsl, bank=bank: e.activation(out=upo[:, oc, tsl], in_=ps[bank][:], func=AF.Copy),
                                     reads=[("ps", bank)], writes=[("upo", oc, th)])
                            else:
                                ch = (s_ - 2) * 4 + oc
                                k2 = nr % 2
                                pb = 4 + k2
                                nr += 1
                                S.op("act", lambda e, bank=bank, k2=k2: e.activation(out=qf[k2][:], in_=ps[bank][:], func=AF.Copy),
                                     reads=[("ps", bank)], writes=[("qf", k2)])
                                S.op("pe", lambda e, k2=k2, pb=pb: e.matmul(ps[pb][:], lhsT=pmat[:], rhs=qf[k2][:], start=True, stop=True),
                                     reads=[("qf", k2), "pmat"], writes=[("ps", pb)])
                                S.op("dve", lambda e, k2=k2, pb=pb, tsl=tsl: e.tensor_mul(out=t1[k2][:], in0=ps[pb][:], in1=ropes[:, tsl]),
                                     reads=[("ps", pb), "ropes"], writes=[("t1", k2)])
                                S.op("dve", lambda e, k2=k2, tsl=tsl: e.tensor_mul(out=t2[k2][:], in0=qf[k2][:], in1=ropec[:, tsl]),
                                     reads=[("qf", k2), "ropec"], writes=[("t2", k2)])
                                S.op("dve", lambda e, k2=k2, ch=ch, tsl=tsl: e.tensor_add(out=qko[:, ch, tsl], in0=t1[k2][:], in1=t2[k2][:]),
                                     reads=[("t1", k2), ("t2", k2)], writes=[("qko", ch, th)])
                S.dma("sp", lambda e: e.dma_start(out=hg_o.rearrange("c p t -> p c t"), in_=hgo), "outA",
                      reads=[("hgo", c, th) for c in range(4) for th in range(2)], writes=["hg_o"])
                S.dma("sp", lambda e: e.dma_start(out=up_o.rearrange("c p t -> p c t"), in_=upo), "outA",
                      reads=[("upo", c, th) for c in range(4) for th in range(2)], writes=["up_o"])
                S.dma("sp", lambda e: e.dma_start(out=qT_o.rearrange("c p t -> p c t"), in_=qko[:, 0:8, :]), "outA",
                      reads=[("qko", c, th) for c in range(8) for th in range(2)], writes=["qT_o"])
                S.dma("sp", lambda e: e.dma_start(out=kT_o.rearrange("c p t -> p c t"), in_=qko[:, 8:16, :]), "outA",
                      reads=[("qko", c, th) for c in range(8, 16) for th in range(2)], writes=["kT_o"])
                S.dma("sp", lambda e: e.dma_start(out=v_o.rearrange("t p f -> p t f"), in_=vsb[:].rearrange("p t h e -> p t (h e)")), "outA",
                      reads=[("vsb", tb, s_) for tb in range(NT) for s_ in (6, 7)], writes=["v_o"])
        S.barrier()
        S.run()
    return nc


_PROGS = {}


def _prog(do_B, do_A):
    key = (do_B, do_A)
    if key not in _PROGS:
        _PROGS[key] = build_program(do_B, do_A)
    return _PROGS[key]


def _const_tables():
    o = np.arange(-2 * 128 - 127, 2 * 128 + 128)
    w = ((np.abs(o) <= 64).astype(np.float32) + ((o % 4 == 0) & (np.abs(o) <= 256)))
    wmap = dict(zip(o.tolist(), w.tolist()))
    k = np.arange(128)[:, None, None]
    rel = np.array([r for _ in range(2) for r in range(-2, 3)])[None, :, None]
    q = np.arange(128)[None, None, :]
    off = rel * 128 + k - q
    masks = np.vectorize(wmap.get)(off).astype(np.float32).reshape(128, 10 * 128).astype(ml_dtypes.bfloat16)
    kk = np.arange(128)[:, None]
    jj = np.arange(64)[None, :]
    mA = (kk >= jj).astype(np.float32)
    mB = ((kk <= jj) & (kk < 64)).astype(np.float32)
    fmask = np.stack([np.tile(mA, (1, 4)), np.tile(mB, (1, 4))], axis=1).astype(ml_dtypes.bfloat16)
    pm = np.zeros((128, 128), np.float32)
    for hh in range(2):
        for m in range(8):
            pm[hh * 64 + m + 8, hh * 64 + m] = -1.0
            pm[hh * 64 + m, hh * 64 + m + 8] = 1.0
    return (masks, fmask), pm


def _rope_tables(c):
    pos = (np.arange(T) + c * T).astype(np.float32)
    inv = (np.float32(500000.0) ** (-np.arange(0, 16, 2, dtype=np.float32) / np.float32(16))).astype(np.float32)
    ang = (pos[:, None] * inv[None, :]).astype(np.float32)
    cs, sn = np.cos(ang).astype(np.float32), np.sin(ang).astype(np.float32)
    C = np.ones((128, T), np.float32)
    Sn = np.zeros((128, T), np.float32)
    for hh in range(2):
        for i in range(16):
            C[hh * 64 + i] = cs[:, i % 8]
            Sn[hh * 64 + i] = sn[:, i % 8]
    return C, Sn


def _corr_table(c):
    out = np.ones((4, 16), np.float32)
    idx = np.concatenate([np.arange(8), np.arange(T - 8, T)]) + c * T
    for gi, w in enumerate((2, 4, 8, 16)):
        lo = np.clip(idx - w // 2, 0, S_LEN)
        hi = np.clip(idx + w - w // 2, 0, S_LEN)
        out[gi] = w / (hi - lo).astype(np.float32)
    return np.ascontiguousarray(np.broadcast_to(out[None], (128, 4, 16)))


def _colmajor(v, n):
    return np.ascontiguousarray(v.reshape(n, 128).T)


def _run_step(step, xs, Aout, P, consts):
    f32 = np.float32
    masks, pm, ident, ropes, corrs = consts
    do_B = step > 0
    do_A = step < DEPTH
    lb, la = step - 1, step
    nc = _prog(do_B, do_A)
    maps = []
    if do_B:
        kT = np.concatenate([Aout[c]["kT_o"] for c in range(NCORES)], axis=2)
        kT = np.pad(kT, ((0, 0), (0, 0), (T, T)))
        vv = np.concatenate([Aout[c]["v_o"].reshape(T, 8, 130) for c in range(NCORES)], axis=0)
        vv = np.pad(vv, ((T, T), (0, 0), (0, 0)))
        hgf = np.pad(np.concatenate([Aout[c]["hg_o"] for c in range(NCORES)], axis=2), ((0, 0), (0, 0), (15, 15)))
        upf = np.pad(np.concatenate([Aout[c]["up_o"] for c in range(NCORES)], axis=2), ((0, 0), (0, 0), (8, 8)))
        cw = np.ascontiguousarray(np.asarray(P["conv_w"][lb], f32).T.reshape(4, 128, 31).transpose(1, 0, 2))
        vec4 = np.stack([_colmajor(np.asarray(P[k][lb], f32), 4) for k in ("conv_b", "conv_ln_g", "conv_ln_b", "pool_scale")], axis=1)
        gm = np.asarray(P["g_mix"][lb], f32)
    for c in range(NCORES):
        m = {"ident": ident, "x_in": xs[c]}
        if do_B:
            seg = vv[c * T:c * T + 3 * T]
            vh = seg.reshape(24, 128, 8, 130).transpose(2, 1, 0, 3)
            sc = seg.reshape(192, 16, 8, 130)
            kseg = kT[:, :, c * T:c * T + 3 * T]
            kTc = np.ascontiguousarray(kseg.reshape(8, 128, 192, 16).transpose(0, 1, 3, 2)).reshape(8, 128, 16 * 192)
            qTc = np.ascontiguousarray(np.asarray(Aout[c]["qT_o"]).reshape(8, 128, 64, 16).transpose(0, 1, 3, 2)).reshape(8, 128, 16 * 64)
            vc = np.zeros((8, 128, 16, 2, 130), seg.dtype)
            vc[:, :, :, 0, :] = sc[0:128].transpose(2, 0, 1, 3)
            vc[:, 0:64, :, 1, :] = sc[128:192].transpose(2, 0, 1, 3)
            m.update({
                "qT": Aout[c]["qT_o"],
                "kTh": np.ascontiguousarray(kT[:, :, c * T:c * T + 3 * T]),
                "vh": np.ascontiguousarray(vh),
                "hgh": np.ascontiguousarray(hgf[:, :, c * T:c * T + T + 30]),
                "uh": np.ascontiguousarray(upf[:, :, c * T:c * T + T + 16]),
                "masks": masks[0], "fmask": masks[1], "vc": vc, "kTc": kTc, "qTc": qTc, "corr": corrs[c], "cw": cw, "vec4": np.ascontiguousarray(vec4),
                "gmixT": _colmajor(gm, 16), "gmixB": np.ascontiguousarray(gm[None, 512:1536]),
                "poolw": np.asarray(P["pool_w"][lb], f32), "w_out": np.asarray(P["w_out"][lb], f32),
                "g_post_mix": np.asarray(P["g_post_mix"][lb], f32)[None], "g_pre_ffn": np.asarray(P["g_pre_ffn"][lb], f32)[None],
                "w_gate": np.asarray(P["w_gate"][lb], f32), "w_up": np.asarray(P["w_up"][lb], f32),
                "w_down": np.asarray(P["w_down"][lb], f32), "g_post_ffn": np.asarray(P["g_post_ffn"][lb], f32)[None],
            })
        if do_A:
            m.update({"w_in": np.asarray(P["w_in"][la], f32), "g_pre_mix": np.asarray(P["g_pre_mix"][la], f32)[None],
                      "rope_c": ropes[c][0], "rope_s": ropes[c][1], "pmat": pm})
        maps.append(m)
    res = run_bass_kernel_spmd(nc, maps, core_ids=list(range(NCORES)))
    Aout = res.results
    if do_B:
        xs = [np.asarray(Aout[c]["x_out"], f32) for c in range(NCORES)]
    return xs, Aout


def _consts():
    masks, pm = _const_tables()
    ident = np.eye(128, dtype=np.float32)
    ropes = [_rope_tables(c) for c in range(NCORES)]
    corrs = [_corr_table(c) for c in range(NCORES)]
    return masks, pm, ident, ropes, corrs


def kernel(x, w_in, conv_w, conv_b, conv_ln_g, conv_ln_b, pool_w, pool_scale, g_mix,
           w_out, g_pre_mix, g_post_mix, g_pre_ffn, g_post_ffn, w_gate, w_up, w_down):
    P = dict(w_in=w_in, conv_w=conv_w, conv_b=conv_b, conv_ln_g=conv_ln_g, conv_ln_b=conv_ln_b, pool_w=pool_w,
             pool_scale=pool_scale, g_mix=g_mix, w_out=w_out, g_pre_mix=g_pre_mix, g_post_mix=g_post_mix,
             g_pre_ffn=g_pre_ffn, g_post_ffn=g_post_ffn, w_gate=w_gate, w_up=w_up, w_down=w_down)
    x = np.asarray(x, np.float32)
    consts = _consts()
    xs = [np.ascontiguousarray(x[0, c * T:(c + 1) * T]) for c in range(NCORES)]
    Aout = None
    for step in range(DEPTH + 1):
        xs, Aout = _run_step(step, xs, Aout, P, consts)
    return np.concatenate(xs, axis=0)[None].astype(np.float32)
```

```python
import contextlib
import numpy as np
import ml_dtypes
import concourse.bass as bass
import concourse.mybir as mybir
from concourse.bass_utils import run_bass_kernel_spmd

F32 = mybir.dt.float32
BF16 = mybir.dt.bfloat16
AF = mybir.ActivationFunctionType
ALU = mybir.AluOpType

NCORES = 8
S_LEN = 8192
T = 1024
NT = 8
D = 2048
DEPTH = 4
INW = 4608
FF = 5632
NHC = FF // 128
EPS = 1e-6
ENGS = ("pe", "act", "dve", "pool", "sp")
import os
DEBUG = bool(os.environ.get("KDEBUG"))


class Sched:
    def __init__(self, nc):
        self.nc = nc
        self.ops = {e: [] for e in ENGS}
        self.cnt = {}
        self.waited = {e: {} for e in ENGS}
        self.last_w = {}
        self.readers = {}
        self.semnames = set(ENGS)
        self.sem_h = {}

    def _deps(self, eng, reads, writes):
        need = {}

        def add(k, v):
            if k == "pe" and eng == "pe":
                return
            if need.get(k, 0) < v:
                need[k] = v

        for k in reads:
            t = self.last_w.get(k)
            if t is not None:
                add(*t)
        for k in writes:
            t = self.last_w.get(k)
            if t is not None:
                add(*t)
            for kk, vv in self.readers.get(k, {}).items():
                add(kk, vv)
        return self._filter(eng, need)

    def _filter(self, eng, need):
        out = []
        w = self.waited[eng]
        for k, v in need.items():
            if w.get(k, 0) < v:
                w[k] = v
                out.append((k, v))
        return out

    def _track(self, tok, reads, writes):
        for k in writes:
            self.last_w[k] = tok
            self.readers[k] = {}
        for k in reads:
            r = self.readers.setdefault(k, {})
            if r.get(tok[0], 0) < tok[1]:
                r[tok[0]] = tok[1]

    def op(self, eng, fn, reads=(), writes=()):
        waits = self._deps(eng, reads, writes)
        self.cnt[eng] = self.cnt.get(eng, 0) + 1
        tok = (eng, self.cnt[eng])
        self._track(tok, reads, writes)
        self.ops[eng].append((waits, fn, eng, 1))
        return tok

    def dma(self, eng, fn, sem, reads=(), writes=()):
        if sem is None:
            self.nuniq = getattr(self, "nuniq", 0) + 1
            sem = f"u{self.nuniq}"
        self.semnames.add(sem)
        waits = self._deps(eng, reads, writes)
        self.cnt[sem] = self.cnt.get(sem, 0) + 16
        tok = (sem, self.cnt[sem])
        self._track(tok, reads, writes)
        self.ops[eng].append((waits, fn, sem, 16))
        return tok

    def barrier(self):
        for e in ENGS:
            waits = self._filter(e, dict(self.cnt))
            if waits:
                self.ops[e].append((waits, None, None, 0))

    def run(self):
        nc = self.nc
        with contextlib.ExitStack() as st:
            for name in sorted(self.semnames):
                self.sem_h[name] = st.enter_context(nc.semaphore("s_" + name))
            block = st.enter_context(nc.Block())
            sem_h = self.sem_h

            def replay(e, lst):
                for waits, fn, sem, inc in lst:
                    for k, v in waits:
                        e.wait_ge(sem_h[k], v)
                    if fn is not None:
                        fn(e).then_inc(sem_h[sem], inc)

            @block.tensor
            def _(e):
                replay(e, self.ops["pe"])

            @block.scalar
            def _(e):
                replay(e, self.ops["act"])

            @block.vector
            def _(e):
                replay(e, self.ops["dve"])

            @block.gpsimd
            def _(e):
                replay(e, self.ops["pool"])

            @block.sync
            def _(e):
                replay(e, self.ops["sp"])


class Arena:
    def __init__(self, t, nbytes, start=0):
        self.t, self.n, self.off = t, nbytes, start

    def take(self, shape, dt):
        esz = 2 if dt == BF16 else 4
        n = esz
        for d in shape:
            n *= d
        n = (n + 31) // 32 * 32
        assert self.off + n <= self.n, (self.off, n, self.n)
        ap = self.t[:, self.off // 4:(self.off + n) // 4]
        if dt != F32:
            ap = ap.bitcast(dt)
        tot = 1
        for d in shape:
            tot *= d
        ap = ap[:, 0:tot]
        if len(shape) == 2:
            ap = ap.rearrange("p (a b) -> p a b", a=shape[0])
        elif len(shape) == 3:
            ap = ap.rearrange("p (a b c) -> p a b c", a=shape[0], b=shape[1])
        self.off += n
        return ap


def build_program(do_B, do_A):
    nc = bass.Bass("TRN2", target_bir_lowering=False)
    S = Sched(nc)

    def din(name, shape, dt=F32):
        return nc.dram_tensor(name, list(shape), dt, kind="ExternalInput").ap()

    def dout(name, shape, dt=F32):
        return nc.dram_tensor(name, list(shape), dt, kind="ExternalOutput").ap()

    ident_d = din("ident", [128, 128])
    x_in = din("x_in", [T, D])
    if do_B:
        qT_d = din("qT", [8, 128, T], BF16)
        kT_d = din("kTh", [8, 128, 3 * T], BF16)
        v_d = din("vh", [8, 128, 24, 130], BF16)
        hg_d = din("hgh", [4, 128, T + 30])
        up_d = din("uh", [4, 128, T + 16])
        mask_d = din("masks", [128, 10 * 128], BF16)
        fmask_d = din("fmask", [128, 2, 256], BF16)
        vc_d = din("vc", [8, 128, 16, 2, 130], BF16)
        kTc_d = din("kTc", [8, 128, 16 * 192], BF16)
        qTc_d = din("qTc", [8, 128, 16 * 64], BF16)
        far_d = nc.dram_tensor("far_scratch", [8, T, 130], F32).ap()
        corr_d = din("corr", [128, 4, 16])
        cw_d = din("cw", [128, 4, 31])
        vec4_d = din("vec4", [128, 4, 4])
        gmixT_d = din("gmixT", [128, 16])
        gmixB_d = din("gmixB", [1, 1024])
        poolw_d = din("poolw", [4, 128, 128])
        wout_d = din("w_out", [D, D])
        gpm_d = din("g_post_mix", [1, D])
        gpf_d = din("g_pre_ffn", [1, D])
        wg_d = din("w_gate", [D, FF])
        wu_d = din("w_up", [D, FF])
        wd_d = din("w_down", [FF, D])
        gpo_d = din("g_post_ffn", [1, D])
        x_out = dout("x_out", [T, D])
        if DEBUG:
            x1_d = dout("x1_dbg", [T, D])
            yT_dbg = dout("yT_dbg", [16, 128, T], BF16)
        else:
            x1_d = nc.dram_tensor("x1_scratch", [T, D], F32).ap()
    if do_A:
        win_d = din("w_in", [D, INW])
        gpre_d = din("g_pre_mix", [1, D])
        ropec_d = din("rope_c", [128, T])
        ropes_d = din("rope_s", [128, T])
        pmat_d = din("pmat", [128, 128])
        qT_o = dout("qT_o", [8, 128, T], BF16)
        kT_o = dout("kT_o", [8, 128, T], BF16)
        v_o = dout("v_o", [NT, 128, 16 * 65], BF16)
        hg_o = dout("hg_o", [4, 128, T])
        up_o = dout("up_o", [4, 128, T])

    with contextlib.ExitStack() as top:
        def sbt(st, name, shape, dt):
            return st.enter_context(nc.sbuf_tensor("sb_" + name, list(shape), dt))

        ps = [top.enter_context(nc.psum_tensor(f"ps{i}", [128, 512], F32)) for i in range(8)]
        ident = sbt(top, "ident_sb", [128, 128], BF16)
        onesf = sbt(top, "onesf", [128, 128], F32)
        R1 = sbt(top, "R1", [128, 16384], F32)
        R2 = sbt(top, "R2", [128, 22528], F32)
        R1B, R2B = 65536, 90112
        h2T = Arena(R1, R1B).take([16, T], BF16)
        o_all = Arena(R1, R1B).take([NT, D], F32)
        aT = Arena(R2, R2B).take([NHC, T], BF16)
        hT = Arena(R2, R2B).take([16, T], BF16)
        ss = sbt(top, "ss", [128, 16], F32)
        rs = sbt(top, "rs", [128, 16], F32)
        ptr = ps[7][:].bitcast(BF16).rearrange("p (j t) -> p j t", j=8)

        S.dma("pool", lambda e: e.dma_start(out=ident[:], in_=ident_d), None, writes=["ident"])
        S.op("dve", lambda e: e.memset(onesf[:], 1.0 / 512.0), writes=["onesf"])

        def norm_transpose_ops(xblk, xkey, gt, gkey, xs, xskey, sq, sqkey, dstT, dkey, tb, slot):
            c = slot
            ops = []
            ops.append(lambda: S.op("act", lambda e: e.activation(out=sq, in_=xblk, func=AF.Square, accum_out=ss[:, c:c + 1]),
                                    reads=[xkey], writes=[sqkey, ("ss", c)]))
            ops.append(lambda: S.op("dve", lambda e: e.tensor_scalar(out=rs[:, c:c + 1], in0=ss[:, c:c + 1], scalar1=1.0 / D, scalar2=EPS,
                                                                     op0=ALU.mult, op1=ALU.add), reads=[("ss", c)], writes=[("rs", c)]))
            ops.append(lambda: S.op("act", lambda e: e.activation(out=rs[:, c:c + 1], in_=rs[:, c:c + 1], func=AF.Sqrt),
                                    reads=[("rs", c)], writes=[("rs", c)]))
            ops.append(lambda: S.op("dve", lambda e: e.reciprocal(out=rs[:, c:c + 1], in_=rs[:, c:c + 1]), reads=[("rs", c)], writes=[("rs", c)]))
            ops.append(lambda: S.op("dve", lambda e: e.scalar_tensor_tensor(out=xs, in0=xblk, scalar=rs[:, c:c + 1], in1=gt,
                                                                            op0=ALU.mult, op1=ALU.mult),
                                    reads=[xkey, gkey, ("rs", c)], writes=[xskey]))
            for half in range(2):
                def tr(e, half=half):
                    for j in range(8):
                        kc = half * 8 + j
                        i = e.transpose(out=ptr[:, j, :], in_=xs[:, kc * 128:(kc + 1) * 128], identity=ident[:])
                    return i
                ops.append(lambda tr=tr: S.op("pe", tr, reads=[xskey, "ident"], writes=[("ps", 7)]))
                ops.append(lambda half=half: S.op("act", lambda e: e.activation(out=dstT[:, half * 8:(half + 1) * 8, tb * 128:(tb + 1) * 128],
                                                                               in_=ptr[:], func=AF.Copy),
                                                  reads=[("ps", 7)], writes=[(dkey, tb, half)]))
            return ops

        def norm_transpose(*a):
            for o in norm_transpose_ops(*a):
                o()

        def pipeline(chains, stagger):
            n = max(len(c) for c in chains) + stagger * (len(chains) - 1)
            for step in range(n):
                for i, ch in enumerate(chains):
                    k = step - i * stagger
                    if 0 <= k < len(ch):
                        ch[k]()

        if do_B:
            with contextlib.ExitStack() as sB:
                gmixT = sbt(sB, "gmixT", [128, 16], F32)
                vec4 = sbt(sB, "vec4", [128, 4, 4], F32)
                sY = contextlib.ExitStack()
                yT = sbt(sY, "yT", [128, 16, T], BF16)
                a1u = Arena(R1, R1B, start=32768)
                wo = [a1u.take([16, 512], BF16) for i in range(2)]
                wview = wout_d.rearrange("(kc p) n -> p kc n", p=128)
                S.dma("sp", lambda e: e.dma_start(out=gmixT[:], in_=gmixT_d), None, writes=["gmixT"])
                S.dma("sp", lambda e: e.dma_start(out=vec4[:], in_=vec4_d), None, writes=["vec4"])

                with contextlib.ExitStack() as s1:
                    a2 = Arena(R2, R2B)
                    yb = a2.take([NT, 1024], F32)
                    hg = a2.take([4, T + 30], F32)
                    acc = a2.take([4, T], F32)
                    uh = a2.take([4, T + 16], F32)
                    masks = a2.take([10 * 128], BF16)
                    ybs = a2.take([1024], BF16)
                    a1 = Arena(R1, R1B)
                    pw = a1.take([3, T + 16], F32)
                    pooled = a1.take([4, T], BF16)
                    meansb = a1.take([512], F32)
                    rstdsb = a1.take([512], F32)
                    tmpA = a1.take([512], F32)
                    tmpB = a1.take([512], F32)
                    sqb = a1.take([1024], BF16)
                    assert a1.off <= 32768
                    gmixB = a1.take([1024], F32)
                    qkv_v = [a1.take([24, 130], BF16) for i in range(2)]
                    qkv_q = [a1.take([T], BF16) for i in range(2)]
                    qkv_k = [a1.take([3 * T], BF16) for i in range(2)]
                    aF = Arena(R2, R2B)
                    vcs = [aF.take([16, 2, 130], BF16) for i in range(2)]
                    farsb = aF.take([16, 130], F32)
                    pTA = [aF.take([256], BF16) for i in range(2)]
                    pTB = [aF.take([256], BF16) for i in range(2)]
                    fmask = aF.take([2, 256], BF16)
                    assert aF.off <= 32768
                    farnat = [sbt(s1, f"farnat{i}", [128, NT, 130], F32) for i in range(2)]
                    totsb = [sbt(s1, f"totsb{i}", [128, 2, 65], F32) for i in range(2)]
                    ya = acc
                    yc = hg
                    cw = sbt(s1, "cw", [128, 4, 31], F32)
                    corr = sbt(s1, "corr", [128, 4, 16], F32)
                    poolw = sbt(s1, "poolw", [128, 4, 128], BF16)
                    pT = [sbt(s1, f"pT{i}", [128, 512], BF16) for i in range(6)]
                    rec = sbt(s1, "rec", [128, 2, 2], F32)

                    S.dma("sp", lambda e: e.dma_start(out=hg, in_=hg_d.rearrange("c p t -> p c t")), None, writes=["hg"])
                    S.dma("sp", lambda e: e.dma_start(out=uh, in_=up_d.rearrange("c p t -> p c t")), None, writes=["uh"])
                    S.dma("sp", lambda e: e.dma_start(out=cw[:], in_=cw_d), None, writes=["cw"])
                    S.dma("sp", lambda e: e.dma_start(out=corr[:], in_=corr_d), None, writes=["corr"])
                    S.dma("sp", lambda e: e.dma_start(out=masks, in_=mask_d), None, writes=["masks"])
                    S.dma("sp", lambda e: e.dma_start(out=gmixB, in_=gmixB_d.partition_broadcast(128)), None, writes=["gmixB"])
                    S.dma("pool", lambda e: e.dma_start(out=poolw[:], in_=poolw_d.rearrange("g c d -> c g d")), None, writes=["poolw"])


                    bg_ops = []
                    for c in range(4):
                        bg_ops.append(lambda c=c: S.op("dve", lambda e: e.tensor_scalar(out=acc[:, c, :], in0=hg[:, c, 0:T], scalar1=cw[:, c, 0:1],
                                                                                      scalar2=vec4[:, 0, c:c + 1], op0=ALU.mult, op1=ALU.add),
                                                       reads=["hg", "cw", "vec4"], writes=[("acc", c)]))
                        for j in range(1, 31):
                            bg_ops.append(lambda c=c, j=j: S.op("dve", lambda e: e.scalar_tensor_tensor(
                                out=acc[:, c, :], in0=hg[:, c, j:j + T], scalar=cw[:, c, j:j + 1], in1=acc[:, c, :], op0=ALU.mult, op1=ALU.add),
                                reads=["hg", "cw", ("acc", c)], writes=[("acc", c)]))
                    for gi in range(4):
                        w = 2 << gi
                        src = uh[:, gi, :]
                        L = T + 16
                        step = 1
                        lvl = 0
                        while step < w:
                            L2 = L - step
                            dst = pw[:, lvl % 3, :]
                            S.op("pool", lambda e, src=src, dst=dst, L2=L2, step=step: e.tensor_add(
                                out=dst[:, 0:L2], in0=src[:, 0:L2], in1=src[:, step:step + L2]),
                                reads=["uh", ("pw", (lvl + 2) % 3)], writes=[("pw", lvl % 3)])
                            src = dst
                            L = L2
                            step *= 2
                            lvl += 1
                        off = 8 - w // 2
                        win = src[:, off:off + T]
                        lastkey = ("pw", (lvl - 1) % 3)
                        S.op("pool", lambda e, win=win, gi=gi: e.tensor_mul(out=win[:, 0:8], in0=win[:, 0:8], in1=corr[:, gi, 0:8]),
                             reads=[lastkey, "corr"], writes=[lastkey])
                        S.op("pool", lambda e, win=win, gi=gi: e.tensor_mul(out=win[:, T - 8:T], in0=win[:, T - 8:T], in1=corr[:, gi, 8:16]),
                             reads=[lastkey], writes=[lastkey])
                        S.op("pool", lambda e, win=win, w=w: e.tensor_scalar(out=win, in0=win, scalar1=1.0 / w, scalar2=None, op0=ALU.mult),
                             reads=[lastkey], writes=[lastkey])
                        S.op("pool", lambda e, win=win, gi=gi: e.tensor_sub(out=pooled[:, gi, :], in0=win, in1=uh[:, gi, 8:8 + T]),
                             reads=[lastkey, "uh"], writes=[("pooled", gi)])

                    S.dma("sp", lambda e: e.dma_start(out=fmask, in_=fmask_d), None, writes=["fmask"])

                    def load_far(hp):
                        sl = hp % 2
                        S.dma("sp", lambda e: e.dma_start(out=qkv_q[sl], in_=qTc_d[hp]), f"q{sl}", writes=[("q", sl)])
                        S.dma("sp", lambda e: e.dma_start(out=qkv_k[sl], in_=kTc_d[hp]), f"k{sl}", writes=[("k", sl)])
                        S.dma("sp", lambda e: e.dma_start(out=vcs[sl], in_=vc_d[hp]), f"v{sl}", writes=[("vc", sl)])

                    funits = [(hp, h, g4) for hp in range(8) for h in range(2) for g4 in range(4)]
                    NF = len(funits)
                    bg_ops.reverse()

                    def far_qk(i):
                        hp, h, g4 = funits[i]
                        sl = hp % 2
                        bA, bB = i % 2, 2 + i % 2

                        def f(e):
                            for c4 in range(4):
                                rho = 4 * g4 + c4
                                e.matmul(ps[bA][:, c4 * 64:(c4 + 1) * 64], lhsT=qkv_k[sl][h * 64:(h + 1) * 64, rho * 192:rho * 192 + 128],
                                         rhs=qkv_q[sl][h * 64:(h + 1) * 64, rho * 64:(rho + 1) * 64], start=True, stop=True)
                                ins = e.matmul(ps[bB][0:64, c4 * 64:(c4 + 1) * 64], lhsT=qkv_k[sl][h * 64:(h + 1) * 64, rho * 192 + 128:(rho + 1) * 192],
                                               rhs=qkv_q[sl][h * 64:(h + 1) * 64, rho * 64:(rho + 1) * 64], start=True, stop=True)
                            return ins
                        S.op("pe", f, reads=[("q", sl), ("k", sl)], writes=[("ps", bA), ("ps", bB)])
                        k2 = i % 2
                        S.op("act", lambda e: e.activation(out=pTA[k2], in_=ps[bA][:, 0:256], func=AF.Exp, scale=0.125), reads=[("ps", bA)], writes=[("pTA", k2)])
                        S.op("act", lambda e: e.activation(out=pTB[k2][0:64, :], in_=ps[bB][0:64, 0:256], func=AF.Exp, scale=0.125), reads=[("ps", bB)], writes=[("pTB", k2)])
                        S.op("dve", lambda e: e.tensor_mul(out=pTA[k2], in0=pTA[k2], in1=fmask[:, 0, :]), reads=[("pTA", k2), "fmask"], writes=[("pTA", k2)])
                        S.op("dve", lambda e: e.tensor_mul(out=pTB[k2][0:64, :], in0=pTB[k2][0:64, :], in1=fmask[0:64, 1, :]), reads=[("pTB", k2), "fmask"], writes=[("pTB", k2)])

                    def far_pv(i):
                        hp, h, g4 = funits[i]
                        sl = hp % 2
                        k2 = i % 2
                        bO = 4 + i % 2

                        def f(e):
                            for c4 in range(4):
                                rho = 4 * g4 + c4
                                e.matmul(ps[bO][0:64, c4 * 128:c4 * 128 + 65], lhsT=pTA[k2][:, c4 * 64:(c4 + 1) * 64],
                                         rhs=vcs[sl][:, rho, 0, h * 65:(h + 1) * 65], start=True, stop=False)
                                ins = e.matmul(ps[bO][0:64, c4 * 128:c4 * 128 + 65], lhsT=pTB[k2][0:64, c4 * 64:(c4 + 1) * 64],
                                               rhs=vcs[sl][0:64, rho, 1, h * 65:(h + 1) * 65], start=False, stop=True)
                            return ins
                        S.op("pe", f, reads=[("pTA", k2), ("pTB", k2), ("vc", sl)], writes=[("ps", bO)])
                        S.op("act", lambda e: e.activation(out=farsb[0:64, 4 * g4:4 * g4 + 4, h * 65:(h + 1) * 65],
                                                           in_=ps[bO][0:64, :].rearrange("p (c f) -> p c f", c=4)[:, :, 0:65], func=AF.Copy),
                             reads=[("ps", bO)], writes=[("farsb", h, g4)])
                        if h == 1 and g4 == 3:
                            S.dma("sp", lambda e: e.dma_start(out=far_d[hp].rearrange("(j r) f -> j r f", r=16), in_=farsb[0:64, :, :]), "fard",
                                  reads=[("farsb", hh, gg) for hh in range(2) for gg in range(4)], writes=[("fard", hp)])

                    load_far(0)
                    for i in range(0 if not os.environ.get("KSKIP_FAR") else NF + 1, NF + 1):
                        if bg_ops:
                            bg_ops.pop()()
                        if i < NF:
                            far_qk(i)
                        if i >= 1:
                            far_pv(i - 1)
                        if i < NF and funits[i][1] == 0 and funits[i][2] == 1 and funits[i][0] + 1 < 8:
                            load_far(funits[i][0] + 1)
                    S.barrier()

                    def load_qkv(hp):
                        sl = hp % 2
                        S.dma("sp", lambda e: e.dma_start(out=qkv_q[sl], in_=qT_d[hp]), f"q{sl}", writes=[("q", sl)])
                        S.dma("sp", lambda e: e.dma_start(out=qkv_k[sl], in_=kT_d[hp]), f"k{sl}", writes=[("k", sl)])
                        S.dma("sp", lambda e: e.dma_start(out=qkv_v[sl], in_=v_d[hp]), f"v{sl}", writes=[("v", sl)])
                        S.dma("sp", lambda e: e.dma_start(out=farnat[sl][:], in_=far_d[hp].rearrange("(qb p) f -> p qb f", p=128)), f"fn{sl}",
                              reads=[("fard", hp)], writes=[("farnat", sl)])

                    units = []
                    for hp in range(8):
                        for qb in range(NT):
                            for h in range(2):
                                for g, (r0, n) in enumerate(((-2, 4), (2, 1))):
                                    units.append((hp, qb, h, g, r0, n))
                    NU = len(units)

                    def emit_qk(i):
                        hp, qb, h, g, r0, n = units[i]
                        sl = hp % 2
                        bank = (0, 1, 2, 5, 6)[i % 5]
                        kb0 = qb + 8 + r0

                        def f(e):
                            for j in range(n):
                                ins = e.matmul(ps[bank][:, j * 128:(j + 1) * 128],
                                               lhsT=qkv_k[sl][h * 64:(h + 1) * 64, (kb0 + j) * 128:(kb0 + j + 1) * 128],
                                               rhs=qkv_q[sl][h * 64:(h + 1) * 64, qb * 128:(qb + 1) * 128], start=True, stop=True)
                            return ins
                        S.op("pe", f, reads=[("q", sl), ("k", sl)], writes=[("ps", bank)])
                        slot = i % 6
                        S.op("act", lambda e: e.activation(out=pT[slot][:, 0:n * 128], in_=ps[bank][:, 0:n * 128], func=AF.Exp, scale=0.125),
                             reads=[("ps", bank)], writes=[("pT", slot)])
                        S.op("dve", lambda e: e.tensor_mul(out=pT[slot][:, 0:n * 128], in0=pT[slot][:, 0:n * 128],
                                                           in1=masks[:, (r0 + 2) * 128:(r0 + 2 + n) * 128]),
                             reads=[("pT", slot), "masks"], writes=[("pT", slot)])

                    def emit_pv(i):
                        hp, qb, h, g, r0, n = units[i]
                        sl = hp % 2
                        slot = i % 6
                        par = (hp * NT + qb) % 2
                        ob = 3 + par
                        kb0 = qb + 8 + r0

                        def f(e):
                            for j in range(n):
                                ins = e.matmul(ps[ob][:, h * 128:h * 128 + 65], lhsT=pT[slot][:, j * 128:(j + 1) * 128],
                                               rhs=qkv_v[sl][:, kb0 + j, h * 65:(h + 1) * 65],
                                               start=(g == 0 and j == 0), stop=(g == 1))
                            return ins
                        S.op("pe", f, reads=[("pT", slot), ("v", sl)], writes=[("ps", ob)])
                        if h == 1 and g == 1:
                            S.op("dve", lambda e: e.tensor_add(out=totsb[par][:], in0=ps[ob][:, 0:256].rearrange("p (h c) -> p h c", h=2)[:, :, 0:65],
                                                               in1=farnat[sl][:, qb, :].rearrange("p (h c) -> p h c", h=2)),
                                 reads=[("ps", ob), ("farnat", sl)], writes=[("totsb", par)])
                            S.op("dve", lambda e: e.reciprocal(out=rec[:, par, :], in_=totsb[par][:, :, 64]),
                                 reads=[("totsb", par)], writes=[("rec", par)])
                            for hh in range(2):
                                S.op("dve", lambda e, hh=hh: e.tensor_scalar(
                                    out=yb[:, qb, hp * 128 + hh * 64:hp * 128 + hh * 64 + 64], in0=totsb[par][:, hh, 0:64],
                                    scalar1=rec[:, par, hh:hh + 1], scalar2=None, op0=ALU.mult),
                                    reads=[("totsb", par), ("rec", par)], writes=[("yb", qb, hp, hh)])

                    load_qkv(0)
                    LAG = 4
                    for i in range(0 if not os.environ.get("KSKIP_NEAR") else NU + LAG, NU + LAG):
                        if i % 2 == 1 and bg_ops:
                            bg_ops.pop()()
                        if i < NU:
                            emit_qk(i)
                        if i - LAG >= 0:
                            emit_pv(i - LAG)
                        if i < NU:
                            hp_, qb_, h_, g_ = units[i][:4]
                            if qb_ == 0 and h_ == 1 and g_ == 1 and hp_ + 1 < 8:
                                load_qkv(hp_ + 1)
                    while bg_ops:
                        bg_ops.pop()()
                    ybkeys = [[("yb", qb, hp, hh) for hp in range(8) for hh in range(2)] for qb in range(NT)]
                    for qb in range(NT):
                        c = 8 + (qb % 2)
                        S.op("act", lambda e, qb=qb, c=c: e.activation(out=sqb, in_=yb[:, qb, :], func=AF.Square, accum_out=ss[:, c:c + 1]),
                             reads=ybkeys[qb], writes=["sqb", ("ss", c)])
                        S.op("dve", lambda e, c=c: e.tensor_scalar(out=rs[:, c:c + 1], in0=ss[:, c:c + 1], scalar1=1.0 / 1024.0, scalar2=EPS,
                                                                   op0=ALU.mult, op1=ALU.add), reads=[("ss", c)], writes=[("rs", c)])
                        S.op("act", lambda e, c=c: e.activation(out=rs[:, c:c + 1], in_=rs[:, c:c + 1], func=AF.Sqrt),
                             reads=[("rs", c)], writes=[("rs", c)])
                        S.op("dve", lambda e, c=c: e.reciprocal(out=rs[:, c:c + 1], in_=rs[:, c:c + 1]), reads=[("rs", c)], writes=[("rs", c)])
                        S.op("dve", lambda e, qb=qb, c=c: e.scalar_tensor_tensor(out=ybs, in0=yb[:, qb, :], scalar=rs[:, c:c + 1], in1=gmixB,
                                                                                 op0=ALU.mult, op1=ALU.mult),
                             reads=ybkeys[qb] + [("rs", c), "gmixB"], writes=["ybs"])

                        def tr(e):
                            for j in range(8):
                                i_ = e.transpose(out=ptr[:, j, :], in_=ybs[:, j * 128:(j + 1) * 128], identity=ident[:])
                            return i_
                        S.op("pe", tr, reads=["ybs", "ident"], writes=[("ps", 7)])
                        S.op("act", lambda e, qb=qb: e.activation(out=yT[:, 4:12, qb * 128:(qb + 1) * 128], in_=ptr[:], func=AF.Copy),
                             reads=[("ps", 7)], writes=[("yT", "b", qb)])

                    dead = ["gmixB"] + [(n_, i_) for n_ in ("v", "q", "k") for i_ in range(2)]
                    for sl in range(2):
                        S.dma("pool", lambda e, sl=sl: e.dma_start(out=wo[sl], in_=wview[:, :, sl * 512:(sl + 1) * 512]),
                              f"wo{sl}", writes=[("wo", sl)] + dead)
                    def rms_feat(src, srckeys, base, tagk):
                        for th in range(2):
                            tsl = slice(th * 512, (th + 1) * 512)
                            for c in range(4):
                                S.op("act", lambda e, c=c, tsl=tsl: e.activation(out=tmpA, in_=src[:, c, tsl], func=AF.Square),
                                     reads=[srckeys(c, th)], writes=["tmpA"])
                                S.op("pe", lambda e, c=c: e.matmul(ps[0][:], lhsT=onesf[:], rhs=tmpA, start=(c == 0), stop=(c == 3)),
                                     reads=["tmpA", "onesf"], writes=[("ps", 0)])
                            S.op("dve", lambda e: e.tensor_scalar(out=rstdsb, in0=ps[0][:], scalar1=EPS, scalar2=None, op0=ALU.add),
                                 reads=[("ps", 0)], writes=["rstdsb"])
                            S.op("act", lambda e: e.activation(out=rstdsb, in_=rstdsb, func=AF.Sqrt), reads=["rstdsb"], writes=["rstdsb"])
                            S.op("dve", lambda e: e.reciprocal(out=rstdsb, in_=rstdsb), reads=["rstdsb"], writes=["rstdsb"])
                            for c in range(4):
                                S.op("dve", lambda e, c=c, tsl=tsl: e.scalar_tensor_tensor(out=yT[:, base + c, tsl], in0=src[:, c, tsl],
                                                                                  scalar=gmixT[:, base + c:base + c + 1], in1=rstdsb,
                                                                                  op0=ALU.mult, op1=ALU.mult),
                                     reads=[srckeys(c, th), "rstdsb", "gmixT"], writes=[("yT", tagk, c, th)])

                    for th in range(2):
                        tsl = slice(th * 512, (th + 1) * 512)
                        for c in range(4):
                            S.op("pe", lambda e, c=c, tsl=tsl: e.matmul(ps[1][:], lhsT=onesf[:], rhs=acc[:, c, tsl], start=(c == 0), stop=(c == 3)),
                                 reads=[("acc", c), "onesf"], writes=[("ps", 1)])
                        for c in range(4):
                            S.op("act", lambda e, c=c, tsl=tsl: e.activation(out=tmpA, in_=acc[:, c, tsl], func=AF.Square),
                                 reads=[("acc", c)], writes=["tmpA"])
                            S.op("pe", lambda e, c=c: e.matmul(ps[2][:], lhsT=onesf[:], rhs=tmpA, start=(c == 0), stop=(c == 3)),
                                 reads=["tmpA", "onesf"], writes=[("ps", 2)])
                        S.op("act", lambda e: e.activation(out=meansb, in_=ps[1][:], func=AF.Copy), reads=[("ps", 1)], writes=["meansb"])
                        S.op("dve", lambda e: e.tensor_mul(out=tmpB, in0=meansb, in1=meansb), reads=["meansb"], writes=["tmpB"])
                        S.op("dve", lambda e: e.tensor_sub(out=rstdsb, in0=ps[2][:], in1=tmpB), reads=[("ps", 2), "tmpB"], writes=["rstdsb"])
                        S.op("dve", lambda e: e.tensor_scalar(out=rstdsb, in0=rstdsb, scalar1=EPS, scalar2=None, op0=ALU.add),
                             reads=["rstdsb"], writes=["rstdsb"])
                        S.op("act", lambda e: e.activation(out=rstdsb, in_=rstdsb, func=AF.Sqrt), reads=["rstdsb"], writes=["rstdsb"])
                        S.op("dve", lambda e: e.reciprocal(out=rstdsb, in_=rstdsb), reads=["rstdsb"], writes=["rstdsb"])
                        for c in range(4):
                            S.op("dve", lambda e, c=c, tsl=tsl: e.tensor_sub(out=tmpB, in0=acc[:, c, tsl], in1=meansb),
                                 reads=[("acc", c), "meansb"], writes=["tmpB"])
                            S.op("dve", lambda e: e.tensor_mul(out=tmpB, in0=tmpB, in1=rstdsb), reads=["tmpB", "rstdsb"], writes=["tmpB"])
                            S.op("act", lambda e, c=c, tsl=tsl: e.activation(out=ya[:, c, tsl], in_=tmpB, func=AF.Silu,
                                                                    scale=vec4[:, 1, c:c + 1], bias=vec4[:, 2, c:c + 1]),
                                 reads=["tmpB", "vec4"], writes=[("ya", c, th)])
                    rms_feat(ya, lambda c, th: ("ya", c, th), 0, "a")

                    for gi in range(4):
                        for th in range(2):
                            tsl = slice(th * 512, (th + 1) * 512)
                            S.op("pe", lambda e, gi=gi, tsl=tsl: e.matmul(ps[1][:], lhsT=poolw[:, gi, :], rhs=pooled[:, gi, tsl], start=True, stop=True),
                                 reads=[("pooled", gi), "poolw"], writes=[("ps", 1)])
                            S.op("act", lambda e, gi=gi, tsl=tsl: e.activation(out=yc[:, gi, tsl], in_=ps[1][:], func=AF.Copy, scale=vec4[:, 3, gi:gi + 1]),
                                 reads=[("ps", 1), "vec4"], writes=[("yc", gi, th)])
                    rms_feat(yc, lambda c, th: ("yc", c, th), 12, "c")
                    if DEBUG:
                        S.barrier()
                        S.dma("sp", lambda e: e.dma_start(out=yT_dbg.rearrange("c p t -> p c t"), in_=yT[:]), "dbg", writes=["dbg"])
                S.barrier()

                with contextlib.ExitStack() as s4:
                    a2 = Arena(R2, R2B)
                    omix = a2.take([NT, D], F32)
                    gt1 = a2.take([D], F32)
                    gt2 = a2.take([D], F32)
                    xbs = [a2.take([D], F32), sbt(s4, "xb_b", [128, D], F32)[:]]
                    sq4 = sbt(s4, "sq4", [128, D], BF16)[:]
                    xs4s = [sbt(s4, f"xs4{i}", [128, D], BF16)[:] for i in range(2)]
                    S.dma("sp", lambda e: e.dma_start(out=gt1, in_=gpm_d.partition_broadcast(128)), None, writes=["gt1"])
                    S.dma("sp", lambda e: e.dma_start(out=gt2, in_=gpf_d.partition_broadcast(128)), None, writes=["gt2"])
                    for dc in range(4):
                        sl = dc % 2
                        if dc >= 2:
                            S.dma("pool", lambda e, sl=sl, dc=dc: e.dma_start(out=wo[sl], in_=wview[:, :, dc * 512:(dc + 1) * 512]),
                                  f"wo{sl}", writes=[("wo", sl)])
                        for tb in range(NT):
                            bank = tb % 4

                            def f(e, tb=tb, sl=sl, bank=bank):
                                for kc in range(16):
                                    ins = e.matmul(ps[bank][:], lhsT=yT[:, kc, tb * 128:(tb + 1) * 128], rhs=wo[sl][:, kc, :],
                                                   start=(kc == 0), stop=(kc == 15))
                                return ins
                            S.op("pe", f, reads=[("wo", sl)] + [("yT", "b", tb)] + [("yT", k, c, tb // 4) for k in ("a", "c") for c in range(4)],
                                 writes=[("ps", bank)])
                            S.op("act", lambda e, tb=tb, dc=dc, bank=bank: e.activation(out=omix[:, tb, dc * 512:(dc + 1) * 512], in_=ps[bank][:], func=AF.Copy),
                                 reads=[("ps", bank)], writes=[("omix", tb, dc)])
                    chains = []
                    for tb in range(NT):
                        b = tb % 2
                        c = tb % 2
                        xb, xbk = xbs[b], ("xb", b)
                        okeys = [("omix", tb, dc) for dc in range(4)]
                        ch = []
                        ch.append(lambda tb=tb, xb=xb, xbk=xbk, b=b: S.dma("sp", lambda e: e.dma_start(out=xb, in_=x_in[tb * 128:(tb + 1) * 128, :]), f"xb{b}", writes=[xbk]))
                        ch.append(lambda tb=tb, c=c, okeys=okeys: S.op("act", lambda e: e.activation(out=sq4, in_=omix[:, tb, :], func=AF.Square, accum_out=ss[:, c:c + 1]),
                                                                       reads=okeys, writes=["sq4", ("ss", c)]))
                        ch.append(lambda c=c: S.op("dve", lambda e: e.tensor_scalar(out=rs[:, c:c + 1], in0=ss[:, c:c + 1], scalar1=1.0 / D, scalar2=EPS,
                                                                                    op0=ALU.mult, op1=ALU.add), reads=[("ss", c)], writes=[("rs", c)]))
                        ch.append(lambda c=c: S.op("act", lambda e: e.activation(out=rs[:, c:c + 1], in_=rs[:, c:c + 1], func=AF.Sqrt), reads=[("rs", c)], writes=[("rs", c)]))
                        ch.append(lambda c=c: S.op("dve", lambda e: e.reciprocal(out=rs[:, c:c + 1], in_=rs[:, c:c + 1]), reads=[("rs", c)], writes=[("rs", c)]))
                        ch.append(lambda tb=tb, c=c, okeys=okeys: S.op("dve", lambda e: e.scalar_tensor_tensor(out=omix[:, tb, :], in0=omix[:, tb, :], scalar=rs[:, c:c + 1], in1=gt1,
                                                                                                          op0=ALU.mult, op1=ALU.mult),
                                                                       reads=okeys + [("rs", c), "gt1"], writes=okeys))
                        ch.append(lambda tb=tb, xb=xb, xbk=xbk, okeys=okeys: S.op("dve", lambda e: e.tensor_add(out=xb, in0=xb, in1=omix[:, tb, :]), reads=okeys + [xbk], writes=[xbk]))
                        ch.append(lambda tb=tb, xb=xb, xbk=xbk, b=b: S.dma("sp", lambda e: e.dma_start(out=x1_d[tb * 128:(tb + 1) * 128, :], in_=xb), f"x1w{b}", reads=[xbk], writes=[("x1d", tb)]))
                        ch += norm_transpose_ops(xb, xbk, gt2, "gt2", xs4s[b], ("xs4", b), sq4, "sq4", h2T, "h2T", tb, 2 + c)
                        chains.append(ch)
                    pipeline(chains, 8)
                S.barrier()
                sY.close()

                with contextlib.ExitStack() as s5:
                    wg = [sbt(s5, f"wg{i}", [128, 16, 256], BF16) for i in range(2)]
                    wu = [sbt(s5, f"wu{i}", [128, 16, 256], BF16) for i in range(2)]
                    sg = [sbt(s5, f"sg{i}", [128, 512], F32) for i in range(2)]
                    gview = wg_d.rearrange("(kc p) n -> p kc n", p=128)
                    uview = wu_d.rearrange("(kc p) n -> p kc n", p=128)
                    h2keys = [("h2T", tb, half) for tb in range(NT) for half in range(2)]
                    n5 = 0
                    for hg2 in range(NHC // 2):
                        sl = hg2 % 2
                        S.dma("pool", lambda e, sl=sl, hg2=hg2: e.dma_start(out=wg[sl][:], in_=gview[:, :, hg2 * 256:(hg2 + 1) * 256]), f"wg{sl}", writes=[("wg", sl)])
                        S.dma("pool", lambda e, sl=sl, hg2=hg2: e.dma_start(out=wu[sl][:], in_=uview[:, :, hg2 * 256:(hg2 + 1) * 256]), f"wu{sl}", writes=[("wu", sl)])
                        for hh in range(2):
                            hc = hg2 * 2 + hh
                            for th in range(2):
                                bg = (n5 % 3) * 2
                                bu = bg + 1
                                sgs = n5 % 2
                                n5 += 1

                                def f(e, sl=sl, hh=hh, th=th, bg=bg, bu=bu):
                                    for kc in range(16):
                                        e.matmul(ps[bg][:], lhsT=wg[sl][:, kc, hh * 128:(hh + 1) * 128], rhs=h2T[:, kc, th * 512:(th + 1) * 512],
                                                 start=(kc == 0), stop=(kc == 15))
                                    for kc in range(16):
                                        ins = e.matmul(ps[bu][:], lhsT=wu[sl][:, kc, hh * 128:(hh + 1) * 128], rhs=h2T[:, kc, th * 512:(th + 1) * 512],
                                                       start=(kc == 0), stop=(kc == 15))
                                    return ins
                                S.op("pe", f, reads=[("wg", sl), ("wu", sl)] + [("h2T", tb, half) for tb in range(th * 4, th * 4 + 4) for half in range(2)],
                                     writes=[("ps", bg), ("ps", bu)])
                                S.op("act", lambda e, bg=bg, sgs=sgs: e.activation(out=sg[sgs][:], in_=ps[bg][:], func=AF.Silu),
                                     reads=[("ps", bg)], writes=[("sg", sgs)])
                                S.op("dve", lambda e, bu=bu, sgs=sgs, hc=hc, th=th: e.tensor_mul(out=aT[:, hc, th * 512:(th + 1) * 512], in0=sg[sgs][:], in1=ps[bu][:]),
                                     reads=[("sg", sgs), ("ps", bu)], writes=[("aT", hc, th)])
                S.barrier()

                with contextlib.ExitStack() as s6:
                    wdn = [sbt(s6, f"wd{i}", [128, 4, 512], BF16) for i in range(2)]
                    gt3 = sbt(s6, "gt3", [128, D], F32)[:]
                    xb6 = sbt(s6, "xb6", [128, D], F32)[:]
                    sq6 = sbt(s6, "sq6", [128, D], BF16)[:]
                    S.dma("sp", lambda e: e.dma_start(out=gt3, in_=gpo_d.partition_broadcast(128)), None, writes=["gt3"])
                    dview = wd_d.rearrange("(hc p) n -> p hc n", p=128)
                    n6 = 0
                    for dc in range(4):
                        for hq in range(11):
                            sl = n6 % 2
                            n6 += 1
                            S.dma("pool", lambda e, sl=sl, hq=hq, dc=dc: e.dma_start(out=wdn[sl][:], in_=dview[:, hq * 4:(hq + 1) * 4, dc * 512:(dc + 1) * 512]),
                                  f"wd{sl}", writes=[("wd", sl)])
                            for tb in range(NT):
                                def f(e, sl=sl, hq=hq, tb=tb):
                                    for j in range(4):
                                        hc = hq * 4 + j
                                        ins = e.matmul(ps[tb][:], lhsT=aT[:, hc, tb * 128:(tb + 1) * 128], rhs=wdn[sl][:, j, :],
                                                       start=(hc == 0), stop=(hc == NHC - 1))
                                    return ins
                                S.op("pe", f, reads=[("wd", sl)] + [("aT", hq * 4 + j, tb // 4) for j in range(4)], writes=[("ps", tb)])
                        for tb in range(NT):
                            S.op("act", lambda e, tb=tb, dc=dc: e.activation(out=o_all[:, tb, dc * 512:(dc + 1) * 512], in_=ps[tb][:], func=AF.Copy),
                                 reads=[("ps", tb)], writes=[("o", tb, dc)])
                    xb6s = [xb6, sbt(s6, "xb6_b", [128, D], F32)[:]]
                    if do_A:
                        sA0 = contextlib.ExitStack()
                        gtA = sbt(sA0, "gtA", [128, D], F32)[:]
                        xsAs = [sbt(sA0, f"xsA{i}", [128, D], BF16)[:] for i in range(2)]
                        S.dma("sp", lambda e: e.dma_start(out=gtA, in_=gpre_d.partition_broadcast(128)), None, writes=["gtA"])
                    chains = []
                    for tb in range(NT):
                        c = tb % 2
                        b = tb % 2
                        xb, xbk = xb6s[b], ("xb6", b)
                        okeys = [("o", tb, dc) for dc in range(4)]
                        ch = []
                        ch.append(lambda tb=tb, xb=xb, xbk=xbk, b=b: S.dma("sp", lambda e: e.dma_start(out=xb, in_=x1_d[tb * 128:(tb + 1) * 128, :]), f"xb{b}", reads=[("x1d", tb)], writes=[xbk]))
                        ch.append(lambda tb=tb, c=c, okeys=okeys: S.op("act", lambda e: e.activation(out=sq6, in_=o_all[:, tb, :], func=AF.Square, accum_out=ss[:, c:c + 1]),
                                                                       reads=okeys, writes=["sq6", ("ss", c)]))
                        ch.append(lambda c=c: S.op("dve", lambda e: e.tensor_scalar(out=rs[:, c:c + 1], in0=ss[:, c:c + 1], scalar1=1.0 / D, scalar2=EPS,
                                                                                    op0=ALU.mult, op1=ALU.add), reads=[("ss", c)], writes=[("rs", c)]))
                        ch.append(lambda c=c: S.op("act", lambda e: e.activation(out=rs[:, c:c + 1], in_=rs[:, c:c + 1], func=AF.Sqrt), reads=[("rs", c)], writes=[("rs", c)]))
                        ch.append(lambda c=c: S.op("dve", lambda e: e.reciprocal(out=rs[:, c:c + 1], in_=rs[:, c:c + 1]), reads=[("rs", c)], writes=[("rs", c)]))
                        ch.append(lambda tb=tb, c=c, okeys=okeys: S.op("dve", lambda e: e.scalar_tensor_tensor(out=o_all[:, tb, :], in0=o_all[:, tb, :], scalar=rs[:, c:c + 1], in1=gt3,
                                                                                                          op0=ALU.mult, op1=ALU.mult),
                                                                       reads=okeys + [("rs", c), "gt3"], writes=okeys))
                        ch.append(lambda tb=tb, xb=xb, xbk=xbk, okeys=okeys: S.op("dve", lambda e: e.tensor_add(out=xb, in0=xb, in1=o_all[:, tb, :]), reads=okeys + [xbk], writes=[xbk]))
                        ch.append(lambda tb=tb, xb=xb, xbk=xbk, b=b: S.dma("sp", lambda e: e.dma_start(out=x_out[tb * 128:(tb + 1) * 128, :], in_=xb), f"xow{b}", reads=[xbk], writes=[("xout", tb)]))
                        if do_A:
                            ch += norm_transpose_ops(xb, xbk, gtA, "gtA", xsAs[b], ("xsA", b), sq6, "sq6", hT, "hT", tb, 4 + c)
                        chains.append(ch)
                    pipeline(chains, 8)
                    if do_A:
                        S.barrier()
                        sA0.close()
            S.barrier()

        if do_A:
            with contextlib.ExitStack() as sA:
                if not do_B:
                    a1 = Arena(R1, R1B)
                    gtA = a1.take([D], F32)
                    xsAs = [a1.take([D], BF16) for i in range(2)]
                    xbAs = [a1.take([D], F32) for i in range(2)]
                    sqA = a1.take([D], BF16)
                    S.dma("sp", lambda e: e.dma_start(out=gtA, in_=gpre_d.partition_broadcast(128)), None, writes=["gtA"])
                    chains = []
                    for tb in range(NT):
                        b = tb % 2
                        ch = [lambda tb=tb, b=b: S.dma("sp", lambda e: e.dma_start(out=xbAs[b], in_=x_in[tb * 128:(tb + 1) * 128, :]), f"xb{b}", writes=[("xbA", b)])]
                        ch += norm_transpose_ops(xbAs[b], ("xbA", b), gtA, "gtA", xsAs[b], ("xsA", b), sqA, "sqA", hT, "hT", tb, 4 + b)
                        chains.append(ch)
                    pipeline(chains, 5)
                    S.barrier()
                a1 = Arena(R1, R1B)
                qko = a1.take([16, T], BF16)
                asb = a1.take([4, T], F32)
                hgo = a1.take([4, T], F32)
                a2 = Arena(R2, R2B, start=32768)
                wi = [a2.take([16, 512], BF16) for i in range(2)]
                upo = a2.take([4, T], F32)
                ropec = a2.take([T], F32)
                ropes = a2.take([T], F32)
                pmat = sbt(sA, "pmat_sb", [128, 128], F32)
                sgA = [sbt(sA, f"sgA{i}", [128, 512], F32) for i in range(2)]
                qf = [sbt(sA, f"qf{i}", [128, 512], F32) for i in range(2)]
                t1 = [sbt(sA, f"t1{i}", [128, 512], F32) for i in range(2)]
                t2 = [sbt(sA, f"t2{i}", [128, 512], F32) for i in range(2)]
                vsb = sbt(sA, "vsb", [128, NT, 16, 65], BF16)
                S.dma("sp", lambda e: e.dma_start(out=ropec, in_=ropec_d), None, writes=["ropec"])
                S.dma("sp", lambda e: e.dma_start(out=ropes, in_=ropes_d), None, writes=["ropes"])
                S.dma("sp", lambda e: e.dma_start(out=pmat[:], in_=pmat_d), None, writes=["pmat"])
                S.op("dve", lambda e: e.memset(vsb[:], 1.0), writes=["vsb_ones"])
                hkeys = [("hT", tb, half) for tb in range(NT) for half in range(2)]
                wiv = win_d.rearrange("(kc p) n -> p kc n", p=128)
                nmm = 0
                nr = 0
                def load_wi(s_):
                    sl = s_ % 2
                    S.dma("pool", lambda e: e.dma_start(out=wi[sl], in_=wiv[:, :, s_ * 512:(s_ + 1) * 512]), f"wi{sl}", writes=[("wi", sl)])
                load_wi(0)
                for s_ in range(9):
                    sl = s_ % 2
                    if s_ + 1 < 9:
                        load_wi(s_ + 1)
                    if s_ in (6, 7):
                        for tb in range(NT):
                            bank = nmm % 4
                            nmm += 1

                            def f(e, sl=sl, tb=tb, bank=bank):
                                for kc in range(16):
                                    ins = e.matmul(ps[bank][:], lhsT=hT[:, kc, tb * 128:(tb + 1) * 128], rhs=wi[sl][:, kc, :], start=(kc == 0), stop=(kc == 15))
                                return ins
                            S.op("pe", f, reads=[("wi", sl)] + hkeys, writes=[("ps", bank)])
                            h0 = (s_ - 6) * 8
                            S.op("act", lambda e, tb=tb, bank=bank, h0=h0: e.activation(out=vsb[:, tb, h0:h0 + 8, 0:64],
                                                                                        in_=ps[bank][:].rearrange("p (h e) -> p h e", h=8), func=AF.Copy),
                                 reads=[("ps", bank), "vsb_ones"], writes=[("vsb", tb, s_)])
                        continue
                    for oc in range(4):
                        for th in range(2):
                            tsl = slice(th * 512, (th + 1) * 512)
                            bank = nmm % 4
                            nmm += 1

                            def f(e, sl=sl, oc=oc, th=th, bank=bank):
                                for kc in range(16):
                                    ins = e.matmul(ps[bank][:], lhsT=wi[sl][:, kc, oc * 128:(oc + 1) * 128], rhs=hT[:, kc, th * 512:(th + 1) * 512],
                                                   start=(kc == 0), stop=(kc == 15))
                                return ins
                            S.op("pe", f, reads=[("wi", sl)] + hkeys, writes=[("ps", bank)])
                            if s_ == 0:
                                S.op("act", lambda e, oc=oc, tsl=tsl, bank=bank: e.activation(out=asb[:, oc, tsl], in_=ps[bank][:], func=AF.Copy),
                                     reads=[("ps", bank)], writes=[("asb", oc, th)])
                            elif s_ == 1:
                                k2 = nmm % 2
                                S.op("act", lambda e, bank=bank, k2=k2: e.activation(out=sgA[k2][:], in_=ps[bank][:], func=AF.Sigmoid),
                                     reads=[("ps", bank)], writes=[("sgA", k2)])
                                S.op("dve", lambda e, oc=oc, tsl=tsl, k2=k2: e.tensor_mul(out=hgo[:, oc, tsl], in0=asb[:, oc, tsl], in1=sgA[k2][:]),
                                     reads=[("sgA", k2), ("asb", oc, th)], writes=[("hgo", oc, th)])
                            elif s_ == 8:
                                S.op("act", lambda e, oc=oc, tsl=tsl, bank=bank: e.activation(out=upo[:, oc, tsl], in_=ps[bank][:], func=AF.Copy),
                                     reads=[("ps", bank)], writes=[("upo", oc, th)])
                            else:
                                ch = (s_ - 2) * 4 + oc
                                k2 = nr % 2
                                pb = 4 + k2
                                nr += 1
                                S.op("act", lambda e, bank=bank, k2=k2: e.activation(out=qf[k2][:], in_=ps[bank][:], func=AF.Copy),
                                     reads=[("ps", bank)], writes=[("qf", k2)])
                                S.op("pe", lambda e, k2=k2, pb=pb: e.matmul(ps[pb][:], lhsT=pmat[:], rhs=qf[k2][:], start=True, stop=True),
                                     reads=[("qf", k2), "pmat"], writes=[("ps", pb)])
                                S.op("dve", lambda e, k2=k2, pb=pb, tsl=tsl: e.tensor_mul(out=t1[k2][:], in0=ps[pb][:], in1=ropes[:, tsl]),
                                     reads=[("ps", pb), "ropes"], writes=[("t1", k2)])
                                S.op("dve", lambda e, k2=k2, tsl=tsl: e.tensor_mul(out=t2[k2][:], in0=qf[k2][:], in1=ropec[:, tsl]),
                                     reads=[("qf", k2), "ropec"], writes=[("t2", k2)])
                                S.op("dve", lambda e, k2=k2, ch=ch, tsl=tsl: e.tensor_add(out=qko[:, ch, tsl], in0=t1[k2][:], in1=t2[k2][:]),
                                     reads=[("t1", k2), ("t2", k2)], writes=[("qko", ch, th)])
                S.dma("sp", lambda e: e.dma_start(out=hg_o.rearrange("c p t -> p c t"), in_=hgo), "outA",
                      reads=[("hgo", c, th) for c in range(4) for th in range(2)], writes=["hg_o"])
                S.dma("sp", lambda e: e.dma_start(out=up_o.rearrange("c p t -> p c t"), in_=upo), "outA",
                      reads=[("upo", c, th) for c in range(4) for th in range(2)], writes=["up_o"])
                S.dma("sp", lambda e: e.dma_start(out=qT_o.rearrange("c p t -> p c t"), in_=qko[:, 0:8, :]), "outA",
                      reads=[("qko", c, th) for c in range(8) for th in range(2)], writes=["qT_o"])
                S.dma("sp", lambda e: e.dma_start(out=kT_o.rearrange("c p t -> p c t"), in_=qko[:, 8:16, :]), "outA",
                      reads=[("qko", c, th) for c in range(8, 16) for th in range(2)], writes=["kT_o"])
                S.dma("sp", lambda e: e.dma_start(out=v_o.rearrange("t p f -> p t f"), in_=vsb[:].rearrange("p t h e -> p t (h e)")), "outA",
                      reads=[("vsb", tb, s_) for tb in range(NT) for s_ in (6, 7)], writes=["v_o"])
        S.barrier()
        S.run()
    return nc


_PROGS = {}


def _prog(do_B, do_A):
    key = (do_B, do_A)
    if key not in _PROGS:
        _PROGS[key] = build_program(do_B, do_A)
    return _PROGS[key]


def _const_tables():
    o = np.arange(-2 * 128 - 127, 2 * 128 + 128)
    w = ((np.abs(o) <= 64).astype(np.float32) + ((o % 4 == 0) & (np.abs(o) <= 256)))
    wmap = dict(zip(o.tolist(), w.tolist()))
    k = np.arange(128)[:, None, None]
    rel = np.array([r for _ in range(2) for r in range(-2, 3)])[None, :, None]
    q = np.arange(128)[None, None, :]
    off = rel * 128 + k - q
    masks = np.vectorize(wmap.get)(off).astype(np.float32).reshape(128, 10 * 128).astype(ml_dtypes.bfloat16)
    kk = np.arange(128)[:, None]
    jj = np.arange(64)[None, :]
    mA = (kk >= jj).astype(np.float32)
    mB = ((kk <= jj) & (kk < 64)).astype(np.float32)
    fmask = np.stack([np.tile(mA, (1, 4)), np.tile(mB, (1, 4))], axis=1).astype(ml_dtypes.bfloat16)
    pm = np.zeros((128, 128), np.float32)
    for hh in range(2):
        for m in range(8):
            pm[hh * 64 + m + 8, hh * 64 + m] = -1.0
            pm[hh * 64 + m, hh * 64 + m + 8] = 1.0
    return (masks, fmask), pm


def _rope_tables(c):
    pos = (np.arange(T) + c * T).astype(np.float32)
    inv = (np.float32(500000.0) ** (-np.arange(0, 16, 2, dtype=np.float32) / np.float32(16))).astype(np.float32)
    ang = (pos[:, None] * inv[None, :]).astype(np.float32)
    cs, sn = np.cos(ang).astype(np.float32), np.sin(ang).astype(np.float32)
    C = np.ones((128, T), np.float32)
    Sn = np.zeros((128, T), np.float32)
    for hh in range(2):
        for i in range(16):
            C[hh * 64 + i] = cs[:, i % 8]
            Sn[hh * 64 + i] = sn[:, i % 8]
    return C, Sn


def _corr_table(c):
    out = np.ones((4, 16), np.float32)
    idx = np.concatenate([np.arange(8), np.arange(T - 8, T)]) + c * T
    for gi, w in enumerate((2, 4, 8, 16)):
        lo = np.clip(idx - w // 2, 0, S_LEN)
        hi = np.clip(idx + w - w // 2, 0, S_LEN)
        out[gi] = w / (hi - lo).astype(np.float32)
    return np.ascontiguousarray(np.broadcast_to(out[None], (128, 4, 16)))


def _colmajor(v, n):
    return np.ascontiguousarray(v.reshape(n, 128).T)


def _run_step(step, xs, Aout, P, consts):
    f32 = np.float32
    masks, pm, ident, ropes, corrs = consts
    do_B = step > 0
    do_A = step < DEPTH
    lb, la = step - 1, step
    nc = _prog(do_B, do_A)
    maps = []
    if do_B:
        kT = np.concatenate([Aout[c]["kT_o"] for c in range(NCORES)], axis=2)
        kT = np.pad(kT, ((0, 0), (0, 0), (T, T)))
        vv = np.concatenate([Aout[c]["v_o"].reshape(T, 8, 130) for c in range(NCORES)], axis=0)
        vv = np.pad(vv, ((T, T), (0, 0), (0, 0)))
        hgf = np.pad(np.concatenate([Aout[c]["hg_o"] for c in range(NCORES)], axis=2), ((0, 0), (0, 0), (15, 15)))
        upf = np.pad(np.concatenate([Aout[c]["up_o"] for c in range(NCORES)], axis=2), ((0, 0), (0, 0), (8, 8)))
        cw = np.ascontiguousarray(np.asarray(P["conv_w"][lb], f32).T.reshape(4, 128, 31).transpose(1, 0, 2))
        vec4 = np.stack([_colmajor(np.asarray(P[k][lb], f32), 4) for k in ("conv_b", "conv_ln_g", "conv_ln_b", "pool_scale")], axis=1)
        gm = np.asarray(P["g_mix"][lb], f32)
    for c in range(NCORES):
        m = {"ident": ident, "x_in": xs[c]}
        if do_B:
            seg = vv[c * T:c * T + 3 * T]
            vh = seg.reshape(24, 128, 8, 130).transpose(2, 1, 0, 3)
            sc = seg.reshape(192, 16, 8, 130)
            kseg = kT[:, :, c * T:c * T + 3 * T]
            kTc = np.ascontiguousarray(kseg.reshape(8, 128, 192, 16).transpose(0, 1, 3, 2)).reshape(8, 128, 16 * 192)
            qTc = np.ascontiguousarray(np.asarray(Aout[c]["qT_o"]).reshape(8, 128, 64, 16).transpose(0, 1, 3, 2)).reshape(8, 128, 16 * 64)
            vc = np.zeros((8, 128, 16, 2, 130), seg.dtype)
            vc[:, :, :, 0, :] = sc[0:128].transpose(2, 0, 1, 3)
            vc[:, 0:64, :, 1, :] = sc[128:192].transpose(2, 0, 1, 3)
            m.update({
                "qT": Aout[c]["qT_o"],
                "kTh": np.ascontiguousarray(kT[:, :, c * T:c * T + 3 * T]),
                "vh": np.ascontiguousarray(vh),
                "hgh": np.ascontiguousarray(hgf[:, :, c * T:c * T + T + 30]),
                "uh": np.ascontiguousarray(upf[:, :, c * T:c * T + T + 16]),
                "masks": masks[0], "fmask": masks[1], "vc": vc, "kTc": kTc, "qTc": qTc, "corr": corrs[c], "cw": cw, "vec4": np.ascontiguousarray(vec4),
                "gmixT": _colmajor(gm, 16), "gmixB": np.ascontiguousarray(gm[None, 512:1536]),
                "poolw": np.asarray(P["pool_w"][lb], f32), "w_out": np.asarray(P["w_out"][lb], f32),
                "g_post_mix": np.asarray(P["g_post_mix"][lb], f32)[None], "g_pre_ffn": np.asarray(P["g_pre_ffn"][lb], f32)[None],
                "w_gate": np.asarray(P["w_gate"][lb], f32), "w_up": np.asarray(P["w_up"][lb], f32),
                "w_down": np.asarray(P["w_down"][lb], f32), "g_post_ffn": np.asarray(P["g_post_ffn"][lb], f32)[None],
            })
        if do_A:
            m.update({"w_in": np.asarray(P["w_in"][la], f32), "g_pre_mix": np.asarray(P["g_pre_mix"][la], f32)[None],
                      "rope_c": ropes[c][0], "rope_s": ropes[c][1], "pmat": pm})
        maps.append(m)
    res = run_bass_kernel_spmd(nc, maps, core_ids=list(range(NCORES)))
    Aout = res.results
    if do_B:
        xs = [np.asarray(Aout[c]["x_out"], f32) for c in range(NCORES)]
    return xs, Aout


def _consts():
    masks, pm = _const_tables()
    ident = np.eye(128, dtype=np.float32)
    ropes = [_rope_tables(c) for c in range(NCORES)]
    corrs = [_corr_table(c) for c in range(NCORES)]
    return masks, pm, ident, ropes, corrs


def kernel(x, w_in, conv_w, conv_b, conv_ln_g, conv_ln_b, pool_w, pool_scale, g_mix,
           w_out, g_pre_mix, g_post_mix, g_pre_ffn, g_post_ffn, w_gate, w_up, w_down):
    P = dict(w_in=w_in, conv_w=conv_w, conv_b=conv_b, conv_ln_g=conv_ln_g, conv_ln_b=conv_ln_b, pool_w=pool_w,
             pool_scale=pool_scale, g_mix=g_mix, w_out=w_out, g_pre_mix=g_pre_mix, g_post_mix=g_post_mix,
             g_pre_ffn=g_pre_ffn, g_post_ffn=g_post_ffn, w_gate=w_gate, w_up=w_up, w_down=w_down)
    x = np.asarray(x, np.float32)
    consts = _consts()
    xs = [np.ascontiguousarray(x[0, c * T:(c + 1) * T]) for c in range(NCORES)]
    Aout = None
    for step in range(DEPTH + 1):
        xs, Aout = _run_step(step, xs, Aout, P, consts)
    return np.concatenate(xs, axis=0)[None].astype(np.float32)
```
